# Optimizing a Trainium2 kernel written in Bass

```python
import math
import numpy as np
import jax
import jax.numpy as jnp
from jax import lax

D_MODEL = 1024
BATCH = 16
SEQ = 2048
DEPTH = 2

HEAD_DIM = 64
N_HEADS_TOTAL = D_MODEL // HEAD_DIM
N_HEADS_B = N_HEADS_TOTAL // 4
N_HEADS_C = (N_HEADS_TOTAL - N_HEADS_B) // 2
N_HEADS_A = N_HEADS_TOTAL - N_HEADS_B - N_HEADS_C
A_WIDTH = N_HEADS_A * HEAD_DIM
B_WIDTH = N_HEADS_B * HEAD_DIM
C_WIDTH = N_HEADS_C * HEAD_DIM
MIX_WIDTH = A_WIDTH + B_WIDTH + C_WIDTH
COL_SIZES = (3 * A_WIDTH, A_WIDTH, N_HEADS_A, N_HEADS_A,
             B_WIDTH, B_WIDTH, B_WIDTH, B_WIDTH,
             C_WIDTH, C_WIDTH, C_WIDTH, C_WIDTH)
IN_COLS = sum(COL_SIZES)
CONV_WIDTH = 4
GDN_CHUNK = 64
BLOCK = 128
ROPE_DIM = HEAD_DIM // 4
ROPE_THETA = 500000.0
DILATED_PAIRS = ((128, 1), (512, 4), (2048, 16))
RMS_EPS = 1e-6

kernel_name = "hybrid_gdn_stickbreak_dilated"


def rmsnorm(x, w):
    x32 = x.astype(jnp.float32)
    y = x32 * lax.rsqrt(jnp.mean(x32 * x32, axis=-1, keepdims=True) + RMS_EPS)
    return (y * w.astype(jnp.float32)).astype(x.dtype)


def l2norm(x):
    return x * lax.rsqrt(jnp.sum(x * x, axis=-1, keepdims=True) + RMS_EPS)


def causal_depthwise_conv(x, w):
    kw, ch = w.shape
    return lax.conv_general_dilated(
        x, w.astype(x.dtype)[:, None, :], window_strides=(1,),
        padding=((kw - 1, 0),), dimension_numbers=('NWC', 'WIO', 'NWC'),
        feature_group_count=ch)


def partial_rope(x, positions):
    half = ROPE_DIM // 2
    inv_freq = ROPE_THETA ** (-jnp.arange(half, dtype=jnp.float32) / half)
    ang = positions.astype(jnp.float32)[:, None] * inv_freq[None, :]
    cos = jnp.cos(ang)[None, :, None, :]
    sin = jnp.sin(ang)[None, :, None, :]
    x32 = x.astype(jnp.float32)
    x1 = x32[..., :half]
    x2 = x32[..., half:ROPE_DIM]
    out = jnp.concatenate([x1 * cos - x2 * sin, x2 * cos + x1 * sin, x32[..., ROPE_DIM:]], axis=-1)
    return out.astype(x.dtype)


def gated_delta_rule(q, k, v, g, beta):
    b, t, h, dk = q.shape
    dv = v.shape[-1]
    c = GDN_CHUNK
    n = t // c
    f32 = jnp.float32
    q = l2norm(q.astype(f32)) * (dk ** -0.5)
    k = l2norm(k.astype(f32))
    v = v.astype(f32)

    def chunk4(z):
        return z.reshape(b, n, c, h, z.shape[-1]).transpose(0, 1, 3, 2, 4)

    def chunk3(z):
        return z.astype(f32).reshape(b, n, c, h).transpose(0, 1, 3, 2)

    q, k, v = chunk4(q), chunk4(k), chunk4(v)
    g, beta = chunk3(g), chunk3(beta)
    gc = jnp.cumsum(g, axis=-1)
    idx = jnp.arange(c)
    incl = idx[:, None] >= idx[None, :]
    strict = idx[:, None] > idx[None, :]
    decay = jnp.exp(jnp.where(incl, gc[..., :, None] - gc[..., None, :], -jnp.inf))
    kb = k * beta[..., None]
    a = jnp.where(strict, jnp.einsum('bnhid,bnhjd->bnhij', kb, k) * decay, 0.0)
    eye = jnp.eye(c, dtype=f32)
    tmat = lax.linalg.triangular_solve(eye + a, jnp.broadcast_to(eye, a.shape),
                                       left_side=True, lower=True, unit_diagonal=True)
    u = tmat @ (v * beta[..., None])
    w = tmat @ (kb * jnp.exp(gc)[..., None])
    qk = jnp.einsum('bnhid,bnhjd->bnhij', q, k) * decay
    qg = q * jnp.exp(gc)[..., None]
    kg = k * jnp.exp(gc[..., -1:] - gc)[..., None]
    g_last = jnp.exp(gc[..., -1])

    def step(state, inp):
        qg_i, kg_i, u_i, w_i, qk_i, gl_i = inp
        v_new = u_i - w_i @ state
        o_i = qg_i @ state + qk_i @ v_new
        state = state * gl_i[..., None, None] + jnp.einsum('bhcd,bhce->bhde', kg_i, v_new)
        return state, o_i

    xs = tuple(jnp.moveaxis(z, 1, 0) for z in (qg, kg, u, w, qk, g_last))
    s0 = jnp.zeros((b, h, dk, dv), f32)
    _, o = lax.scan(step, s0, xs)
    return o.transpose(1, 0, 3, 2, 4).reshape(b, t, h, dv)


def stick_breaking_attention(q, k, v):
    b, t, h, dh = q.shape
    nb = t // BLOCK
    f32 = jnp.float32
    qb = (q.astype(f32) * (dh ** -0.5)).reshape(b, nb, BLOCK, h, dh).transpose(1, 0, 3, 2, 4)
    kt = k.astype(f32).transpose(0, 2, 1, 3)
    vt = v.astype(f32).transpose(0, 2, 1, 3)
    key_pos = jnp.arange(t)

    def one_block(args):
        q_blk, blk = args
        z = jnp.einsum('bhqd,bhkd->bhqk', q_blk, kt)
        q_pos = blk * BLOCK + jnp.arange(BLOCK)
        earlier = key_pos[None, :] < q_pos[:, None]
        log_beta = jnp.where(earlier, jax.nn.log_sigmoid(z), -jnp.inf)
        log_keep = jnp.where(earlier, jax.nn.log_sigmoid(-z), 0.0)
        log_keep_between = lax.cumsum(log_keep, axis=3, reverse=True) - log_keep
        wts = jnp.exp(log_beta + log_keep_between)
        return jnp.einsum('bhqk,bhkd->bhqd', wts, vt)

    o = lax.map(one_block, (qb, jnp.arange(nb)))
    return o.transpose(1, 0, 3, 2, 4).reshape(b, t, h, dh)


def dilated_window_attention(q, k, v, window, dilation):
    b, t, h, dh = q.shape
    steps = window // dilation
    length = t // dilation
    nb = -(-length // BLOCK)
    lp = nb * BLOCK
    f32 = jnp.float32

    def to_blocks(z):
        z = z.astype(f32).reshape(b, length, dilation, h, dh).transpose(0, 2, 3, 1, 4)
        z = jnp.pad(z, ((0, 0), (0, 0), (0, 0), (0, lp - length), (0, 0)))
        return z.reshape(b, dilation, h, nb, BLOCK, dh)

    def with_prev(z):
        prev = jnp.pad(z, ((0, 0), (0, 0), (0, 0), (1, 0), (0, 0), (0, 0)))[:, :, :, :-1]
        return jnp.concatenate([prev, z], axis=-2)

    qb = to_blocks(q) * (dh ** -0.5)
    kk = with_prev(to_blocks(k))
    vv = with_prev(to_blocks(v))
    s = jnp.einsum('brhnqe,brhnke->brhnqk', qb, kk)
    qi = jnp.arange(BLOCK)[:, None]
    kj = jnp.arange(2 * BLOCK)[None, :]
    rel = qi - kj + BLOCK
    key_idx = jnp.arange(nb)[:, None, None] * BLOCK + kj[None] - BLOCK
    mask = (rel >= 0) & (rel <= steps) & (key_idx >= 0)
    s = jnp.where(mask, s, -jnp.inf)
    m = jnp.max(s, axis=-1, keepdims=True)
    p = jnp.exp(s - m)
    denom = jnp.sum(p, axis=-1, keepdims=True)
    o = jnp.einsum('brhnqk,brhnke->brhnqe', p, vv) / denom
    lse = (m + jnp.log(denom))[..., 0]
    o = o.reshape(b, dilation, h, lp, dh)[:, :, :, :length].transpose(0, 3, 1, 2, 4).reshape(b, t, h, dh)
    lse = lse.reshape(b, dilation, h, lp)[..., :length].transpose(0, 3, 1, 2).reshape(b, t, h)
    return o, lse


def hybrid_layer(x, norm_w, w_in, conv_w, a_log, dt_bias, gdn_norm_w, q_norm_w, k_norm_w, w_out, positions):
    b, t, _ = x.shape
    f32 = jnp.float32
    hdn = rmsnorm(x, norm_w)
    proj = hdn @ w_in
    split_points = np.cumsum(COL_SIZES)[:-1].tolist()
    (qkv_a, z_a, beta_a, alpha_a, q_b, k_b, v_b, z_b,
     q_c, k_c, v_c, z_c) = jnp.split(proj, split_points, axis=-1)

    def heads(z, n_heads):
        return z.reshape(b, t, n_heads, HEAD_DIM)

    qkv_a = jax.nn.silu(causal_depthwise_conv(qkv_a, conv_w))
    q_a, k_a, v_a = jnp.split(qkv_a, 3, axis=-1)
    beta = jax.nn.sigmoid(beta_a.astype(f32))
    g = -jnp.exp(a_log.astype(f32)) * jax.nn.softplus(alpha_a.astype(f32) + dt_bias.astype(f32))
    o_a = gated_delta_rule(heads(q_a, N_HEADS_A), heads(k_a, N_HEADS_A), heads(v_a, N_HEADS_A), g, beta)
    o_a = rmsnorm(o_a, gdn_norm_w).reshape(b, t, A_WIDTH).astype(x.dtype) * jax.nn.silu(z_a)

    o_b = stick_breaking_attention(heads(q_b, N_HEADS_B), heads(k_b, N_HEADS_B), heads(v_b, N_HEADS_B))
    o_b = o_b.reshape(b, t, B_WIDTH).astype(x.dtype) * jax.nn.silu(z_b)

    qc = partial_rope(rmsnorm(heads(q_c, N_HEADS_C), q_norm_w), positions)
    kc = partial_rope(rmsnorm(heads(k_c, N_HEADS_C), k_norm_w), positions)
    vc = heads(v_c, N_HEADS_C)
    outs = []
    lses = []
    for window, dilation in DILATED_PAIRS:
        o_g, lse_g = dilated_window_attention(qc, kc, vc, window, dilation)
        outs.append(o_g)
        lses.append(lse_g)
    mix_w = jax.nn.softmax(jnp.stack(lses, axis=0), axis=0)
    o_c = jnp.einsum('gbth,gbthd->bthd', mix_w, jnp.stack(outs, axis=0))
    o_c = o_c.reshape(b, t, C_WIDTH).astype(x.dtype) * jax.nn.silu(z_c)

    mixed = jnp.concatenate([o_a, o_b, o_c], axis=-1)
    return x + mixed @ w_out


def setup_inputs(seed: int = 0) -> dict:
    key = jax.random.key(seed)
    ks = jax.random.split(key, 10)
    f32 = jnp.float32
    x = jax.random.normal(ks[0], (BATCH, SEQ, D_MODEL), f32)
    norm_w = 1.0 + 0.02 * jax.random.normal(ks[1], (DEPTH, D_MODEL), f32)
    w_in = jax.random.normal(ks[2], (DEPTH, D_MODEL, IN_COLS), f32) * (D_MODEL ** -0.5)
    conv_w = jax.random.normal(ks[3], (DEPTH, CONV_WIDTH, 3 * A_WIDTH), f32) * (CONV_WIDTH ** -0.5)
    a_log = jnp.log(jax.random.uniform(ks[4], (DEPTH, N_HEADS_A), f32, minval=1.0, maxval=16.0))
    dt = jnp.exp(jax.random.uniform(ks[5], (DEPTH, N_HEADS_A), f32,
                                    minval=math.log(1e-3), maxval=math.log(1e-1)))
    dt_bias = dt + jnp.log(-jnp.expm1(-dt))
    gdn_norm_w = 1.0 + 0.02 * jax.random.normal(ks[6], (DEPTH, HEAD_DIM), f32)
    q_norm_w = 1.0 + 0.02 * jax.random.normal(ks[7], (DEPTH, HEAD_DIM), f32)
    k_norm_w = 1.0 + 0.02 * jax.random.normal(ks[8], (DEPTH, HEAD_DIM), f32)
    w_out = jax.random.normal(ks[9], (DEPTH, MIX_WIDTH, D_MODEL), f32) * (MIX_WIDTH ** -0.5)
    return {"x": x, "norm_w": norm_w, "w_in": w_in, "conv_w": conv_w, "a_log": a_log,
            "dt_bias": dt_bias, "gdn_norm_w": gdn_norm_w, "q_norm_w": q_norm_w,
            "k_norm_w": k_norm_w, "w_out": w_out}


def reference(x, norm_w, w_in, conv_w, a_log, dt_bias, gdn_norm_w, q_norm_w, k_norm_w, w_out):
    positions = jnp.arange(x.shape[1], dtype=jnp.int32)
    for layer in range(DEPTH):
        x = hybrid_layer(x, norm_w[layer], w_in[layer], conv_w[layer], a_log[layer], dt_bias[layer],
                         gdn_norm_w[layer], q_norm_w[layer], k_norm_w[layer], w_out[layer], positions)
    return x
```

```python
import numpy as np
import ml_dtypes
import concourse.bass as bass
import concourse.mybir as mybir
from concourse.bass_utils import run_bass_kernel_spmd

F32 = mybir.dt.float32
BF16 = mybir.dt.bfloat16
AF = mybir.ActivationFunctionType
ALU = mybir.AluOpType
AX = mybir.AxisListType

D = 1024
KC = 8
IN_COLS = 4108
OFF_QA, OFF_KA, OFF_VA, OFF_ZA, OFF_BETA, OFF_ALPHA = 0, 384, 768, 1152, 1536, 1542
OFF_QB, OFF_KB, OFF_VB, OFF_ZB = 1548, 1804, 2060, 2316
OFF_QC, OFF_KC, OFF_VC, OFF_ZC = 2572, 2956, 3340, 3724
EPS = 1e-6
ROPE_THETA = 500000.0


class Res:
    __slots__ = ("name", "w", "r", "excl")

    def __init__(self, name, excl=False):
        self.name = name
        self.w = None
        self.r = []
        self.excl = excl


class Sched:
    def __init__(self, nc, n_dma_sems=24):
        self.nc = nc
        self.eng = {"pe": nc.tensor, "act": nc.scalar, "dve": nc.vector, "pool": nc.gpsimd, "sp": nc.sync}
        self.sems = {}
        self.cnt = {}
        for e in ("pe", "act", "dve", "pool"):
            self.sems[e] = nc.alloc_semaphore("s_" + e)
            self.cnt[e] = 0
        self.dma_sems = []
        for i in range(n_dma_sems):
            k = "d%d" % i
            self.sems[k] = nc.alloc_semaphore("s_" + k)
            self.cnt[k] = 0
            self.dma_sems.append(k)
        self.dma_rr = 0
        self.waited = {}
        self.pe_pending = False
        self.nwaits = 0
        self.nins = 0

    def _wait(self, e, key, val):
        if val <= 0:
            return
        if e == "pe" and key == "pe":
            return
        if key == "pe" and val > self.cnt["pe"]:
            raise RuntimeError("wait on a PE group that has not been closed with inc=True")
        if self.waited.get((e, key), 0) >= val:
            return
        self.waited[(e, key)] = val
        self.eng[e].wait_ge(self.sems[key], val)
        self.nwaits += 1

    def _deps(self, e, reads, writes):
        for t in reads:
            if t.w is not None:
                self._wait(e, *t.w)
        for t in writes:
            if t.w is not None:
                self._wait(e, *t.w)
            for rr in t.r:
                self._wait(e, *rr)

    @staticmethod
    def _compact(lst):
        best = {}
        for k, v in lst:
            if best.get(k, 0) < v:
                best[k] = v
        return list(best.items())

    def _record(self, key, val, reads, writes):
        for t in reads:
            t.r.append((key, val))
            if len(t.r) > 16:
                t.r = self._compact(t.r)
        for t in writes:
            t.w = (key, val)
            t.r = []

    def op(self, e, ins_fn, reads=(), writes=(), inc=True):
        if any(t.excl for t in reads):
            writes = list(writes) + [t for t in reads if t.excl]
            reads = [t for t in reads if not t.excl]
        self._deps(e, reads, writes)
        ins = ins_fn()
        self.nins += 1
        if e == "pe" and not inc:
            self.pe_pending = True
            self._record("pe", self.cnt["pe"] + 1, reads, writes)
            return ins
        if e == "pe":
            self.pe_pending = False
        self.cnt[e] += 1
        ins.then_inc(self.sems[e], 1)
        self._record(e, self.cnt[e], reads, writes)
        return ins

    def dma(self, out, in_, reads=(), writes=(), q="sp", **kw):
        key = self.dma_sems[self.dma_rr % len(self.dma_sems)]
        self.dma_rr += 1
        self._wait(q, key, self.cnt[key])
        self._deps(q, reads, writes)
        ins = self.eng[q].dma_start(out=out, in_=in_, **kw)
        self.nins += 1
        self.cnt[key] += 16
        ins.then_inc(self.sems[key], 16)
        self._record(key, self.cnt[key], reads, writes)
        return ins

    def barrier(self):
        assert not self.pe_pending
        for e in ("pe", "act", "dve", "pool"):
            for k in ("pe", "act", "dve", "pool"):
                if k != e:
                    self._wait(e, k, self.cnt[k])

    def finish(self, out_res):
        for t in out_res:
            if t.w is not None:
                self._wait("sp", *t.w)


def _consts(T):
    NT = T // 128
    QB = min(512, T)
    KPB = QB // 128
    c = {}
    idx = np.arange(128)
    c["identf"] = np.eye(128, dtype=np.float32)
    c["identb"] = np.eye(128, dtype=np.float32).astype(ml_dtypes.bfloat16)
    c["tri_gt"] = (idx[:, None] > idx[None, :]).astype(np.float32)
    c["tri_le"] = (idx[:, None] <= idx[None, :]).astype(np.float32)
    c["onesf"] = np.ones((128, 128), np.float32)
    c["onesb"] = np.ones((128, 64), np.float32).astype(ml_dtypes.bfloat16)
    bo = np.zeros((128, 128), np.float32)
    bo[:64, :64] = 1.0
    bo[64:, 64:] = 1.0
    c["blockones"] = bo
    incl = (idx[:, None] <= idx[None, :]).astype(np.float32)
    nst = -(idx[:, None] < idx[None, :]).astype(np.float32)
    c["incl2"] = np.stack([incl, incl], 1)
    c["nst2"] = np.stack([nst, nst], 1)
    c["ident2"] = np.stack([np.eye(128, dtype=np.float32)] * 2, 1)
    OFF0 = (KPB - 1) * 128
    W = OFF0 + (T - QB) + QB
    dl = np.arange(W)[None, :] - idx[:, None] - OFF0
    m = ((dl >= 0) & (dl <= 128)).astype(np.float32)
    m += ((dl >= 0) & (dl <= 512) & (dl % 4 == 0)).astype(np.float32)
    m += ((dl >= 0) & (dl <= 2048) & (dl % 16 == 0)).astype(np.float32)
    c["mtab"] = m.astype(ml_dtypes.bfloat16)
    half = 8
    inv_freq = (ROPE_THETA ** (-np.arange(half, dtype=np.float32) / half)).astype(np.float32)
    pos = np.arange(T, dtype=np.float32)
    ang = (pos[:, None] * inv_freq[None, :]).astype(np.float32)
    c["cost"] = np.ascontiguousarray(np.cos(ang).astype(np.float32).reshape(NT, 128, half).transpose(1, 0, 2))
    c["sint"] = np.ascontiguousarray(np.sin(ang).astype(np.float32).reshape(NT, 128, half).transpose(1, 0, 2))
    return c


def build_program(T=2048, NSEQ=2, NL=2, dbg=False, phases="nabco"):
    NT = T // 128
    QB = min(512, T)
    NQB = T // QB
    KPB = QB // 128
    OFF0 = (KPB - 1) * 128
    QA = min(256, T)
    KA = QA // 128
    nc = bass.Bass("TRN2", target_bir_lowering=False)
    cst = _consts(T)

    def din(name, shape, dt=F32):
        return nc.dram_tensor(name, list(shape), dt, kind="ExternalInput").ap()

    x_d = din("x", [NSEQ, T, D])
    w_in_d = din("w_in", [NL, D, IN_COLS])
    w_out_d = din("w_out", [NL, D, D])
    norm_w_d = din("norm_w", [NL, D])
    conv_w_d = din("conv_w", [NL, 4, 1152])
    a_log_d = din("a_log", [NL, 6])
    dt_bias_d = din("dt_bias", [NL, 6])
    gdn_w_d = din("gdn_norm_w", [NL, 64])
    qn_w_d = din("q_norm_w", [NL, 64])
    kn_w_d = din("k_norm_w", [NL, 64])
    cd = {}
    for k, v in cst.items():
        cd[k] = din("c_" + k, v.shape, BF16 if v.dtype == ml_dtypes.bfloat16 else F32)
    out_d = nc.dram_tensor("out", [NSEQ, T, D], F32, kind="ExternalOutput").ap()
    if dbg:
        dbg_d = nc.dram_tensor("dbg_mix", [128, 8, T], F32, kind="ExternalOutput").ap()

    S = Sched(nc)
    eng = S.eng

    def E(e, name, R, W, *a, **k):
        return S.op(e, lambda: getattr(eng[e], name)(*a, **k), R, W)

    def MM(out, lhsT, rhs, R, W, start=True, stop=True, inc=True):
        return S.op("pe", lambda: nc.tensor.matmul(out, lhsT=lhsT, rhs=rhs, start=start, stop=stop), R, W, inc=inc)

    def TR(out, in_, ident, R, W, inc=True):
        return S.op("pe", lambda: nc.tensor.transpose(out, in_, ident), R, W, inc=inc)

    class Tl:
        def __init__(self, name, shape, dt, nres=1):
            self.t = nc.alloc_sbuf_tensor(name, list(shape), dt)
            self.r = [Res("%s_%d" % (name, i)) for i in range(nres)]
            self.R = self.r[0]

    X = Tl("X", [128, NT, D], F32, NT)
    hT = Tl("hT", [128, KC, T], BF16, NQB)
    mixT = Tl("mixT", [128, KC, T], BF16, KC)
    wbuf = [Tl("wbuf%d" % i, [128, KC, 512], BF16, 4) for i in range(2)]
    stage = [Tl("stage%d" % i, [128, KC, 128], F32) for i in range(2)]
    C = {}
    for k, v in cst.items():
        C[k] = Tl("k_" + k, v.shape, BF16 if v.dtype == ml_dtypes.bfloat16 else F32)
    normw = Tl("normw", [128, NL, KC], F32)
    convw = Tl("convw", [128, NL, 9, 4], F32)
    nea = Tl("nea", [128, NL, 6], F32)
    dtb = Tl("dtb", [128, NL, 6], F32)
    gw = Tl("gw", [128, NL, 64], F32)
    wqk = Tl("wqk", [128, NL, 4, 64], F32)
    wba = Tl("wba", [128, KC, 12], BF16)
    wbas = Tl("wbas", [128, KC, 12], F32)
    ps = [nc.alloc_psum_tensor("ps%d" % i, [128, 512], F32) for i in range(8)]
    pr = [Res("ps%d" % i, excl=True) for i in range(8)]

    for k in cst:
        S.dma(C[k].t[:], cd[k], writes=[C[k].R])
    for l in range(NL):
        S.dma(normw.t[:, l, :], norm_w_d[l].rearrange("(c p) -> p c", p=128), writes=[normw.R],
              allow_slow_non_contiguous=True)
        for g9 in range(9):
            S.dma(convw.t[:, l, g9, :], conv_w_d[l][:, g9 * 128:(g9 + 1) * 128].rearrange("k p -> p k"), writes=[convw.R],
                  allow_slow_non_contiguous=True)
        S.dma(nea.t[:, l, :], a_log_d[l].partition_broadcast(128), writes=[nea.R])
        S.dma(dtb.t[:, l, :], dt_bias_d[l].partition_broadcast(128), writes=[dtb.R])
        S.dma(gw.t[:, l, :], gdn_w_d[l].partition_broadcast(128), writes=[gw.R])
        for j in range(2):
            S.dma(wqk.t[:, l, j, :], qn_w_d[l].partition_broadcast(128), writes=[wqk.R])
            S.dma(wqk.t[:, l, 2 + j, :], kn_w_d[l].partition_broadcast(128), writes=[wqk.R])
    E("act", "activation", [nea.R], [nea.R], nea.t[:], nea.t[:], AF.Exp)
    E("dve", "tensor_scalar", [nea.R], [nea.R], nea.t[:], nea.t[:], -1.0, None, ALU.mult)
    E("dve", "tensor_scalar", [wqk.R], [wqk.R], wqk.t[:, :, 0:2, :], wqk.t[:, :, 0:2, :], 0.125, None, ALU.mult)

    bank_rr = [0]
    big_reg = nc.gpsimd.to_reg(1e30)

    def nbank(lo=0, hi=8):
        b = lo + bank_rr[0] % (hi - lo)
        bank_rr[0] += 1
        return b

    stg_rr = [0]

    def load_cols(slot, src2d_list):
        for g, src in src2d_list:
            st = stage[stg_rr[0] % 2]
            stg_rr[0] += 1
            S.dma(st.t[:], src.rearrange("(c p) n -> p c n", p=128), writes=[st.R])
            E("pool", "tensor_copy", [st.R], [wbuf[slot].r[g]], wbuf[slot].t[:, :, g * 128:(g + 1) * 128], st.t[:])

    def unit_cols(l, offs):
        return [(g, w_in_d[l][:, o:o + 128]) for g, o in enumerate(offs)]

    def phase_norm(l, ar):
        ss, junk, xn = ar["ss"], ar["junk"], ar["xn"]
        for i in range(NT):
            E("act", "activation", [X.r[i]], [junk.R, ss.R], junk.t[:], X.t[:, i, :], AF.Square,
              accum_out=ss.t[:, i:i + 1])
        E("act", "activation", [ss.R], [ss.R], ss.t[:], ss.t[:], AF.Ln, scale=1.0 / D, bias=EPS)
        E("act", "activation", [ss.R], [ss.R], ss.t[:], ss.t[:], AF.Exp, scale=-0.5)
        for i in range(NT):
            xt = xn[i % 2]
            E("dve", "tensor_scalar", [X.r[i], ss.R], [xt.R], xt.t[:], X.t[:, i, :], ss.t[:, i:i + 1], None, ALU.mult)
            b = nbank(0, 2)
            pb = ps[b][:].bitcast(BF16)
            for c in range(KC):
                TR(pb[:, c * 128:(c + 1) * 128], xt.t[:, c * 128:(c + 1) * 128], C["identb"].t[:],
                   [xt.R, C["identb"].R], [pr[b]], inc=(c == KC - 1))
            E("dve", "tensor_tensor", [pr[b], normw.R], [hT.r[i // KPB]],
              hT.t[:, :, i * 128:(i + 1) * 128], pb.rearrange("p (c n) -> p c n", c=KC),
              normw.t[:, l, :].unsqueeze(2).broadcast_to([128, KC, 128]), ALU.mult)

    def proj_fm(slot, g, tb, b, w=None):
        w = w or QB
        for c in range(KC):
            MM(ps[b][:, 0:w], wbuf[slot].t[:, c, g * 128:(g + 1) * 128], hT.t[:, c, tb * w:(tb + 1) * w],
               [wbuf[slot].r[g], hT.r[(tb * w) // QB]], [pr[b]], start=(c == 0), stop=(c == KC - 1), inc=(c == KC - 1))

    def proj_tm(slot, c0, ncol, i, out_ap, b):
        gs = sorted(set([c0 // 128, (c0 + ncol - 1) // 128]))
        gs = list(range(gs[0], gs[-1] + 1))
        for c in range(KC):
            MM(out_ap, hT.t[:, c, i * 128:(i + 1) * 128], wbuf[slot].t[:, c, c0:c0 + ncol],
               [wbuf[slot].r[g] for g in gs] + [hT.r[i // KPB]], [pr[b]],
               start=(c == 0), stop=(c == KC - 1), inc=(c == KC - 1))

    def silu_from_psum(b, ncol, et, out_ap, out_res, extra_reads=()):
        E("act", "activation", [pr[b]], [et.R], et.t[:, 0:ncol], ps[b][:, 0:ncol], AF.Exp, scale=-1.0)
        E("pool", "tensor_scalar", [et.R], [et.R], et.t[:, 0:ncol], et.t[:, 0:ncol], 1.0, None, ALU.add)
        E("dve", "reciprocal", [et.R], [et.R], et.t[:, 0:ncol], et.t[:, 0:ncol])
        E("dve", "tensor_tensor", [pr[b], et.R] + list(extra_reads), [out_res], out_ap, ps[b][:, 0:ncol],
          et.t[:, 0:ncol], ALU.mult)

    def bc_inproj_common(slot, ar, kind):
        zs, et = ar["zs"], ar["et"]
        for tb in range(NQB):
            b = nbank(0, 2)
            proj_fm(slot, 3, tb, b)
            silu_from_psum(b, QB, et[tb % 2], zs.t[:, tb * QB:(tb + 1) * QB], zs.R)

    def b_unit(l, u, slot, ar):
        qT, kTp, vt, zs = ar["qT"], ar["kTp"], ar["vt"], ar["zs"]
        for tb in range(NQB):
            b = nbank(0, 2)
            proj_fm(slot, 0, tb, b)
            E("act", "activation", [pr[b]], [qT.R], qT.t[:, tb * QB:(tb + 1) * QB], ps[b][:, 0:QB], AF.Copy, scale=0.125)
            b = nbank(0, 2)
            proj_fm(slot, 1, tb, b)
            E("act", "copy", [pr[b]], [kTp.r[0]], kTp.t[0:64, 0, tb * QB:(tb + 1) * QB], ps[b][0:64, 0:QB])
            E("dve", "tensor_copy", [pr[b]], [kTp.r[1]], kTp.t[64:128, 1, tb * QB:(tb + 1) * QB], ps[b][64:128, 0:QB])
        for i in range(NT):
            b = nbank(0, 2)
            proj_tm(slot, 256, 128, i, ps[b][:, 0:128], b)
            E("act", "copy", [pr[b]], [vt.R], vt.t[:, i, :], ps[b][:, 0:128])
        bc_inproj_common(slot, ar, "b")
        sp, Pb, wts, Rb = ar["sp"], ar["P"], ar["wts"], ar["Rb"]
        for qb in range(NQB):
            qbase = qb * QB
            nkt = (qb + 1) * KPB
            ob = 7
            zb = [2, 3]
            cb = [4, 5]
            cnt = 0
            for kt in range(nkt - 1, -1, -1):
                kbase = kt * 128
                diag = kbase + 127 >= qbase
                first = (kt == nkt - 1)
                last = (kt == 0)
                for h in range(2):
                    spb = sp[h][cnt % 2]
                    Pt = Pb[h][cnt % 2]
                    wt = wts[h][cnt % 2]
                    MM(ps[zb[h]][:, 0:QB], kTp.t[:, h, kbase:kbase + 128], qT.t[:, qbase:qbase + QB],
                       [kTp.r[h], qT.R], [pr[zb[h]]])
                    E("act", "activation", [pr[zb[h]]], [spb.R], spb.t[:, 0:QB], ps[zb[h]][:, 0:QB], AF.Exp, scale=-1.0)
                    E("act", "activation", [spb.R], [spb.R], spb.t[:, 0:QB], spb.t[:, 0:QB], AF.Ln, bias=1.0)
                    E("dve", "tensor_tensor", [pr[zb[h]], spb.R], [Pt.R], Pt.t[:, 0:QB], ps[zb[h]][:, 0:QB],
                      spb.t[:, 0:QB], ALU.add)
                    if diag:
                        E("pool", "affine_select", [Pt.R], [Pt.R], out=Pt.t[:, 0:QB], in_=Pt.t[:, 0:QB],
                          pattern=[[1, QB]], compare_op=ALU.is_gt, fill=0.0, base=qbase - kbase, channel_multiplier=-1)
                    MM(ps[cb[h]][:, 0:QB], C["tri_gt"].t[:], Pt.t[:, 0:QB], [C["tri_gt"].R, Pt.R], [pr[cb[h]]],
                       start=True, stop=first, inc=first)
                    if not first:
                        MM(ps[cb[h]][:, 0:QB], C["onesf"].t[:], Rb[h].t[:, 0:QB], [C["onesf"].R, Rb[h].R], [pr[cb[h]]],
                           start=False, stop=True)
                    E("dve", "tensor_tensor", [pr[cb[h]], spb.R], [spb.R], spb.t[:, 0:QB], ps[cb[h]][:, 0:QB],
                      spb.t[:, 0:QB], ALU.add)
                    if diag:
                        E("pool", "affine_select", [spb.R], [spb.R], out=spb.t[:, 0:QB], in_=spb.t[:, 0:QB],
                          pattern=[[1, QB]], compare_op=ALU.is_gt, fill=big_reg, base=qbase - kbase, channel_multiplier=-1)
                    E("act", "activation", [spb.R], [wt.R], wt.t[:, 0:QB], spb.t[:, 0:QB], AF.Exp, scale=-1.0)
                    if not last:
                        if first:
                            E("pool", "tensor_copy", [Pt.R], [Rb[h].R], Rb[h].t[:, 0:QB], Pt.t[:, 0:QB])
                        else:
                            E("pool", "tensor_tensor", [Pt.R, Rb[h].R], [Rb[h].R], out=Rb[h].t[:, 0:QB], in0=Rb[h].t[:, 0:QB],
                              in1=Pt.t[:, 0:QB], op=ALU.add)
                    MM(ps[ob][64 * h:64 * h + 64, 0:QB], vt.t[:, kt, 64 * h:64 * h + 64], wt.t[:, 0:QB],
                       [vt.R, wt.R], [pr[ob]], start=first, stop=last)
                cnt += 1
            E("dve", "tensor_tensor", [pr[ob], zs.R], [mixT.r[3 + u]], mixT.t[:, 3 + u, qbase:qbase + QB],
              ps[ob][:, 0:QB], zs.t[:, qbase:qbase + QB], ALU.mult)

    def c_unit(l, u, slot, ar):
        import os as _os
        CSTOP = int(_os.environ.get("CSTOP", "99"))
        qT, kTp, vt, zs = ar["qT"], ar["kTp"], ar["vt"], ar["zs"]
        qk, sq, s4, qkb = ar["qk"], ar["sq"], ar["s4"], ar["qkb"]
        rt = ar["rt"]
        for i in range(NT):
            b = nbank(0, 2)
            proj_tm(slot, 0, 384, i, ps[b][:, 0:384], b)
            E("act", "copy", [pr[b]], [vt.R], vt.t[:, i, :], ps[b][:, 256:384])
            q4 = qk[i % 2]
            E("act", "copy", [pr[b]], [q4.R], q4.t[:], ps[b][:, 0:256].rearrange("p (a d) -> p a d", a=4))
            if CSTOP <= 1:
                continue
            E("pool", "tensor_tensor", [q4.R], [sq.R], out=sq.t[:], in0=q4.t[:], in1=q4.t[:], op=ALU.mult)
            E("dve", "tensor_reduce", [sq.R], [s4.R], out=s4.t[:], in_=sq.t[:], axis=AX.X, op=ALU.add)
            E("act", "activation", [s4.R], [s4.R], s4.t[:], s4.t[:], AF.Ln, scale=1.0 / 64, bias=EPS)
            E("act", "activation", [s4.R], [s4.R], s4.t[:], s4.t[:], AF.Exp, scale=-0.5)
            E("dve", "tensor_tensor", [q4.R, s4.R], [q4.R], q4.t[:], q4.t[:],
              s4.t[:].unsqueeze(2).broadcast_to([128, 4, 64]), ALU.mult)
            E("pool", "tensor_tensor", [q4.R, wqk.R], [q4.R], out=q4.t[:], in0=q4.t[:], in1=wqk.t[:, l, :, :], op=ALU.mult)
            if CSTOP <= 2:
                continue
            qb_ = qkb[i % 2]
            E("act", "copy", [q4.R], [qb_.R], qb_.t[:], q4.t[:])
            cs = C["cost"].t[:, i, :].unsqueeze(1).broadcast_to([128, 4, 8])
            sn = C["sint"].t[:, i, :].unsqueeze(1).broadcast_to([128, 4, 8])
            x1 = q4.t[:, :, 0:8]
            x2 = q4.t[:, :, 8:16]
            E("dve", "tensor_tensor", [q4.R, C["cost"].R], [rt.R], rt.t[:, 0, :, :], x1, cs, ALU.mult)
            E("dve", "tensor_tensor", [q4.R, C["sint"].R], [rt.R], rt.t[:, 1, :, :], x2, sn, ALU.mult)
            E("dve", "tensor_tensor", [q4.R, C["cost"].R], [rt.R], rt.t[:, 2, :, :], x2, cs, ALU.mult)
            E("dve", "tensor_tensor", [q4.R, C["sint"].R], [rt.R], rt.t[:, 3, :, :], x1, sn, ALU.mult)
            E("dve", "tensor_tensor", [rt.R], [qb_.R], qb_.t[:, :, 0:8], rt.t[:, 0, :, :], rt.t[:, 1, :, :], ALU.subtract)
            E("dve", "tensor_tensor", [rt.R], [qb_.R], qb_.t[:, :, 8:16], rt.t[:, 2, :, :], rt.t[:, 3, :, :], ALU.add)
            if CSTOP <= 3:
                continue
            b2 = nbank(0, 2)
            pb = ps[b2][:].bitcast(BF16)
            qflat = qb_.t[:].rearrange("p a d -> p (a d)")
            TR(pb[:, 0:128], qflat[:, 0:128], C["identb"].t[:], [qb_.R, C["identb"].R], [pr[b2]], inc=False)
            TR(pb[:, 128:256], qflat[:, 128:256], C["identb"].t[:], [qb_.R, C["identb"].R], [pr[b2]])
            tq = ar["tq"][i % 2]
            E("dve", "tensor_copy", [pr[b2]], [tq.R], tq.t[:], pb[:, 0:256])
            E("act", "copy", [tq.R], [qT.R], qT.t[:, i * 128:(i + 1) * 128], tq.t[:, 0:128])
            E("pool", "tensor_copy", [tq.R], [kTp.r[0]], kTp.t[0:64, 0, i * 128:(i + 1) * 128], tq.t[0:64, 128:256])
            E("pool", "tensor_copy", [tq.R], [kTp.r[1]], kTp.t[64:128, 1, i * 128:(i + 1) * 128], tq.t[64:128, 128:256])
        if CSTOP <= 4:
            return
        bc_inproj_common(slot, ar, "c")
        if CSTOP <= 5:
            return
        Eb, EMb, rec = ar["E"], ar["EM"], ar["rec"]
        for qb in range(NQB):
            qbase = qb * QB
            nkt = (qb + 1) * KPB
            nb_, db_ = 6, 7
            sb = [2, 3, 4, 5]
            cnt = 0
            for kt in range(nkt - 1, -1, -1):
                kbase = kt * 128
                off = qbase - kbase + OFF0
                first = (kt == nkt - 1)
                last = (kt == 0)
                for h in range(2):
                    b = sb[cnt % 4]
                    Et = Eb[cnt % 2]
                    EMt = EMb[cnt % 2]
                    cnt += 1
                    MM(ps[b][:, 0:QB], kTp.t[:, h, kbase:kbase + 128], qT.t[:, qbase:qbase + QB],
                       [kTp.r[h], qT.R], [pr[b]])
                    E("act", "activation", [pr[b]], [Et.R], Et.t[:, 0:QB], ps[b][:, 0:QB], AF.Exp)
                    E("pool", "tensor_tensor", [Et.R, C["mtab"].R], [EMt.R], out=EMt.t[:, 0:QB], in0=Et.t[:, 0:QB],
                      in1=C["mtab"].t[:, off:off + QB], op=ALU.mult)
                    MM(ps[nb_][64 * h:64 * h + 64, 0:QB], vt.t[:, kt, 64 * h:64 * h + 64], EMt.t[:, 0:QB],
                       [vt.R, EMt.R], [pr[nb_]], start=first, stop=last, inc=False)
                    MM(ps[db_][64 * h:64 * h + 64, 0:QB], C["onesb"].t[:, 0:64], EMt.t[:, 0:QB],
                       [C["onesb"].R, EMt.R], [pr[db_]], start=first, stop=last)
            E("dve", "reciprocal", [pr[db_]], [rec.R], rec.t[:, 0:QB], ps[db_][:, 0:QB])
            E("dve", "tensor_tensor", [pr[nb_], rec.R], [rec.R], rec.t[:, 0:QB], ps[nb_][:, 0:QB], rec.t[:, 0:QB], ALU.mult)
            E("pool", "tensor_tensor", [rec.R, zs.R], [mixT.r[5 + u]], out=mixT.t[:, 5 + u, qbase:qbase + QB],
              in0=rec.t[:, 0:QB], in1=zs.t[:, qbase:qbase + QB], op=ALU.mult)

    def a_prep(l, ar):
        beta, gg, t12 = ar["beta"], ar["g"], ar["t12"]
        S.dma(wbas.t[:], w_in_d[l][:, OFF_BETA:OFF_BETA + 12].rearrange("(c p) n -> p c n", p=128), writes=[wbas.R])
        E("pool", "tensor_copy", [wbas.R], [wba.R], wba.t[:], wbas.t[:])
        b = nbank(0, 2)
        for i in range(NT):
            for c in range(KC):
                MM(ps[b][:, i * 12:(i + 1) * 12], hT.t[:, c, i * 128:(i + 1) * 128], wba.t[:, c, :],
                   [wba.R, hT.r[i // KPB]], [pr[b]], start=(c == 0), stop=(c == KC - 1),
                   inc=(c == KC - 1 and i == NT - 1))
        pv = ps[b][:, 0:NT * 12].rearrange("p (n k) -> p n k", k=12)
        E("act", "activation", [pr[b]], [beta.R], beta.t[:], pv[:, :, 0:6], AF.Exp, scale=-1.0)
        E("pool", "tensor_scalar", [beta.R], [beta.R], beta.t[:], beta.t[:], 1.0, None, ALU.add)
        E("dve", "reciprocal", [beta.R], [beta.R], beta.t[:], beta.t[:])
        E("dve", "tensor_tensor", [pr[b], dtb.R], [t12.R], t12.t[:], pv[:, :, 6:12],
          dtb.t[:, l, :].unsqueeze(1).broadcast_to([128, NT, 6]), ALU.add)
        E("act", "activation", [t12.R], [t12.R], t12.t[:], t12.t[:], AF.Exp)
        E("act", "activation", [t12.R], [t12.R], t12.t[:], t12.t[:], AF.Ln, bias=1.0)
        E("dve", "tensor_tensor", [t12.R, nea.R], [gg.R], gg.t[:], t12.t[:],
          nea.t[:, l, :].unsqueeze(1).broadcast_to([128, NT, 6]), ALU.mult)

    def a_unit(l, u, slot, ar):
        beta, gg = ar["beta"], ar["g"]
        xb, acc, et, yq, qkvT = ar["xb"], ar["acc"], ar["et"], ar["yq"], ar["qkvT"]
        gz = ar["gz"]
        Ssb = ar["S"]
        identf, tri_le, onesf = C["identf"], C["tri_le"], C["onesf"]
        E("pool", "memset", [], [Ssb.R], Ssb.t[:], 0.0)
        for g in range(3):
            E("pool", "memset", [], [xb[g].R], xb[g].t[:, 0:3], 0.0)
        for tb in range(T // QA):
            for g in range(3):
                b = nbank(0, 2)
                proj_fm(slot, g, tb, b, QA)
                gi = g * 3 + u
                xg = xb[g]
                E("act", "copy", [pr[b]], [xg.R], xg.t[:, 3:3 + QA], ps[b][:, 0:QA])
                a_ = acc[g % 2]
                E("dve", "tensor_scalar", [xg.R, convw.R], [a_.R], a_.t[:, 0:QA], xg.t[:, 0:QA],
                  convw.t[:, l, gi, 0:1], None, ALU.mult)
                for k in range(1, 4):
                    E("dve", "scalar_tensor_tensor", [xg.R, convw.R, a_.R], [a_.R], a_.t[:, 0:QA], xg.t[:, k:k + QA],
                      convw.t[:, l, gi, k:k + 1], a_.t[:, 0:QA], ALU.mult, ALU.add)
                E("pool", "tensor_copy", [xg.R], [xg.R], xg.t[:, 0:3], xg.t[:, QA:QA + 3])
                e_ = et[g % 2]
                E("act", "activation", [a_.R], [e_.R], e_.t[:, 0:QA], a_.t[:, 0:QA], AF.Exp, scale=-1.0)
                E("pool", "tensor_scalar", [e_.R], [e_.R], e_.t[:, 0:QA], e_.t[:, 0:QA], 1.0, None, ALU.add)
                E("dve", "reciprocal", [e_.R], [e_.R], e_.t[:, 0:QA], e_.t[:, 0:QA])
                if g == 2:
                    E("dve", "tensor_tensor", [a_.R, e_.R], [qkvT[2].R], qkvT[2].t[:, 0:QA], a_.t[:, 0:QA], e_.t[:, 0:QA], ALU.mult)
                else:
                    y = yq[g]
                    E("dve", "tensor_tensor", [a_.R, e_.R], [y.R], y.t[:, 0:QA], a_.t[:, 0:QA], e_.t[:, 0:QA], ALU.mult)
                    E("pool", "tensor_tensor", [y.R], [e_.R], out=e_.t[:, 0:QA], in0=y.t[:, 0:QA], in1=y.t[:, 0:QA], op=ALU.mult)
                    b2 = nbank(0, 2)
                    MM(ps[b2][:, 0:QA], C["blockones"].t[:], e_.t[:, 0:QA], [C["blockones"].R, e_.R], [pr[b2]])
                    E("act", "activation", [pr[b2]], [e_.R], e_.t[:, 0:QA], ps[b2][:, 0:QA], AF.Ln, bias=EPS)
                    E("act", "activation", [e_.R], [e_.R], e_.t[:, 0:QA], e_.t[:, 0:QA], AF.Exp, scale=-0.5)
                    E("dve", "scalar_tensor_tensor", [y.R, e_.R], [qkvT[g].R], qkvT[g].t[:, 0:QA], y.t[:, 0:QA],
                      0.125 if g == 0 else 1.0, e_.t[:, 0:QA], ALU.mult, ALU.mult)
            b = nbank(0, 2)
            for j in range(KA):
                i = tb * KA + j
                proj_tm(slot, 384, 128, i, ps[b][:, j * 128:(j + 1) * 128], b)
            silu_from_psum(b, QA, et[0], gz.t[:].rearrange("p a d -> p (a d)"), gz.R)
            E("pool", "tensor_tensor", [gz.R, gw.R], [gz.R], out=gz.t[:],
              in0=gz.t[:],
              in1=gw.t[:, l, :].unsqueeze(1).broadcast_to([128, KA * 2, 64]), op=ALU.mult)
            for j in range(KA):
                a_tile(l, u, tb * KA + j, j, ar)

    def a_tile(l, u, i, j, ar):
        import os as _os
        ASTOP = int(_os.environ.get("ASTOP", "99"))
        if ASTOP <= 1:
            return
        beta, gg = ar["beta"], ar["g"]
        qkvT, gz, Ssb = ar["qkvT"], ar["gz"], ar["S"]
        identf, tri_le, onesf = C["identf"], C["tri_le"], C["onesf"]
        k = i % 2
        kv, gcs, egc, sbeg, gb, d0, DT, DTs, DTi = (ar[n][k] for n in ("kv", "gcs", "egc", "sbeg", "gb", "d0", "DT", "DTs", "DTi"))
        egr, kTp, M0p, qkT, MN, Pm, bv, usb, wTp, qgTp, vnew = (ar[n][k] for n in ("egr", "kTp", "M0p", "qkT", "MN", "Pm", "bv", "usb", "wTp", "qgTp", "vnew"))
        osb, osq, oss, ot, obf = (ar[n][k] for n in ("osb", "osq", "oss", "ot", "obf"))
        cs = slice(j * 128, (j + 1) * 128)
        bi = beta.t[:, i, 2 * u:2 * u + 2]
        gi = gg.t[:, i, 2 * u:2 * u + 2]
        qT_, kT_, vT_ = qkvT[0], qkvT[1], qkvT[2]
        b = 2
        TR(ps[b][:, 0:128], kT_.t[:, cs], identf.t[:], [kT_.R, identf.R], [pr[b]], inc=False)
        TR(ps[b][:, 128:256], vT_.t[:, cs], identf.t[:], [vT_.R, identf.R], [pr[b]], inc=False)
        MM(ps[b][:, 256:258], tri_le.t[:], gi, [tri_le.R, gg.R], [pr[b]])
        E("act", "copy", [pr[b]], [kv.R], kv.t[:], ps[b][:, 0:256].rearrange("p (a h d) -> p a h d", a=2, h=2))
        E("dve", "tensor_copy", [pr[b]], [gcs.R], gcs.t[:], ps[b][:, 256:258])
        E("act", "activation", [gcs.R], [egc.R], egc.t[:], gcs.t[:], AF.Exp)
        E("pool", "tensor_tensor", [egc.R, beta.R], [sbeg.R], out=sbeg.t[:], in0=egc.t[:], in1=bi, op=ALU.mult)
        if ASTOP <= 2:
            return
        for h in range(2):
            E("pool", "tensor_scalar", [onesf.R, gg.R], [gb.R], gb.t[:, h, :], onesf.t[:], gi[:, h:h + 1], None, ALU.mult)
            E("pool", "tensor_scalar", [onesf.R, gg.R], [gb.R], gb.t[:, 2, 64 * h:64 * h + 64], onesf.t[:, 0:64], gi[:, h:h + 1], None, ALU.mult)
        b = 3
        for h in range(3):
            MM(ps[b][:, h * 128:(h + 1) * 128], gb.t[:, h, :], tri_le.t[:], [gb.R, tri_le.R], [pr[b]], inc=(h == 2))
        for h in range(2):
            E("dve", "tensor_scalar", [pr[b], gcs.R], [d0.R], d0.t[:, h, :], ps[b][:, h * 128:(h + 1) * 128],
              gcs.t[:, h:h + 1], 0.0, ALU.subtract, ALU.min)
        E("act", "activation", [d0.R], [DT.R], DT.t[:], d0.t[:], AF.Exp)
        E("act", "activation", [pr[b]], [egr.R], egr.t[:], ps[b][:, 256:384], AF.Exp)
        E("pool", "tensor_tensor", [DT.R, C["nst2"].R], [DTs.R], out=DTs.t[:], in0=DT.t[:], in1=C["nst2"].t[:], op=ALU.mult)
        E("pool", "tensor_tensor", [DT.R, C["incl2"].R], [DTi.R], out=DTi.t[:], in0=DT.t[:], in1=C["incl2"].t[:], op=ALU.mult)
        if ASTOP <= 3:
            return
        E("pool", "tensor_copy", [kT_.R], [kTp.R], kTp.t[0:64, 0, :], kT_.t[0:64, cs])
        E("pool", "tensor_copy", [kT_.R], [kTp.R], kTp.t[64:128, 1, :], kT_.t[64:128, cs])
        A4 = int(_os.environ.get("A4", "99"))
        if A4 <= 1:
            return
        b = 4
        AV = _os.environ.get("AV", "kq")
        for h in range(2):
            if "x" in AV:
                MM(ps[b][:, h * 128:(h + 1) * 128], kT_.t[:, cs], kT_.t[:, cs], [kTp.R, kT_.R], [pr[b]], inc=(h == 1))
            if "z" in AV:
                MM(ps[b][:, h * 128:(h + 1) * 128], gb.t[:, h, :], C["tri_le"].t[:], [gb.R, tri_le.R], [pr[b]], inc=(h == 1))
            if "y" in AV:
                MM(ps[b][:, h * 128:(h + 1) * 128], kTp.t[:, h, :], C["tri_le"].t[:], [kTp.R, kT_.R], [pr[b]], inc=(h == 1))
            if "k" in AV:
                MM(ps[b][:, h * 128:(h + 1) * 128], kTp.t[:, h, :], kT_.t[:, cs], [kTp.R, kT_.R], [pr[b]], inc=("q" not in AV and h == 1))
        for h in range(2):
            if "q" in AV:
                MM(ps[b][:, 256 + h * 128:256 + (h + 1) * 128], kTp.t[:, h, :], qT_.t[:, cs], [kTp.R, qT_.R], [pr[b]], inc=(h == 1))
        if A4 <= 2:
            return
        E("dve", "tensor_tensor", [pr[b], DTs.R], [M0p.R], M0p.t[:], ps[b][:, 0:256].rearrange("p (h n) -> p h n", h=2), DTs.t[:], ALU.mult)
        E("dve", "tensor_tensor", [pr[b], DTi.R], [qkT.R], qkT.t[:], ps[b][:, 256:512].rearrange("p (h n) -> p h n", h=2), DTi.t[:], ALU.mult)
        if ASTOP <= 4:
            return
        b = 5
        for h in range(2):
            TR(ps[b][:, h * 128:(h + 1) * 128], M0p.t[:, h, :], identf.t[:], [M0p.R, identf.R], [pr[b]], inc=(h == 1))
        mn = MN[0]
        for h in range(2):
            E("dve", "tensor_scalar", [pr[b], beta.R], [mn.R], mn.t[:, 2 + h, :], ps[b][:, h * 128:(h + 1) * 128], bi[:, h:h + 1], None, ALU.mult)
        b = 5
        for h in range(2):
            TR(ps[b][:, h * 128:(h + 1) * 128], mn.t[:, 2 + h, :], identf.t[:], [mn.R, identf.R], [pr[b]], inc=(h == 1))
        E("act", "copy", [pr[b]], [mn.R], mn.t[:, 0:2, :], ps[b][:, 0:256].rearrange("p (h n) -> p h n", h=2))
        E("dve", "tensor_tensor", [pr[b], C["ident2"].R], [Pm.R], Pm.t[:], ps[b][:, 0:256].rearrange("p (h n) -> p h n", h=2), C["ident2"].t[:], ALU.add)
        if ASTOP <= 5:
            return
        for lv in range(1, 7):
            mo = MN[(lv - 1) % 2]
            mw = MN[lv % 2]
            b = 6
            if lv < 6:
                for h in range(2):
                    MM(ps[b][:, h * 128:(h + 1) * 128], mo.t[:, 2 + h, :], mo.t[:, h, :], [mo.R], [pr[b]], inc=False)
            for h in range(2):
                MM(ps[b][:, 256 + h * 128:256 + (h + 1) * 128], mo.t[:, h, :], mo.t[:, 2 + h, :], [mo.R], [pr[b]], inc=(h == 1))
            if lv < 6:
                E("act", "copy", [pr[b]], [mw.R], mw.t[:, 0:2, :], ps[b][:, 0:256].rearrange("p (h n) -> p h n", h=2))
            E("dve", "tensor_copy", [pr[b]], [mw.R], mw.t[:, 2:4, :], ps[b][:, 256:512].rearrange("p (h n) -> p h n", h=2))
            b = 7
            for h in range(2):
                MM(ps[b][:, h * 128:(h + 1) * 128], mw.t[:, 2 + h, :], Pm.t[:, h, :], [mw.R, Pm.R], [pr[b]], inc=(h == 1))
            E("dve", "tensor_tensor", [pr[b], Pm.R], [Pm.R], Pm.t[:], Pm.t[:], ps[b][:, 0:256].rearrange("p (h n) -> p h n", h=2), ALU.add)
        if ASTOP <= 6:
            return
        E("pool", "tensor_tensor", [kv.R, beta.R], [bv.R], out=bv.t[:, 0, :, :], in0=kv.t[:, 1, :, :],
          in1=bi.unsqueeze(2).broadcast_to([128, 2, 64]), op=ALU.mult)
        E("pool", "tensor_tensor", [kv.R, sbeg.R], [bv.R], out=bv.t[:, 1, :, :], in0=kv.t[:, 0, :, :],
          in1=sbeg.t[:].unsqueeze(2).broadcast_to([128, 2, 64]), op=ALU.mult)
        E("pool", "tensor_tensor", [kv.R, DT.R], [bv.R], out=bv.t[:, 2, :, :], in0=kv.t[:, 0, :, :],
          in1=DT.t[:, :, 127:128].broadcast_to([128, 2, 64]), op=ALU.mult)
        b = 3
        for h in range(2):
            MM(ps[b][:, h * 64:(h + 1) * 64], Pm.t[:, h, :], bv.t[:, 0, h, :], [Pm.R, bv.R], [pr[b]], inc=False)
        for h in range(2):
            MM(ps[b][64 * h:64 * h + 64, 128:256], bv.t[:, 1, h, :], Pm.t[:, h, :], [Pm.R, bv.R], [pr[b]], inc=(h == 1))
        E("act", "copy", [pr[b]], [usb.R], usb.t[:], ps[b][:, 0:128].rearrange("p (h d) -> p h d", h=2))
        E("dve", "tensor_copy", [pr[b]], [wTp.R], wTp.t[0:64, 0, :], ps[b][0:64, 128:256])
        E("dve", "tensor_copy", [pr[b]], [wTp.R], wTp.t[64:128, 1, :], ps[b][64:128, 128:256])
        E("pool", "tensor_tensor", [qT_.R, egr.R], [qgTp.R], out=qgTp.t[0:64, 0, :], in0=qT_.t[0:64, cs], in1=egr.t[0:64, :], op=ALU.mult)
        E("pool", "tensor_tensor", [qT_.R, egr.R], [qgTp.R], out=qgTp.t[64:128, 1, :], in0=qT_.t[64:128, cs], in1=egr.t[64:128, :], op=ALU.mult)
        if ASTOP <= 7:
            return
        b = 4
        for h in range(2):
            MM(ps[b][:, h * 64:(h + 1) * 64], wTp.t[:, h, :], Ssb.t[:], [wTp.R, Ssb.R], [pr[b]], inc=(h == 1))
        E("dve", "tensor_tensor", [usb.R, pr[b]], [vnew.R], vnew.t[:], usb.t[:], ps[b][:, 0:128].rearrange("p (h d) -> p h d", h=2), ALU.subtract)
        bo = 6
        for h in range(2):
            MM(ps[bo][:, h * 64:(h + 1) * 64], qgTp.t[:, h, :], Ssb.t[:], [qgTp.R, Ssb.R], [pr[bo]], start=True, stop=False, inc=False)
            MM(ps[bo][:, h * 64:(h + 1) * 64], qkT.t[:, h, :], vnew.t[:, h, :], [qkT.R, vnew.R], [pr[bo]], start=False, stop=True, inc=(h == 1))
        b = 7
        for h in range(2):
            MM(ps[b][64 * h:64 * h + 64, 0:64], bv.t[:, 2, h, :], vnew.t[:, h, :], [bv.R, vnew.R], [pr[b]], inc=(h == 1))
        E("dve", "scalar_tensor_tensor", [Ssb.R, egr.R, pr[b]], [Ssb.R], Ssb.t[:], Ssb.t[:], egr.t[:, 127:128], ps[b][:, 0:64], ALU.mult, ALU.add)
        if ASTOP <= 8:
            return
        E("act", "copy", [pr[bo]], [osb.R], osb.t[:], ps[bo][:, 0:128].rearrange("p (h d) -> p h d", h=2))
        E("pool", "tensor_tensor", [osb.R], [osq.R], out=osq.t[:], in0=osb.t[:], in1=osb.t[:], op=ALU.mult)
        E("dve", "tensor_reduce", [osq.R], [oss.R], out=oss.t[:], in_=osq.t[:], axis=AX.X, op=ALU.add)
        E("act", "activation", [oss.R], [oss.R], oss.t[:], oss.t[:], AF.Ln, scale=1.0 / 64, bias=EPS)
        E("act", "activation", [oss.R], [oss.R], oss.t[:], oss.t[:], AF.Exp, scale=-0.5)
        E("dve", "tensor_tensor", [osb.R, oss.R], [ot.R], ot.t[:], osb.t[:], oss.t[:].unsqueeze(2).broadcast_to([128, 2, 64]), ALU.mult)
        E("dve", "tensor_tensor", [ot.R, gz.R], [obf.R], obf.t[:], ot.t[:], gz.t[:, 2 * j:2 * j + 2, :], ALU.mult)
        b = nbank(0, 2)
        pb = ps[b][:].bitcast(BF16)
        TR(pb[:, 0:128], obf.t[:].rearrange("p h d -> p (h d)"), C["identb"].t[:], [obf.R, C["identb"].R], [pr[b]])
        E("dve", "tensor_copy", [pr[b]], [mixT.r[u]], mixT.t[:, u, i * 128:(i + 1) * 128], pb[:, 0:128])

    def phase_out(l):
        for hf in range(2):
            for i in range(NT):
                b = nbank(0, 4)
                for c in range(KC):
                    MM(ps[b][:, 0:512], mixT.t[:, c, i * 128:(i + 1) * 128], wbuf[hf].t[:, c, :],
                       [mixT.r[c]] + wbuf[hf].r, [pr[b]], start=(c == 0), stop=(c == KC - 1), inc=(c == KC - 1))
                E("dve", "tensor_tensor", [pr[b], X.r[i]], [X.r[i]], X.t[:, i, hf * 512:(hf + 1) * 512],
                  X.t[:, i, hf * 512:(hf + 1) * 512], ps[b][:, 0:512], ALU.add)

    import contextlib

    class Arena:
        def __init__(self):
            self.stack = contextlib.ExitStack()

        def tl(self, name, shape, dt):
            t = Tl.__new__(Tl)
            t.t = self.stack.enter_context(nc.sbuf_tensor(name, list(shape), dt))
            t.r = [Res(name)]
            t.R = t.r[0]
            return t

        def close(self):
            S.barrier()
            self.stack.close()

    uid = [0]

    def nm(s):
        uid[0] += 1
        return "%s_%d" % (s, uid[0])

    for s in range(NSEQ):
        for i0 in range(0, NT, 4):
            n = min(4, NT - i0)
            S.dma(X.t[:, i0:i0 + n, :], x_d[s, i0 * 128:(i0 + n) * 128, :].rearrange("(n p) d -> p n d", p=128),
                  writes=[X.r[i] for i in range(i0, i0 + n)])
        for l in range(NL):
            units = [("a", u, [OFF_QA + 128 * u, OFF_KA + 128 * u, OFF_VA + 128 * u, OFF_ZA + 128 * u]) for u in range(3)]
            units += [("b", u, [OFF_QB + 128 * u, OFF_KB + 128 * u, OFF_VB + 128 * u, OFF_ZB + 128 * u]) for u in range(2)]
            units += [("c", u, [OFF_QC + 128 * u, OFF_KC + 128 * u, OFF_VC + 128 * u, OFF_ZC + 128 * u]) for u in range(3)]
            load_cols(0, unit_cols(l, units[0][2]))
            A = Arena()
            ar = {"ss": A.tl(nm("ss"), [128, NT], F32), "junk": A.tl(nm("junk"), [128, D], BF16),
                  "xn": [A.tl(nm("xn"), [128, D], BF16) for _ in range(2)]}
            phase_norm(l, ar)
            A.close()
            A = Arena()
            ar = {"beta": A.tl(nm("beta"), [128, NT, 6], F32), "g": A.tl(nm("g"), [128, NT, 6], F32),
                  "t12": A.tl(nm("t12"), [128, NT, 6], F32),
                  "xb": [A.tl(nm("xb"), [128, QA + 3], F32) for _ in range(3)],
                  "qkvT": [A.tl(nm("qkvT"), [128, QA], F32) for _ in range(3)],
                  "gz": A.tl(nm("gz"), [128, 2 * KA, 64], F32),
                  "S": A.tl(nm("S"), [128, 64], F32)}
            _acc = A.tl(nm("acc"), [128, QA], F32)
            ar["acc"] = [_acc, _acc]
            ar["et"] = [A.tl(nm("et"), [128, QA], F32) for _ in range(2)]
            _yq = A.tl(nm("yq"), [128, QA], F32)
            ar["yq"] = [_yq, _yq]

            def two(n_, shp, dt=F32):
                t_ = A.tl(nm(n_), shp, dt)
                return [t_, t_]
            ar.update({"kv": two("kv", [128, 2, 2, 64]), "gcs": two("gcs", [128, 2]), "egc": two("egc", [128, 2]),
                       "sbeg": two("sbeg", [128, 2]), "gb": two("gb", [128, 3, 128]), "d0": two("d0", [128, 2, 128]),
                       "DTs": two("DTs", [128, 2, 128]), "DTi": two("DTi", [128, 2, 128]),
                       "egr": two("egr", [128, 128]), "kTp": two("kTpA", [128, 2, 128]), "M0p": two("M0p", [128, 2, 128]),
                       "qkT": two("qkT", [128, 2, 128]),
                       "Pm": two("Pm", [128, 2, 128]), "bv": two("bv", [128, 3, 2, 64]), "usb": two("usb", [128, 2, 64]),
                       "wTp": two("wTp", [128, 2, 128]), "qgTp": two("qgTp", [128, 2, 128]), "vnew": two("vnew", [128, 2, 64]),
                       "osb": two("osb", [128, 2, 64]), "osq": two("osq", [128, 2, 64]), "oss": two("oss", [128, 2]),
                       "ot": two("ot", [128, 2, 64]), "obf": two("obf", [128, 2, 64], BF16)})
            ar["DT"] = ar["d0"]
            _mn = [A.tl(nm("MN"), [128, 4, 128], F32) for _ in range(2)]
            ar["MN"] = [_mn, _mn]
            for n_ in ("kTp", "wTp", "qgTp"):
                E("pool", "memset", [], [ar[n_][0].R], ar[n_][0].t[:], 0.0)
            if "a" in phases:
                a_prep(l, ar)
            ui = 0
            for (kind, u, offs) in units[0:3]:
                load_cols((ui + 1) % 2, unit_cols(l, units[ui + 1][2]))
                if "a" in phases:
                    a_unit(l, u, ui % 2, ar)
                ui += 1
            A.close()
            for kind_, ulist in (("b", units[3:5]), ("c", units[5:8])):
                A = Arena()
                ar = {"qT": A.tl(nm("qT"), [128, T], BF16), "kTp": A.tl(nm("kTp"), [128, 2, T], BF16),
                      "vt": A.tl(nm("vt"), [128, NT, 128], BF16), "zs": A.tl(nm("zs"), [128, T], BF16)}
                _et = A.tl(nm("etb"), [128, QB], F32)
                ar["et"] = [_et, _et]
                if kind_ == "b":
                    ar["sp"] = [[A.tl(nm("sp"), [128, QB], F32) for _ in range(2)] for _ in range(2)]
                    _P = [A.tl(nm("P"), [128, QB], F32) for _ in range(2)]
                    ar["P"] = [[_P[0], _P[0]], [_P[1], _P[1]]]
                    _w = [A.tl(nm("wts"), [128, QB], BF16) for _ in range(2)]
                    ar["wts"] = [[_w[0], _w[0]], [_w[1], _w[1]]]
                    ar["Rb"] = [A.tl(nm("Rb"), [128, QB], F32) for _ in range(2)]
                else:
                    ar.update({"E": [A.tl(nm("E"), [128, QB], BF16) for _ in range(2)],
                               "EM": [A.tl(nm("EM"), [128, QB], BF16) for _ in range(2)],
                               "rec": A.tl(nm("rec"), [128, QB], F32),
                               "qk": [A.tl(nm("qk"), [128, 4, 64], F32) for _ in range(2)],
                               "qkb": [A.tl(nm("qkb"), [128, 4, 64], BF16) for _ in range(2)],
                               "sq": A.tl(nm("sq"), [128, 4, 64], F32), "s4": A.tl(nm("s4"), [128, 4], F32),
                               "rt": A.tl(nm("rt"), [128, 4, 4, 8], F32),
                               "tq": [A.tl(nm("tq"), [128, 256], BF16) for _ in range(2)]})
                ar["kTp"].r = [Res("kTp0"), Res("kTp1")]
                E("pool", "memset", [], ar["kTp"].r, ar["kTp"].t[:], 0.0)
                for (kind, u, offs) in ulist:
                    if ui + 1 < len(units):
                        load_cols((ui + 1) % 2, unit_cols(l, units[ui + 1][2]))
                    else:
                        load_cols((ui + 1) % 2, [(g, w_out_d[l][:, g * 128:(g + 1) * 128]) for g in range(4)])
                    if kind == "b" and "b" in phases:
                        b_unit(l, u, ui % 2, ar)
                    if kind == "c" and "c" in phases:
                        c_unit(l, u, ui % 2, ar)
                    ui += 1
                A.close()
            load_cols(1, [(g, w_out_d[l][:, 512 + g * 128:512 + (g + 1) * 128]) for g in range(4)])
            if dbg and l == NL - 1 and s == NSEQ - 1:
                A = Arena()
                dm = A.tl(nm("dm"), [128, T], F32)
                for c in range(KC):
                    E("dve", "tensor_copy", [mixT.r[c]], [dm.R], dm.t[:], mixT.t[:, c, :])
                    S.dma(dbg_d[:, c, :], dm.t[:], reads=[dm.R], writes=[Res("dbgout")])
                E("pool", "memset", [], [dm.R], dm.t[:, 0:1], 0.0)
                A.close()
            phase_out(l)
            S.barrier()
        outres = Res("out")
        for i0 in range(0, NT, 4):
            n = min(4, NT - i0)
            S.dma(out_d[s, i0 * 128:(i0 + n) * 128, :].rearrange("(n p) d -> p n d", p=128), X.t[:, i0:i0 + n, :],
                  reads=[X.r[i] for i in range(i0, i0 + n)], writes=[outres])
        S.finish([outres])
    return nc, cst, S


_CACHE = {}


def kernel(x, norm_w, w_in, conv_w, a_log, dt_bias, gdn_norm_w, q_norm_w, k_norm_w, w_out):
    n_cores = 8
    B, T, _ = x.shape
    NSEQ = B // n_cores
    NL = w_in.shape[0]
    key = (T, NSEQ, NL)
    if key not in _CACHE:
        _CACHE[key] = build_program(T, NSEQ, NL)
    nc, cst, _ = _CACHE[key]
    f = lambda a: np.ascontiguousarray(np.asarray(a, dtype=np.float32))
    shared = {"w_in": f(w_in), "w_out": f(w_out), "norm_w": f(norm_w), "conv_w": f(conv_w), "a_log": f(a_log),
              "dt_bias": f(dt_bias), "gdn_norm_w": f(gdn_norm_w), "q_norm_w": f(q_norm_w), "k_norm_w": f(k_norm_w)}
    for k, v in cst.items():
        shared["c_" + k] = v
    xs = f(x)
    in_maps = []
    for c in range(n_cores):
        m = dict(shared)
        m["x"] = xs[c * NSEQ:(c + 1) * NSEQ]
        in_maps.append(m)
    res = run_bass_kernel_spmd(nc, in_maps, core_ids=list(range(n_cores)))
    return np.concatenate([np.asarray(r["out"], dtype=np.float32) for r in res.results], axis=0)
```

```python
import numpy as np
import ml_dtypes
import concourse.bass as bass
import concourse.mybir as mybir
from concourse.bass_utils import run_bass_kernel_spmd

F32 = mybir.dt.float32
BF16 = mybir.dt.bfloat16
AF = mybir.ActivationFunctionType
ALU = mybir.AluOpType
AX = mybir.AxisListType

D = 1024
KC = 8
IN_COLS = 4108
OFF_QA, OFF_KA, OFF_VA, OFF_ZA, OFF_BETA, OFF_ALPHA = 0, 384, 768, 1152, 1536, 1542
OFF_QB, OFF_KB, OFF_VB, OFF_ZB = 1548, 1804, 2060, 2316
OFF_QC, OFF_KC, OFF_VC, OFF_ZC = 2572, 2956, 3340, 3724
EPS = 1e-6
ROPE_THETA = 500000.0


class Res:
    __slots__ = ("name", "w", "r", "excl")

    def __init__(self, name, excl=False):
        self.name = name
        self.w = None
        self.r = []
        self.excl = excl


class Sched:
    def __init__(self, nc, n_dma_sems=24):
        self.nc = nc
        self.eng = {"pe": nc.tensor, "act": nc.scalar, "dve": nc.vector, "pool": nc.gpsimd, "sp": nc.sync}
        self.sems = {}
        self.cnt = {}
        for e in ("pe", "act", "dve", "pool"):
            self.sems[e] = nc.alloc_semaphore("s_" + e)
            self.cnt[e] = 0
        self.dma_sems = []
        for i in range(n_dma_sems):
            k = "d%d" % i
            self.sems[k] = nc.alloc_semaphore("s_" + k)
            self.cnt[k] = 0
            self.dma_sems.append(k)
        self.dma_rr = 0
        self.waited = {}
        self.pe_pending = False
        self.nwaits = 0
        self.nins = 0

    def _wait(self, e, key, val):
        if val <= 0:
            return
        if e == "pe" and key == "pe":
            return
        if key == "pe" and val > self.cnt["pe"]:
            raise RuntimeError("wait on a PE group that has not been closed with inc=True")
        if self.waited.get((e, key), 0) >= val:
            return
        self.waited[(e, key)] = val
        self.eng[e].wait_ge(self.sems[key], val)
        self.nwaits += 1

    def _deps(self, e, reads, writes):
        for t in reads:
            if t.w is not None:
                self._wait(e, *t.w)
        for t in writes:
            if t.w is not None:
                self._wait(e, *t.w)
            for rr in t.r:
                self._wait(e, *rr)

    @staticmethod
    def _compact(lst):
        best = {}
        for k, v in lst:
            if best.get(k, 0) < v:
                best[k] = v
        return list(best.items())

    def _record(self, key, val, reads, writes):
        for t in reads:
            t.r.append((key, val))
            if len(t.r) > 16:
                t.r = self._compact(t.r)
        for t in writes:
            t.w = (key, val)
            t.r = []

    def op(self, e, ins_fn, reads=(), writes=(), inc=True):
        if any(t.excl for t in reads):
            writes = list(writes) + [t for t in reads if t.excl]
            reads = [t for t in reads if not t.excl]
        self._deps(e, reads, writes)
        ins = ins_fn()
        self.nins += 1
        if e == "pe" and not inc:
            self.pe_pending = True
            self._record("pe", self.cnt["pe"] + 1, reads, writes)
            return ins
        if e == "pe":
            self.pe_pending = False
        self.cnt[e] += 1
        ins.then_inc(self.sems[e], 1)
        self._record(e, self.cnt[e], reads, writes)
        return ins

    def dma(self, out, in_, reads=(), writes=(), q="sp", **kw):
        key = self.dma_sems[self.dma_rr % len(self.dma_sems)]
        self.dma_rr += 1
        self._wait(q, key, self.cnt[key])
        self._deps(q, reads, writes)
        ins = self.eng[q].dma_start(out=out, in_=in_, **kw)
        self.nins += 1
        self.cnt[key] += 16
        ins.then_inc(self.sems[key], 16)
        self._record(key, self.cnt[key], reads, writes)
        return ins

    def barrier(self):
        assert not self.pe_pending
        for e in ("pe", "act", "dve", "pool"):
            for k in ("pe", "act", "dve", "pool"):
                if k != e:
                    self._wait(e, k, self.cnt[k])

    def finish(self, out_res):
        for t in out_res:
            if t.w is not None:
                self._wait("sp", *t.w)


def _consts(T):
    NT = T // 128
    QB = min(512, T)
    KPB = QB // 128
    c = {}
    idx = np.arange(128)
    c["identf"] = np.eye(128, dtype=np.float32)
    c["identb"] = np.eye(128, dtype=np.float32).astype(ml_dtypes.bfloat16)
    c["tri_gt"] = (idx[:, None] > idx[None, :]).astype(np.float32)
    c["tri_le"] = (idx[:, None] <= idx[None, :]).astype(np.float32)
    c["onesf"] = np.ones((128, 128), np.float32)
    c["onesb"] = np.ones((128, 64), np.float32).astype(ml_dtypes.bfloat16)
    bo = np.zeros((128, 128), np.float32)
    bo[:64, :64] = 1.0
    bo[64:, 64:] = 1.0
    c["blockones"] = bo
    incl = (idx[:, None] <= idx[None, :]).astype(np.float32)
    nst = -(idx[:, None] < idx[None, :]).astype(np.float32)
    c["incl2"] = np.stack([incl, incl], 1)
    c["nst2"] = np.stack([nst, nst], 1)
    c["ident2"] = np.stack([np.eye(128, dtype=np.float32)] * 2, 1)
    OFF0 = (KPB - 1) * 128
    W = OFF0 + (T - QB) + QB
    dl = np.arange(W)[None, :] - idx[:, None] - OFF0
    m = ((dl >= 0) & (dl <= 128)).astype(np.float32)
    m += ((dl >= 0) & (dl <= 512) & (dl % 4 == 0)).astype(np.float32)
    m += ((dl >= 0) & (dl <= 2048) & (dl % 16 == 0)).astype(np.float32)
    c["mtab"] = m.astype(ml_dtypes.bfloat16)
    half = 8
    inv_freq = (ROPE_THETA ** (-np.arange(half, dtype=np.float32) / half)).astype(np.float32)
    pos = np.arange(T, dtype=np.float32)
    ang = (pos[:, None] * inv_freq[None, :]).astype(np.float32)
    c["cost"] = np.ascontiguousarray(np.cos(ang).astype(np.float32).reshape(NT, 128, half).transpose(1, 0, 2))
    c["sint"] = np.ascontiguousarray(np.sin(ang).astype(np.float32).reshape(NT, 128, half).transpose(1, 0, 2))
    return c


def build_program(T=2048, NSEQ=2, NL=2, dbg=False, phases="nabco"):
    NT = T // 128
    QB = min(512, T)
    NQB = T // QB
    KPB = QB // 128
    OFF0 = (KPB - 1) * 128
    QA = min(256, T)
    KA = QA // 128
    nc = bass.Bass("TRN2", target_bir_lowering=False)
    cst = _consts(T)

    def din(name, shape, dt=F32):
        return nc.dram_tensor(name, list(shape), dt, kind="ExternalInput").ap()

    x_d = din("x", [NSEQ, T, D])
    w_in_d = din("w_in", [NL, D, IN_COLS])
    w_out_d = din("w_out", [NL, D, D])
    norm_w_d = din("norm_w", [NL, D])
    conv_w_d = din("conv_w", [NL, 4, 1152])
    a_log_d = din("a_log", [NL, 6])
    dt_bias_d = din("dt_bias", [NL, 6])
    gdn_w_d = din("gdn_norm_w", [NL, 64])
    qn_w_d = din("q_norm_w", [NL, 64])
    kn_w_d = din("k_norm_w", [NL, 64])
    cd = {}
    for k, v in cst.items():
        cd[k] = din("c_" + k, v.shape, BF16 if v.dtype == ml_dtypes.bfloat16 else F32)
    out_d = nc.dram_tensor("out", [NSEQ, T, D], F32, kind="ExternalOutput").ap()
    if dbg:
        dbg_d = nc.dram_tensor("dbg_mix", [128, 8, T], F32, kind="ExternalOutput").ap()

    S = Sched(nc)
    eng = S.eng

    def E(e, name, R, W, *a, **k):
        return S.op(e, lambda: getattr(eng[e], name)(*a, **k), R, W)

    def MM(out, lhsT, rhs, R, W, start=True, stop=True, inc=True):
        return S.op("pe", lambda: nc.tensor.matmul(out, lhsT=lhsT, rhs=rhs, start=start, stop=stop), R, W, inc=inc)

    def TR(out, in_, ident, R, W, inc=True):
        return S.op("pe", lambda: nc.tensor.transpose(out, in_, ident), R, W, inc=inc)

    class Tl:
        def __init__(self, name, shape, dt, nres=1):
            self.t = nc.alloc_sbuf_tensor(name, list(shape), dt)
            self.r = [Res("%s_%d" % (name, i)) for i in range(nres)]
            self.R = self.r[0]

    X = Tl("X", [128, NT, D], F32, NT)
    hT = Tl("hT", [128, KC, T], BF16, NQB)
    mixT = Tl("mixT", [128, KC, T], BF16, KC)
    wbuf = [Tl("wbuf%d" % i, [128, KC, 512], BF16, 4) for i in range(2)]
    stage = [Tl("stage%d" % i, [128, KC, 128], F32) for i in range(2)]
    C = {}
    ARENA_CONSTS = ("mtab", "incl2", "nst2", "ident2")
    for k, v in cst.items():
        if k in ARENA_CONSTS:
            continue
        C[k] = Tl("k_" + k, v.shape, BF16 if v.dtype == ml_dtypes.bfloat16 else F32)
    normw = Tl("normw", [128, NL, KC], F32)
    convw = Tl("convw", [128, NL, 9, 4], F32)
    nea = Tl("nea", [128, NL, 6], F32)
    dtb = Tl("dtb", [128, NL, 6], F32)
    gw = Tl("gw", [128, NL, 64], F32)
    wqk = Tl("wqk", [128, NL, 4, 64], F32)
    wba = Tl("wba", [128, KC, 12], BF16)
    wbas = Tl("wbas", [128, KC, 12], F32)
    ps = [nc.alloc_psum_tensor("ps%d" % i, [128, 512], F32) for i in range(8)]
    pr = [Res("ps%d" % i, excl=True) for i in range(8)]

    for k in C:
        S.dma(C[k].t[:], cd[k], writes=[C[k].R])
    for l in range(NL):
        S.dma(normw.t[:, l, :], norm_w_d[l].rearrange("(c p) -> p c", p=128), writes=[normw.R],
              allow_slow_non_contiguous=True)
        for g9 in range(9):
            S.dma(convw.t[:, l, g9, :], conv_w_d[l][:, g9 * 128:(g9 + 1) * 128].rearrange("k p -> p k"), writes=[convw.R],
                  allow_slow_non_contiguous=True)
        S.dma(nea.t[:, l, :], a_log_d[l].partition_broadcast(128), writes=[nea.R])
        S.dma(dtb.t[:, l, :], dt_bias_d[l].partition_broadcast(128), writes=[dtb.R])
        S.dma(gw.t[:, l, :], gdn_w_d[l].partition_broadcast(128), writes=[gw.R])
        for j in range(2):
            S.dma(wqk.t[:, l, j, :], qn_w_d[l].partition_broadcast(128), writes=[wqk.R])
            S.dma(wqk.t[:, l, 2 + j, :], kn_w_d[l].partition_broadcast(128), writes=[wqk.R])
    E("act", "activation", [nea.R], [nea.R], nea.t[:], nea.t[:], AF.Exp)
    E("dve", "tensor_scalar", [nea.R], [nea.R], nea.t[:], nea.t[:], -1.0, None, ALU.mult)
    E("dve", "tensor_scalar", [wqk.R], [wqk.R], wqk.t[:, :, 0:2, :], wqk.t[:, :, 0:2, :], 0.125, None, ALU.mult)

    bank_rr = [0]
    big_reg = nc.gpsimd.to_reg(1e30)

    def nbank(lo=0, hi=8):
        b = lo + bank_rr[0] % (hi - lo)
        bank_rr[0] += 1
        return b

    stg_rr = [0]

    def load_cols(slot, src2d_list):
        for g, src in src2d_list:
            st = stage[stg_rr[0] % 2]
            stg_rr[0] += 1
            S.dma(st.t[:], src.rearrange("(c p) n -> p c n", p=128), writes=[st.R])
            E("pool", "tensor_copy", [st.R], [wbuf[slot].r[g]], wbuf[slot].t[:, :, g * 128:(g + 1) * 128], st.t[:])

    def unit_cols(l, offs):
        return [(g, w_in_d[l][:, o:o + 128]) for g, o in enumerate(offs)]

    def phase_norm(l, ar):
        ss, junk, xn = ar["ss"], ar["junk"], ar["xn"]
        for i in range(NT):
            E("act", "activation", [X.r[i]], [junk.R, ss.R], junk.t[:], X.t[:, i, :], AF.Square,
              accum_out=ss.t[:, i:i + 1])
        E("act", "activation", [ss.R], [ss.R], ss.t[:], ss.t[:], AF.Ln, scale=1.0 / D, bias=EPS)
        E("act", "activation", [ss.R], [ss.R], ss.t[:], ss.t[:], AF.Exp, scale=-0.5)
        for i in range(NT):
            xt = xn[i % 2]
            E("dve", "tensor_scalar", [X.r[i], ss.R], [xt.R], xt.t[:], X.t[:, i, :], ss.t[:, i:i + 1], None, ALU.mult)
            b = nbank(0, 2)
            pb = ps[b][:].bitcast(BF16)
            for c in range(KC):
                TR(pb[:, c * 128:(c + 1) * 128], xt.t[:, c * 128:(c + 1) * 128], C["identb"].t[:],
                   [xt.R, C["identb"].R], [pr[b]], inc=(c == KC - 1))
            E("dve", "tensor_tensor", [pr[b], normw.R], [hT.r[i // KPB]],
              hT.t[:, :, i * 128:(i + 1) * 128], pb.rearrange("p (c n) -> p c n", c=KC),
              normw.t[:, l, :].unsqueeze(2).broadcast_to([128, KC, 128]), ALU.mult)

    def proj_fm(slot, g, tb, b, w=None):
        w = w or QB
        for c in range(KC):
            MM(ps[b][:, 0:w], wbuf[slot].t[:, c, g * 128:(g + 1) * 128], hT.t[:, c, tb * w:(tb + 1) * w],
               [wbuf[slot].r[g], hT.r[(tb * w) // QB]], [pr[b]], start=(c == 0), stop=(c == KC - 1), inc=(c == KC - 1))

    def proj_tm(slot, c0, ncol, i, out_ap, b):
        gs = sorted(set([c0 // 128, (c0 + ncol - 1) // 128]))
        gs = list(range(gs[0], gs[-1] + 1))
        for c in range(KC):
            MM(out_ap, hT.t[:, c, i * 128:(i + 1) * 128], wbuf[slot].t[:, c, c0:c0 + ncol],
               [wbuf[slot].r[g] for g in gs] + [hT.r[i // KPB]], [pr[b]],
               start=(c == 0), stop=(c == KC - 1), inc=(c == KC - 1))

    def silu_from_psum(b, ncol, et, out_ap, out_res, extra_reads=()):
        E("act", "activation", [pr[b]], [et.R], et.t[:, 0:ncol], ps[b][:, 0:ncol], AF.Exp, scale=-1.0)
        E("pool", "tensor_scalar", [et.R], [et.R], et.t[:, 0:ncol], et.t[:, 0:ncol], 1.0, None, ALU.add)
        E("dve", "reciprocal", [et.R], [et.R], et.t[:, 0:ncol], et.t[:, 0:ncol])
        E("dve", "tensor_tensor", [pr[b], et.R] + list(extra_reads), [out_res], out_ap, ps[b][:, 0:ncol],
          et.t[:, 0:ncol], ALU.mult)

    def bc_inproj_common(slot, ar, kind):
        zs, et = ar["zs"], ar["et"]
        for tb in range(NQB):
            b = nbank(0, 2)
            proj_fm(slot, 3, tb, b)
            silu_from_psum(b, QB, et[tb % 2], zs.t[:, tb * QB:(tb + 1) * QB], zs.R)

    def b_unit(l, u, slot, ar):
        qT, kTp, vt, zs = ar["qT"], ar["kTp"], ar["vt"], ar["zs"]
        for tb in range(NQB):
            b = nbank(0, 2)
            proj_fm(slot, 0, tb, b)
            E("act", "activation", [pr[b]], [qT.R], qT.t[:, tb * QB:(tb + 1) * QB], ps[b][:, 0:QB], AF.Copy, scale=0.125)
            b = nbank(0, 2)
            proj_fm(slot, 1, tb, b)
            E("act", "copy", [pr[b]], [kTp.r[0]], kTp.t[0:64, 0, tb * QB:(tb + 1) * QB], ps[b][0:64, 0:QB])
            E("dve", "tensor_copy", [pr[b]], [kTp.r[1]], kTp.t[64:128, 1, tb * QB:(tb + 1) * QB], ps[b][64:128, 0:QB])
        for i in range(NT):
            b = nbank(0, 2)
            proj_tm(slot, 256, 128, i, ps[b][:, 0:128], b)
            E("act", "copy", [pr[b]], [vt.R], vt.t[:, i, :], ps[b][:, 0:128])
        bc_inproj_common(slot, ar, "b")
        sp, Pb, wts, Rb = ar["sp"], ar["P"], ar["wts"], ar["Rb"]
        ob = 7
        zb = [2, 3]
        cb = [4, 5]
        for qb in range(NQB):
            qbase = qb * QB
            nkt = (qb + 1) * KPB
            kts = list(range(nkt - 1, -1, -1))

            def stage_a(n, h):
                kt = kts[n]
                kbase = kt * 128
                diag = kbase + 127 >= qbase
                spb, Pt = sp[h][n % 2], Pb[h][n % 2]
                MM(ps[zb[h]][:, 0:QB], kTp.t[:, h, kbase:kbase + 128], qT.t[:, qbase:qbase + QB],
                   [kTp.r[h], qT.R], [pr[zb[h]]])
                E("act", "activation", [pr[zb[h]]], [spb.R], spb.t[:, 0:QB], ps[zb[h]][:, 0:QB], AF.Exp, scale=-1.0)
                E("act", "activation", [spb.R], [spb.R], spb.t[:, 0:QB], spb.t[:, 0:QB], AF.Ln, bias=1.0)
                E("dve", "tensor_tensor", [pr[zb[h]], spb.R], [Pt.R], Pt.t[:, 0:QB], ps[zb[h]][:, 0:QB],
                  spb.t[:, 0:QB], ALU.add)
                if diag:
                    E("pool", "affine_select", [Pt.R], [Pt.R], out=Pt.t[:, 0:QB], in_=Pt.t[:, 0:QB],
                      pattern=[[1, QB]], compare_op=ALU.is_gt, fill=0.0, base=qbase - kbase, channel_multiplier=-1)

            def stage_b(n, h):
                kt = kts[n]
                kbase = kt * 128
                diag = kbase + 127 >= qbase
                first = (n == 0)
                last = (n == len(kts) - 1)
                spb, Pt, wt = sp[h][n % 2], Pb[h][n % 2], wts[h][n % 2]
                MM(ps[cb[h]][:, 0:QB], C["tri_gt"].t[:], Pt.t[:, 0:QB], [C["tri_gt"].R, Pt.R], [pr[cb[h]]],
                   start=True, stop=first, inc=first)
                if not first:
                    MM(ps[cb[h]][:, 0:QB], C["onesf"].t[:], Rb[h].t[:, 0:QB], [C["onesf"].R, Rb[h].R], [pr[cb[h]]],
                       start=False, stop=True)
                E("dve", "tensor_tensor", [pr[cb[h]], spb.R], [spb.R], spb.t[:, 0:QB], ps[cb[h]][:, 0:QB],
                  spb.t[:, 0:QB], ALU.add)
                if diag:
                    E("pool", "affine_select", [spb.R], [spb.R], out=spb.t[:, 0:QB], in_=spb.t[:, 0:QB],
                      pattern=[[1, QB]], compare_op=ALU.is_gt, fill=big_reg, base=qbase - kbase, channel_multiplier=-1)
                E("act", "activation", [spb.R], [wt.R], wt.t[:, 0:QB], spb.t[:, 0:QB], AF.Exp, scale=-1.0)
                if not last:
                    if first:
                        E("pool", "tensor_copy", [Pt.R], [Rb[h].R], Rb[h].t[:, 0:QB], Pt.t[:, 0:QB])
                    else:
                        E("pool", "tensor_tensor", [Pt.R, Rb[h].R], [Rb[h].R], out=Rb[h].t[:, 0:QB], in0=Rb[h].t[:, 0:QB],
                          in1=Pt.t[:, 0:QB], op=ALU.add)
                MM(ps[ob][64 * h:64 * h + 64, 0:QB], vt.t[:, kt, 64 * h:64 * h + 64], wt.t[:, 0:QB],
                   [vt.R, wt.R], [pr[ob]], start=first, stop=last)

            for n in range(len(kts) + 1):
                for h in range(2):
                    if n < len(kts):
                        stage_a(n, h)
                for h in range(2):
                    if n >= 1:
                        stage_b(n - 1, h)
            E("dve", "tensor_tensor", [pr[ob], zs.R], [mixT.r[3 + u]], mixT.t[:, 3 + u, qbase:qbase + QB],
              ps[ob][:, 0:QB], zs.t[:, qbase:qbase + QB], ALU.mult)

    def c_unit(l, u, slot, ar):
        import os as _os
        CSTOP = int(_os.environ.get("CSTOP", "99"))
        qT, kTp, vt, zs = ar["qT"], ar["kTp"], ar["vt"], ar["zs"]
        qk, sq, s4, qkb = ar["qk"], ar["sq"], ar["s4"], ar["qkb"]
        rt = ar["rt"]
        for i in range(NT):
            b = nbank(0, 2)
            proj_tm(slot, 0, 384, i, ps[b][:, 0:384], b)
            E("act", "copy", [pr[b]], [vt.R], vt.t[:, i, :], ps[b][:, 256:384])
            q4 = qk[i % 2]
            E("act", "copy", [pr[b]], [q4.R], q4.t[:], ps[b][:, 0:256].rearrange("p (a d) -> p a d", a=4))
            if CSTOP <= 1:
                continue
            E("pool", "tensor_tensor", [q4.R], [sq.R], out=sq.t[:], in0=q4.t[:], in1=q4.t[:], op=ALU.mult)
            E("dve", "tensor_reduce", [sq.R], [s4.R], out=s4.t[:], in_=sq.t[:], axis=AX.X, op=ALU.add)
            E("act", "activation", [s4.R], [s4.R], s4.t[:], s4.t[:], AF.Ln, scale=1.0 / 64, bias=EPS)
            E("act", "activation", [s4.R], [s4.R], s4.t[:], s4.t[:], AF.Exp, scale=-0.5)
            E("dve", "tensor_tensor", [q4.R, s4.R], [q4.R], q4.t[:], q4.t[:],
              s4.t[:].unsqueeze(2).broadcast_to([128, 4, 64]), ALU.mult)
            E("pool", "tensor_tensor", [q4.R, wqk.R], [q4.R], out=q4.t[:], in0=q4.t[:], in1=wqk.t[:, l, :, :], op=ALU.mult)
            if CSTOP <= 2:
                continue
            qb_ = qkb[i % 2]
            E("act", "copy", [q4.R], [qb_.R], qb_.t[:], q4.t[:])
            cs = C["cost"].t[:, i, :].unsqueeze(1).broadcast_to([128, 4, 8])
            sn = C["sint"].t[:, i, :].unsqueeze(1).broadcast_to([128, 4, 8])
            x1 = q4.t[:, :, 0:8]
            x2 = q4.t[:, :, 8:16]
            E("dve", "tensor_tensor", [q4.R, C["cost"].R], [rt.R], rt.t[:, 0, :, :], x1, cs, ALU.mult)
            E("dve", "tensor_tensor", [q4.R, C["sint"].R], [rt.R], rt.t[:, 1, :, :], x2, sn, ALU.mult)
            E("dve", "tensor_tensor", [q4.R, C["cost"].R], [rt.R], rt.t[:, 2, :, :], x2, cs, ALU.mult)
            E("dve", "tensor_tensor", [q4.R, C["sint"].R], [rt.R], rt.t[:, 3, :, :], x1, sn, ALU.mult)
            E("dve", "tensor_tensor", [rt.R], [qb_.R], qb_.t[:, :, 0:8], rt.t[:, 0, :, :], rt.t[:, 1, :, :], ALU.subtract)
            E("dve", "tensor_tensor", [rt.R], [qb_.R], qb_.t[:, :, 8:16], rt.t[:, 2, :, :], rt.t[:, 3, :, :], ALU.add)
            if CSTOP <= 3:
                continue
            b2 = nbank(0, 2)
            pb = ps[b2][:].bitcast(BF16)
            qflat = qb_.t[:].rearrange("p a d -> p (a d)")
            TR(pb[:, 0:128], qflat[:, 0:128], C["identb"].t[:], [qb_.R, C["identb"].R], [pr[b2]], inc=False)
            TR(pb[:, 128:256], qflat[:, 128:256], C["identb"].t[:], [qb_.R, C["identb"].R], [pr[b2]])
            tq = ar["tq"][i % 2]
            E("dve", "tensor_copy", [pr[b2]], [tq.R], tq.t[:], pb[:, 0:256])
            E("act", "copy", [tq.R], [qT.R], qT.t[:, i * 128:(i + 1) * 128], tq.t[:, 0:128])
            E("pool", "tensor_copy", [tq.R], [kTp.r[0]], kTp.t[0:64, 0, i * 128:(i + 1) * 128], tq.t[0:64, 128:256])
            E("pool", "tensor_copy", [tq.R], [kTp.r[1]], kTp.t[64:128, 1, i * 128:(i + 1) * 128], tq.t[64:128, 128:256])
        if CSTOP <= 4:
            return
        bc_inproj_common(slot, ar, "c")
        if CSTOP <= 5:
            return
        Eb, EMb, rec = ar["E"], ar["EM"], ar["rec"]
        nb_, db_ = 6, 7
        sb = [2, 3, 4, 5]
        NE = len(Eb)
        for qb in range(NQB):
            qbase = qb * QB
            nkt = (qb + 1) * KPB
            pairs = [(kt, h) for kt in range(nkt - 1, -1, -1) for h in range(2)]

            def stage_a(n):
                kt, h = pairs[n]
                kbase = kt * 128
                off = qbase - kbase + OFF0
                b = sb[n % 4]
                Et, EMt = Eb[n % NE], EMb[n % NE]
                MM(ps[b][:, 0:QB], kTp.t[:, h, kbase:kbase + 128], qT.t[:, qbase:qbase + QB],
                   [kTp.r[h], qT.R], [pr[b]])
                E("act", "activation", [pr[b]], [Et.R], Et.t[:, 0:QB], ps[b][:, 0:QB], AF.Exp)
                E("pool", "tensor_tensor", [Et.R, ar["mtab"].R], [EMt.R], out=EMt.t[:, 0:QB], in0=Et.t[:, 0:QB],
                  in1=ar["mtab"].t[:, off:off + QB], op=ALU.mult)

            def stage_b(n):
                kt, h = pairs[n]
                first = (kt == nkt - 1)
                last = (kt == 0)
                EMt = EMb[n % NE]
                MM(ps[nb_][64 * h:64 * h + 64, 0:QB], vt.t[:, kt, 64 * h:64 * h + 64], EMt.t[:, 0:QB],
                   [vt.R, EMt.R], [pr[nb_]], start=first, stop=last, inc=False)
                MM(ps[db_][64 * h:64 * h + 64, 0:QB], C["onesb"].t[:, 0:64], EMt.t[:, 0:QB],
                   [C["onesb"].R, EMt.R], [pr[db_]], start=first, stop=last)

            SK = 2
            for n in range(len(pairs) + SK):
                if n < len(pairs):
                    stage_a(n)
                if n >= SK:
                    stage_b(n - SK)
            E("dve", "reciprocal", [pr[db_]], [rec.R], rec.t[:, 0:QB], ps[db_][:, 0:QB])
            E("dve", "tensor_tensor", [pr[nb_], rec.R], [rec.R], rec.t[:, 0:QB], ps[nb_][:, 0:QB], rec.t[:, 0:QB], ALU.mult)
            E("pool", "tensor_tensor", [rec.R, zs.R], [mixT.r[5 + u]], out=mixT.t[:, 5 + u, qbase:qbase + QB],
              in0=rec.t[:, 0:QB], in1=zs.t[:, qbase:qbase + QB], op=ALU.mult)

    def a_prep(l, ar):
        beta, gg, t12 = ar["beta"], ar["g"], ar["t12"]
        S.dma(wbas.t[:], w_in_d[l][:, OFF_BETA:OFF_BETA + 12].rearrange("(c p) n -> p c n", p=128), writes=[wbas.R])
        E("pool", "tensor_copy", [wbas.R], [wba.R], wba.t[:], wbas.t[:])
        b = nbank(0, 2)
        for i in range(NT):
            for c in range(KC):
                MM(ps[b][:, i * 12:(i + 1) * 12], hT.t[:, c, i * 128:(i + 1) * 128], wba.t[:, c, :],
                   [wba.R, hT.r[i // KPB]], [pr[b]], start=(c == 0), stop=(c == KC - 1),
                   inc=(c == KC - 1 and i == NT - 1))
        pv = ps[b][:, 0:NT * 12].rearrange("p (n k) -> p n k", k=12)
        E("act", "activation", [pr[b]], [beta.R], beta.t[:], pv[:, :, 0:6], AF.Exp, scale=-1.0)
        E("pool", "tensor_scalar", [beta.R], [beta.R], beta.t[:], beta.t[:], 1.0, None, ALU.add)
        E("dve", "reciprocal", [beta.R], [beta.R], beta.t[:], beta.t[:])
        E("dve", "tensor_tensor", [pr[b], dtb.R], [t12.R], t12.t[:], pv[:, :, 6:12],
          dtb.t[:, l, :].unsqueeze(1).broadcast_to([128, NT, 6]), ALU.add)
        E("act", "activation", [t12.R], [t12.R], t12.t[:], t12.t[:], AF.Exp)
        E("act", "activation", [t12.R], [t12.R], t12.t[:], t12.t[:], AF.Ln, bias=1.0)
        E("dve", "tensor_tensor", [t12.R, nea.R], [gg.R], gg.t[:], t12.t[:],
          nea.t[:, l, :].unsqueeze(1).broadcast_to([128, NT, 6]), ALU.mult)

    def a_unit(l, u, slot, ar):
        beta, gg = ar["beta"], ar["g"]
        xb, acc, et, yq, qkvT = ar["xb"], ar["acc"], ar["et"], ar["yq"], ar["qkvT"]
        gz = ar["gz"]
        Ssb = ar["S"]
        identf, tri_le, onesf = C["identf"], C["tri_le"], C["onesf"]
        E("pool", "memset", [], [Ssb.R], Ssb.t[:], 0.0)
        for g in range(3):
            E("pool", "memset", [], [xb[g].R], xb[g].t[:, 0:3], 0.0)
        for tb in range(T // QA):
            for g in range(3):
                b = nbank(0, 2)
                proj_fm(slot, g, tb, b, QA)
                gi = g * 3 + u
                xg = xb[g]
                E("act", "copy", [pr[b]], [xg.R], xg.t[:, 3:3 + QA], ps[b][:, 0:QA])
                a_ = acc[g % 2]
                E("dve", "tensor_scalar", [xg.R, convw.R], [a_.R], a_.t[:, 0:QA], xg.t[:, 0:QA],
                  convw.t[:, l, gi, 0:1], None, ALU.mult)
                for k in range(1, 4):
                    E("dve", "scalar_tensor_tensor", [xg.R, convw.R, a_.R], [a_.R], a_.t[:, 0:QA], xg.t[:, k:k + QA],
                      convw.t[:, l, gi, k:k + 1], a_.t[:, 0:QA], ALU.mult, ALU.add)
                E("pool", "tensor_copy", [xg.R], [xg.R], xg.t[:, 0:3], xg.t[:, QA:QA + 3])
                e_ = et[g % 2]
                E("act", "activation", [a_.R], [e_.R], e_.t[:, 0:QA], a_.t[:, 0:QA], AF.Exp, scale=-1.0)
                E("pool", "tensor_scalar", [e_.R], [e_.R], e_.t[:, 0:QA], e_.t[:, 0:QA], 1.0, None, ALU.add)
                E("dve", "reciprocal", [e_.R], [e_.R], e_.t[:, 0:QA], e_.t[:, 0:QA])
                if g == 2:
                    E("dve", "tensor_tensor", [a_.R, e_.R], [qkvT[2].R], qkvT[2].t[:, 0:QA], a_.t[:, 0:QA], e_.t[:, 0:QA], ALU.mult)
                else:
                    y = yq[g]
                    E("dve", "tensor_tensor", [a_.R, e_.R], [y.R], y.t[:, 0:QA], a_.t[:, 0:QA], e_.t[:, 0:QA], ALU.mult)
                    E("pool", "tensor_tensor", [y.R], [e_.R], out=e_.t[:, 0:QA], in0=y.t[:, 0:QA], in1=y.t[:, 0:QA], op=ALU.mult)
                    b2 = nbank(0, 2)
                    MM(ps[b2][:, 0:QA], C["blockones"].t[:], e_.t[:, 0:QA], [C["blockones"].R, e_.R], [pr[b2]])
                    E("act", "activation", [pr[b2]], [e_.R], e_.t[:, 0:QA], ps[b2][:, 0:QA], AF.Ln, bias=EPS)
                    E("act", "activation", [e_.R], [e_.R], e_.t[:, 0:QA], e_.t[:, 0:QA], AF.Exp, scale=-0.5)
                    E("dve", "scalar_tensor_tensor", [y.R, e_.R], [qkvT[g].R], qkvT[g].t[:, 0:QA], y.t[:, 0:QA],
                      0.125 if g == 0 else 1.0, e_.t[:, 0:QA], ALU.mult, ALU.mult)
            b = nbank(0, 2)
            for j in range(KA):
                i = tb * KA + j
                proj_tm(slot, 384, 128, i, ps[b][:, j * 128:(j + 1) * 128], b)
            silu_from_psum(b, QA, et[0], gz.t[:].rearrange("p a d -> p (a d)"), gz.R)
            E("pool", "tensor_tensor", [gz.R, gw.R], [gz.R], out=gz.t[:],
              in0=gz.t[:],
              in1=gw.t[:, l, :].unsqueeze(1).broadcast_to([128, KA * 2, 64]), op=ALU.mult)
            for j in range(KA):
                a_tile(l, u, tb * KA + j, j, ar)

    def a_tile(l, u, i, j, ar):
        import os as _os
        ASTOP = int(_os.environ.get("ASTOP", "99"))
        if ASTOP <= 1:
            return
        beta, gg = ar["beta"], ar["g"]
        qkvT, gz, Ssb = ar["qkvT"], ar["gz"], ar["S"]
        identf, tri_le, onesf = C["identf"], C["tri_le"], C["onesf"]
        k = i % 2
        kv, gcs, egc, sbeg, gb, d0, DT, DTs, DTi = (ar[n][k] for n in ("kv", "gcs", "egc", "sbeg", "gb", "d0", "DT", "DTs", "DTi"))
        egr, kTp, M0p, qkT, MN, Pm, bv, usb, wTp, qgTp, vnew = (ar[n][k] for n in ("egr", "kTp", "M0p", "qkT", "MN", "Pm", "bv", "usb", "wTp", "qgTp", "vnew"))
        osb, osq, oss, ot, obf = (ar[n][k] for n in ("osb", "osq", "oss", "ot", "obf"))
        cs = slice(j * 128, (j + 1) * 128)
        bi = beta.t[:, i, 2 * u:2 * u + 2]
        gi = gg.t[:, i, 2 * u:2 * u + 2]
        qT_, kT_, vT_ = qkvT[0], qkvT[1], qkvT[2]
        b = 2
        TR(ps[b][:, 0:128], kT_.t[:, cs], identf.t[:], [kT_.R, identf.R], [pr[b]], inc=False)
        TR(ps[b][:, 128:256], vT_.t[:, cs], identf.t[:], [vT_.R, identf.R], [pr[b]], inc=False)
        MM(ps[b][:, 256:258], tri_le.t[:], gi, [tri_le.R, gg.R], [pr[b]])
        E("act", "copy", [pr[b]], [kv.R], kv.t[:], ps[b][:, 0:256].rearrange("p (a h d) -> p a h d", a=2, h=2))
        E("dve", "tensor_copy", [pr[b]], [gcs.R], gcs.t[:], ps[b][:, 256:258])
        E("act", "activation", [gcs.R], [egc.R], egc.t[:], gcs.t[:], AF.Exp)
        E("pool", "tensor_tensor", [egc.R, beta.R], [sbeg.R], out=sbeg.t[:], in0=egc.t[:], in1=bi, op=ALU.mult)
        if ASTOP <= 2:
            return
        for h in range(2):
            E("pool", "tensor_scalar", [onesf.R, gg.R], [gb.R], gb.t[:, h, :], onesf.t[:], gi[:, h:h + 1], None, ALU.mult)
            E("pool", "tensor_scalar", [onesf.R, gg.R], [gb.R], gb.t[:, 2, 64 * h:64 * h + 64], onesf.t[:, 0:64], gi[:, h:h + 1], None, ALU.mult)
        b = 3
        for h in range(3):
            MM(ps[b][:, h * 128:(h + 1) * 128], gb.t[:, h, :], tri_le.t[:], [gb.R, tri_le.R], [pr[b]], inc=(h == 2))
        for h in range(2):
            E("dve", "tensor_scalar", [pr[b], gcs.R], [d0.R], d0.t[:, h, :], ps[b][:, h * 128:(h + 1) * 128],
              gcs.t[:, h:h + 1], 0.0, ALU.subtract, ALU.min)
        E("act", "activation", [d0.R], [DT.R], DT.t[:], d0.t[:], AF.Exp)
        E("act", "activation", [pr[b]], [egr.R], egr.t[:], ps[b][:, 256:384], AF.Exp)
        E("pool", "tensor_tensor", [DT.R, ar["nst2"].R], [DTs.R], out=DTs.t[:], in0=DT.t[:], in1=ar["nst2"].t[:], op=ALU.mult)
        E("pool", "tensor_tensor", [DT.R, ar["incl2"].R], [DTi.R], out=DTi.t[:], in0=DT.t[:], in1=ar["incl2"].t[:], op=ALU.mult)
        if ASTOP <= 3:
            return
        E("pool", "tensor_copy", [kT_.R], [kTp.R], kTp.t[0:64, 0, :], kT_.t[0:64, cs])
        E("pool", "tensor_copy", [kT_.R], [kTp.R], kTp.t[64:128, 1, :], kT_.t[64:128, cs])
        A4 = int(_os.environ.get("A4", "99"))
        if A4 <= 1:
            return
        b = 4
        AV = _os.environ.get("AV", "kq")
        for h in range(2):
            if "x" in AV:
                MM(ps[b][:, h * 128:(h + 1) * 128], kT_.t[:, cs], kT_.t[:, cs], [kTp.R, kT_.R], [pr[b]], inc=(h == 1))
            if "z" in AV:
                MM(ps[b][:, h * 128:(h + 1) * 128], gb.t[:, h, :], C["tri_le"].t[:], [gb.R, tri_le.R], [pr[b]], inc=(h == 1))
            if "y" in AV:
                MM(ps[b][:, h * 128:(h + 1) * 128], kTp.t[:, h, :], C["tri_le"].t[:], [kTp.R, kT_.R], [pr[b]], inc=(h == 1))
            if "k" in AV:
                MM(ps[b][:, h * 128:(h + 1) * 128], kTp.t[:, h, :], kT_.t[:, cs], [kTp.R, kT_.R], [pr[b]], inc=("q" not in AV and h == 1))
        for h in range(2):
            if "q" in AV:
                MM(ps[b][:, 256 + h * 128:256 + (h + 1) * 128], kTp.t[:, h, :], qT_.t[:, cs], [kTp.R, qT_.R], [pr[b]], inc=(h == 1))
        if A4 <= 2:
            return
        E("dve", "tensor_tensor", [pr[b], DTs.R], [M0p.R], M0p.t[:], ps[b][:, 0:256].rearrange("p (h n) -> p h n", h=2), DTs.t[:], ALU.mult)
        E("dve", "tensor_tensor", [pr[b], DTi.R], [qkT.R], qkT.t[:], ps[b][:, 256:512].rearrange("p (h n) -> p h n", h=2), DTi.t[:], ALU.mult)
        if ASTOP <= 4:
            return
        b = 5
        for h in range(2):
            TR(ps[b][:, h * 128:(h + 1) * 128], M0p.t[:, h, :], identf.t[:], [M0p.R, identf.R], [pr[b]], inc=(h == 1))
        mn = MN[0]
        for h in range(2):
            E("dve", "tensor_scalar", [pr[b], beta.R], [mn.R], mn.t[:, 2 + h, :], ps[b][:, h * 128:(h + 1) * 128], bi[:, h:h + 1], None, ALU.mult)
        b = 5
        for h in range(2):
            TR(ps[b][:, h * 128:(h + 1) * 128], mn.t[:, 2 + h, :], identf.t[:], [mn.R, identf.R], [pr[b]], inc=(h == 1))
        E("act", "copy", [pr[b]], [mn.R], mn.t[:, 0:2, :], ps[b][:, 0:256].rearrange("p (h n) -> p h n", h=2))
        E("dve", "tensor_tensor", [pr[b], ar["ident2"].R], [Pm.R], Pm.t[:], ps[b][:, 0:256].rearrange("p (h n) -> p h n", h=2), ar["ident2"].t[:], ALU.add)
        if ASTOP <= 5:
            return
        for lv in range(1, 7):
            mo = MN[(lv - 1) % 2]
            mw = MN[lv % 2]
            b = 6
            if lv < 6:
                for h in range(2):
                    MM(ps[b][:, h * 128:(h + 1) * 128], mo.t[:, 2 + h, :], mo.t[:, h, :], [mo.R], [pr[b]], inc=False)
            for h in range(2):
                MM(ps[b][:, 256 + h * 128:256 + (h + 1) * 128], mo.t[:, h, :], mo.t[:, 2 + h, :], [mo.R], [pr[b]], inc=(h == 1))
            if lv < 6:
                E("act", "copy", [pr[b]], [mw.R], mw.t[:, 0:2, :], ps[b][:, 0:256].rearrange("p (h n) -> p h n", h=2))
            E("dve", "tensor_copy", [pr[b]], [mw.R], mw.t[:, 2:4, :], ps[b][:, 256:512].rearrange("p (h n) -> p h n", h=2))
            b = 7
            for h in range(2):
                MM(ps[b][:, h * 128:(h + 1) * 128], mw.t[:, 2 + h, :], Pm.t[:, h, :], [mw.R, Pm.R], [pr[b]], inc=(h == 1))
            E("dve", "tensor_tensor", [pr[b], Pm.R], [Pm.R], Pm.t[:], Pm.t[:], ps[b][:, 0:256].rearrange("p (h n) -> p h n", h=2), ALU.add)
        if ASTOP <= 6:
            return
        E("pool", "tensor_tensor", [kv.R, beta.R], [bv.R], out=bv.t[:, 0, :, :], in0=kv.t[:, 1, :, :],
          in1=bi.unsqueeze(2).broadcast_to([128, 2, 64]), op=ALU.mult)
        E("pool", "tensor_tensor", [kv.R, sbeg.R], [bv.R], out=bv.t[:, 1, :, :], in0=kv.t[:, 0, :, :],
          in1=sbeg.t[:].unsqueeze(2).broadcast_to([128, 2, 64]), op=ALU.mult)
        E("pool", "tensor_tensor", [kv.R, DT.R], [bv.R], out=bv.t[:, 2, :, :], in0=kv.t[:, 0, :, :],
          in1=DT.t[:, :, 127:128].broadcast_to([128, 2, 64]), op=ALU.mult)
        b = 3
        for h in range(2):
            MM(ps[b][:, h * 64:(h + 1) * 64], Pm.t[:, h, :], bv.t[:, 0, h, :], [Pm.R, bv.R], [pr[b]], inc=False)
        for h in range(2):
            MM(ps[b][64 * h:64 * h + 64, 128:256], bv.t[:, 1, h, :], Pm.t[:, h, :], [Pm.R, bv.R], [pr[b]], inc=(h == 1))
        E("act", "copy", [pr[b]], [usb.R], usb.t[:], ps[b][:, 0:128].rearrange("p (h d) -> p h d", h=2))
        E("dve", "tensor_copy", [pr[b]], [wTp.R], wTp.t[0:64, 0, :], ps[b][0:64, 128:256])
        E("dve", "tensor_copy", [pr[b]], [wTp.R], wTp.t[64:128, 1, :], ps[b][64:128, 128:256])
        E("pool", "tensor_tensor", [qT_.R, egr.R], [qgTp.R], out=qgTp.t[0:64, 0, :], in0=qT_.t[0:64, cs], in1=egr.t[0:64, :], op=ALU.mult)
        E("pool", "tensor_tensor", [qT_.R, egr.R], [qgTp.R], out=qgTp.t[64:128, 1, :], in0=qT_.t[64:128, cs], in1=egr.t[64:128, :], op=ALU.mult)
        if ASTOP <= 7:
            return
        b = 4
        for h in range(2):
            MM(ps[b][:, h * 64:(h + 1) * 64], wTp.t[:, h, :], Ssb.t[:], [wTp.R, Ssb.R], [pr[b]], inc=(h == 1))
        E("dve", "tensor_tensor", [usb.R, pr[b]], [vnew.R], vnew.t[:], usb.t[:], ps[b][:, 0:128].rearrange("p (h d) -> p h d", h=2), ALU.subtract)
        bo = 6
        for h in range(2):
            MM(ps[bo][:, h * 64:(h + 1) * 64], qgTp.t[:, h, :], Ssb.t[:], [qgTp.R, Ssb.R], [pr[bo]], start=True, stop=False, inc=False)
            MM(ps[bo][:, h * 64:(h + 1) * 64], qkT.t[:, h, :], vnew.t[:, h, :], [qkT.R, vnew.R], [pr[bo]], start=False, stop=True, inc=(h == 1))
        b = 7
        for h in range(2):
            MM(ps[b][64 * h:64 * h + 64, 0:64], bv.t[:, 2, h, :], vnew.t[:, h, :], [bv.R, vnew.R], [pr[b]], inc=(h == 1))
        E("dve", "scalar_tensor_tensor", [Ssb.R, egr.R, pr[b]], [Ssb.R], Ssb.t[:], Ssb.t[:], egr.t[:, 127:128], ps[b][:, 0:64], ALU.mult, ALU.add)
        if ASTOP <= 8:
            return
        E("act", "copy", [pr[bo]], [osb.R], osb.t[:], ps[bo][:, 0:128].rearrange("p (h d) -> p h d", h=2))
        E("pool", "tensor_tensor", [osb.R], [osq.R], out=osq.t[:], in0=osb.t[:], in1=osb.t[:], op=ALU.mult)
        E("dve", "tensor_reduce", [osq.R], [oss.R], out=oss.t[:], in_=osq.t[:], axis=AX.X, op=ALU.add)
        E("act", "activation", [oss.R], [oss.R], oss.t[:], oss.t[:], AF.Ln, scale=1.0 / 64, bias=EPS)
        E("act", "activation", [oss.R], [oss.R], oss.t[:], oss.t[:], AF.Exp, scale=-0.5)
        E("dve", "tensor_tensor", [osb.R, oss.R], [ot.R], ot.t[:], osb.t[:], oss.t[:].unsqueeze(2).broadcast_to([128, 2, 64]), ALU.mult)
        E("dve", "tensor_tensor", [ot.R, gz.R], [obf.R], obf.t[:], ot.t[:], gz.t[:, 2 * j:2 * j + 2, :], ALU.mult)
        b = nbank(0, 2)
        pb = ps[b][:].bitcast(BF16)
        TR(pb[:, 0:128], obf.t[:].rearrange("p h d -> p (h d)"), C["identb"].t[:], [obf.R, C["identb"].R], [pr[b]])
        E("dve", "tensor_copy", [pr[b]], [mixT.r[u]], mixT.t[:, u, i * 128:(i + 1) * 128], pb[:, 0:128])

    def phase_out(l):
        for hf in range(2):
            for i in range(NT):
                b = nbank(0, 4)
                for c in range(KC):
                    MM(ps[b][:, 0:512], mixT.t[:, c, i * 128:(i + 1) * 128], wbuf[hf].t[:, c, :],
                       [mixT.r[c]] + wbuf[hf].r, [pr[b]], start=(c == 0), stop=(c == KC - 1), inc=(c == KC - 1))
                E("dve", "tensor_tensor", [pr[b], X.r[i]], [X.r[i]], X.t[:, i, hf * 512:(hf + 1) * 512],
                  X.t[:, i, hf * 512:(hf + 1) * 512], ps[b][:, 0:512], ALU.add)

    import contextlib

    class Arena:
        def __init__(self):
            self.stack = contextlib.ExitStack()

        def tl(self, name, shape, dt):
            t = Tl.__new__(Tl)
            t.t = self.stack.enter_context(nc.sbuf_tensor(name, list(shape), dt))
            t.r = [Res(name)]
            t.R = t.r[0]
            return t

        def const(self, k):
            v = cst[k]
            t = self.tl(nm("ac_" + k), v.shape, BF16 if v.dtype == ml_dtypes.bfloat16 else F32)
            S.dma(t.t[:], cd[k], writes=[t.R])
            return t

        def close(self):
            S.barrier()
            for k_ in ("pe", "act", "dve", "pool"):
                S._wait("sp", k_, S.cnt[k_])
            self.stack.close()

    uid = [0]

    def nm(s):
        uid[0] += 1
        return "%s_%d" % (s, uid[0])

    for s in range(NSEQ):
        for i0 in range(0, NT, 4):
            n = min(4, NT - i0)
            S.dma(X.t[:, i0:i0 + n, :], x_d[s, i0 * 128:(i0 + n) * 128, :].rearrange("(n p) d -> p n d", p=128),
                  writes=[X.r[i] for i in range(i0, i0 + n)])
        for l in range(NL):
            units = [("a", u, [OFF_QA + 128 * u, OFF_KA + 128 * u, OFF_VA + 128 * u, OFF_ZA + 128 * u]) for u in range(3)]
            units += [("b", u, [OFF_QB + 128 * u, OFF_KB + 128 * u, OFF_VB + 128 * u, OFF_ZB + 128 * u]) for u in range(2)]
            units += [("c", u, [OFF_QC + 128 * u, OFF_KC + 128 * u, OFF_VC + 128 * u, OFF_ZC + 128 * u]) for u in range(3)]
            load_cols(0, unit_cols(l, units[0][2]))
            A = Arena()
            ar = {"ss": A.tl(nm("ss"), [128, NT], F32), "junk": A.tl(nm("junk"), [128, D], BF16),
                  "xn": [A.tl(nm("xn"), [128, D], BF16) for _ in range(2)]}
            phase_norm(l, ar)
            A.close()
            A = Arena()
            ar = {"beta": A.tl(nm("beta"), [128, NT, 6], F32), "g": A.tl(nm("g"), [128, NT, 6], F32),
                  "t12": A.tl(nm("t12"), [128, NT, 6], F32),
                  "xb": [A.tl(nm("xb"), [128, QA + 3], F32) for _ in range(3)],
                  "qkvT": [A.tl(nm("qkvT"), [128, QA], F32) for _ in range(3)],
                  "gz": A.tl(nm("gz"), [128, 2 * KA, 64], F32),
                  "S": A.tl(nm("S"), [128, 64], F32)}
            _acc = A.tl(nm("acc"), [128, QA], F32)
            ar["acc"] = [_acc, _acc]
            ar["et"] = [A.tl(nm("et"), [128, QA], F32) for _ in range(2)]
            _yq = A.tl(nm("yq"), [128, QA], F32)
            ar["yq"] = [_yq, _yq]

            def two(n_, shp, dt=F32):
                t_ = A.tl(nm(n_), shp, dt)
                return [t_, t_]
            ar.update({"kv": two("kv", [128, 2, 2, 64]), "gcs": two("gcs", [128, 2]), "egc": two("egc", [128, 2]),
                       "sbeg": two("sbeg", [128, 2]), "gb": two("gb", [128, 3, 128]), "d0": two("d0", [128, 2, 128]),
                       "DTs": two("DTs", [128, 2, 128]), "DTi": two("DTi", [128, 2, 128]),
                       "egr": two("egr", [128, 128]), "kTp": two("kTpA", [128, 2, 128]), "M0p": two("M0p", [128, 2, 128]),
                       "qkT": two("qkT", [128, 2, 128]),
                       "Pm": two("Pm", [128, 2, 128]), "bv": two("bv", [128, 3, 2, 64]), "usb": two("usb", [128, 2, 64]),
                       "wTp": two("wTp", [128, 2, 128]), "qgTp": two("qgTp", [128, 2, 128]), "vnew": two("vnew", [128, 2, 64]),
                       "osb": two("osb", [128, 2, 64]), "osq": two("osq", [128, 2, 64]), "oss": two("oss", [128, 2]),
                       "ot": two("ot", [128, 2, 64]), "obf": two("obf", [128, 2, 64], BF16)})
            ar["DT"] = ar["d0"]
            for k_ in ("incl2", "nst2", "ident2"):
                ar[k_] = A.const(k_)
            _mn = [A.tl(nm("MN"), [128, 4, 128], F32) for _ in range(2)]
            ar["MN"] = [_mn, _mn]
            for n_ in ("kTp", "wTp", "qgTp"):
                E("pool", "memset", [], [ar[n_][0].R], ar[n_][0].t[:], 0.0)
            if "a" in phases:
                a_prep(l, ar)
            ui = 0
            for (kind, u, offs) in units[0:3]:
                load_cols((ui + 1) % 2, unit_cols(l, units[ui + 1][2]))
                if "a" in phases:
                    a_unit(l, u, ui % 2, ar)
                ui += 1
            A.close()
            for kind_, ulist in (("b", units[3:5]), ("c", units[5:8])):
                A = Arena()
                ar = {"qT": A.tl(nm("qT"), [128, T], BF16), "kTp": A.tl(nm("kTp"), [128, 2, T], BF16),
                      "vt": A.tl(nm("vt"), [128, NT, 128], BF16), "zs": A.tl(nm("zs"), [128, T], BF16)}
                _et = A.tl(nm("etb"), [128, QB], F32)
                ar["et"] = [_et, _et]
                if kind_ == "b":
                    ar["sp"] = [[A.tl(nm("sp"), [128, QB], F32) for _ in range(2)] for _ in range(2)]
                    ar["P"] = [[A.tl(nm("P"), [128, QB], F32) for _ in range(2)] for _ in range(2)]
                    _w = [A.tl(nm("wts"), [128, QB], BF16) for _ in range(2)]
                    ar["wts"] = [[_w[0], _w[0]], [_w[1], _w[1]]]
                    ar["Rb"] = [A.tl(nm("Rb"), [128, QB], F32) for _ in range(2)]
                else:
                    ar.update({"E": [A.tl(nm("E"), [128, QB], BF16) for _ in range(3)],
                               "EM": [A.tl(nm("EM"), [128, QB], BF16) for _ in range(3)],
                               "rec": A.tl(nm("rec"), [128, QB], F32),
                               "qk": [A.tl(nm("qk"), [128, 4, 64], F32) for _ in range(2)],
                               "qkb": [A.tl(nm("qkb"), [128, 4, 64], BF16) for _ in range(2)],
                               "sq": A.tl(nm("sq"), [128, 4, 64], F32), "s4": A.tl(nm("s4"), [128, 4], F32),
                               "rt": A.tl(nm("rt"), [128, 4, 4, 8], F32),
                               "tq": [A.tl(nm("tq"), [128, 256], BF16) for _ in range(2)]})
                    ar["mtab"] = A.const("mtab")
                ar["kTp"].r = [Res("kTp0"), Res("kTp1")]
                E("pool", "memset", [], ar["kTp"].r, ar["kTp"].t[:], 0.0)
                for (kind, u, offs) in ulist:
                    if ui + 1 < len(units):
                        load_cols((ui + 1) % 2, unit_cols(l, units[ui + 1][2]))
                    else:
                        load_cols((ui + 1) % 2, [(g, w_out_d[l][:, g * 128:(g + 1) * 128]) for g in range(4)])
                    if kind == "b" and "b" in phases:
                        b_unit(l, u, ui % 2, ar)
                    if kind == "c" and "c" in phases:
                        c_unit(l, u, ui % 2, ar)
                    ui += 1
                A.close()
            load_cols(1, [(g, w_out_d[l][:, 512 + g * 128:512 + (g + 1) * 128]) for g in range(4)])
            if dbg and l == NL - 1 and s == NSEQ - 1:
                A = Arena()
                dm = A.tl(nm("dm"), [128, T], F32)
                for c in range(KC):
                    E("dve", "tensor_copy", [mixT.r[c]], [dm.R], dm.t[:], mixT.t[:, c, :])
                    S.dma(dbg_d[:, c, :], dm.t[:], reads=[dm.R], writes=[Res("dbgout")])
                E("pool", "memset", [], [dm.R], dm.t[:, 0:1], 0.0)
                A.close()
            phase_out(l)
            S.barrier()
        outres = Res("out")
        for i0 in range(0, NT, 4):
            n = min(4, NT - i0)
            S.dma(out_d[s, i0 * 128:(i0 + n) * 128, :].rearrange("(n p) d -> p n d", p=128), X.t[:, i0:i0 + n, :],
                  reads=[X.r[i] for i in range(i0, i0 + n)], writes=[outres])
        S.finish([outres])
    return nc, cst, S


_CACHE = {}


def kernel(x, norm_w, w_in, conv_w, a_log, dt_bias, gdn_norm_w, q_norm_w, k_norm_w, w_out):
    n_cores = 8
    B, T, _ = x.shape
    NSEQ = B // n_cores
    NL = w_in.shape[0]
    key = (T, NSEQ, NL)
    if key not in _CACHE:
        _CACHE[key] = build_program(T, NSEQ, NL)
    nc, cst, _ = _CACHE[key]
    f = lambda a: np.ascontiguousarray(np.asarray(a, dtype=np.float32))
    shared = {"w_in": f(w_in), "w_out": f(w_out), "norm_w": f(norm_w), "conv_w": f(conv_w), "a_log": f(a_log),
              "dt_bias": f(dt_bias), "gdn_norm_w": f(gdn_norm_w), "q_norm_w": f(q_norm_w), "k_norm_w": f(k_norm_w)}
    for k, v in cst.items():
        shared["c_" + k] = v
    xs = f(x)
    in_maps = []
    for c in range(n_cores):
        m = dict(shared)
        m["x"] = xs[c * NSEQ:(c + 1) * NSEQ]
        in_maps.append(m)
    res = run_bass_kernel_spmd(nc, in_maps, core_ids=list(range(n_cores)))
    return np.concatenate([np.asarray(r["out"], dtype=np.float32) for r in res.results], axis=0)
```

```python
import numpy as np
import ml_dtypes
import concourse.bass as bass
import concourse.mybir as mybir
from concourse.bass_utils import run_bass_kernel_spmd

F32 = mybir.dt.float32
BF16 = mybir.dt.bfloat16
AF = mybir.ActivationFunctionType
ALU = mybir.AluOpType
AX = mybir.AxisListType

D = 1024
KC = 8
IN_COLS = 4108
OFF_QA, OFF_KA, OFF_VA, OFF_ZA, OFF_BETA, OFF_ALPHA = 0, 384, 768, 1152, 1536, 1542
OFF_QB, OFF_KB, OFF_VB, OFF_ZB = 1548, 1804, 2060, 2316
OFF_QC, OFF_KC, OFF_VC, OFF_ZC = 2572, 2956, 3340, 3724
EPS = 1e-6
ROPE_THETA = 500000.0


class Res:
    __slots__ = ("name", "w", "r", "excl")

    def __init__(self, name, excl=False):
        self.name = name
        self.w = None
        self.r = []
        self.excl = excl


class Sched:
    def __init__(self, nc, n_dma_sems=24):
        self.nc = nc
        self.eng = {"pe": nc.tensor, "act": nc.scalar, "dve": nc.vector, "pool": nc.gpsimd, "sp": nc.sync}
        self.sems = {}
        self.cnt = {}
        for e in ("pe", "act", "dve", "pool"):
            self.sems[e] = nc.alloc_semaphore("s_" + e)
            self.cnt[e] = 0
        self.dma_sems = []
        for i in range(n_dma_sems):
            k = "d%d" % i
            self.sems[k] = nc.alloc_semaphore("s_" + k)
            self.cnt[k] = 0
            self.dma_sems.append(k)
        self.dma_rr = 0
        self.waited = {}
        self.pe_pending = False
        self.nwaits = 0
        self.nins = 0

    def _wait(self, e, key, val):
        if val <= 0:
            return
        if e == "pe" and key == "pe":
            return
        if key == "pe" and val > self.cnt["pe"]:
            raise RuntimeError("wait on a PE group that has not been closed with inc=True")
        if self.waited.get((e, key), 0) >= val:
            return
        self.waited[(e, key)] = val
        self.eng[e].wait_ge(self.sems[key], val)
        self.nwaits += 1

    def _deps(self, e, reads, writes):
        for t in reads:
            if t.w is not None:
                self._wait(e, *t.w)
        for t in writes:
            if t.w is not None:
                self._wait(e, *t.w)
            for rr in t.r:
                self._wait(e, *rr)

    @staticmethod
    def _compact(lst):
        best = {}
        for k, v in lst:
            if best.get(k, 0) < v:
                best[k] = v
        return list(best.items())

    def _record(self, key, val, reads, writes):
        for t in reads:
            t.r.append((key, val))
            if len(t.r) > 16:
                t.r = self._compact(t.r)
        for t in writes:
            t.w = (key, val)
            t.r = []

    def op(self, e, ins_fn, reads=(), writes=(), inc=True):
        if any(t.excl for t in reads):
            writes = list(writes) + [t for t in reads if t.excl]
            reads = [t for t in reads if not t.excl]
        self._deps(e, reads, writes)
        ins = ins_fn()
        self.nins += 1
        if e == "pe" and not inc:
            self.pe_pending = True
            self._record("pe", self.cnt["pe"] + 1, reads, writes)
            return ins
        if e == "pe":
            self.pe_pending = False
        self.cnt[e] += 1
        ins.then_inc(self.sems[e], 1)
        self._record(e, self.cnt[e], reads, writes)
        return ins

    def dma(self, out, in_, reads=(), writes=(), q="sp", **kw):
        key = self.dma_sems[self.dma_rr % len(self.dma_sems)]
        self.dma_rr += 1
        self._wait(q, key, self.cnt[key])
        self._deps(q, reads, writes)
        ins = self.eng[q].dma_start(out=out, in_=in_, **kw)
        self.nins += 1
        self.cnt[key] += 16
        ins.then_inc(self.sems[key], 16)
        self._record(key, self.cnt[key], reads, writes)
        return ins

    def barrier(self):
        assert not self.pe_pending
        for e in ("pe", "act", "dve", "pool"):
            for k in ("pe", "act", "dve", "pool"):
                if k != e:
                    self._wait(e, k, self.cnt[k])

    def finish(self, out_res):
        for t in out_res:
            if t.w is not None:
                self._wait("sp", *t.w)


def _consts(T):
    NT = T // 128
    QB = min(512, T)
    KPB = QB // 128
    c = {}
    idx = np.arange(128)
    c["identf"] = np.eye(128, dtype=np.float32)
    c["identb"] = np.eye(128, dtype=np.float32).astype(ml_dtypes.bfloat16)
    c["tri_gt"] = (idx[:, None] > idx[None, :]).astype(np.float32)
    c["tri_le"] = (idx[:, None] <= idx[None, :]).astype(np.float32)
    c["onesf"] = np.ones((128, 128), np.float32)
    c["onesb"] = np.ones((128, 64), np.float32).astype(ml_dtypes.bfloat16)
    bo = np.zeros((128, 128), np.float32)
    bo[:64, :64] = 1.0
    bo[64:, 64:] = 1.0
    c["blockones"] = bo
    incl = (idx[:, None] <= idx[None, :]).astype(np.float32)
    nst = -(idx[:, None] < idx[None, :]).astype(np.float32)
    c["incl2"] = np.stack([incl, incl], 1)
    c["nst2"] = np.stack([nst, nst], 1)
    c["ident2"] = np.stack([np.eye(128, dtype=np.float32)] * 2, 1)
    OFF0 = (KPB - 1) * 128
    W = OFF0 + (T - QB) + QB
    dl = np.arange(W)[None, :] - idx[:, None] - OFF0
    m = ((dl >= 0) & (dl <= 128)).astype(np.float32)
    m += ((dl >= 0) & (dl <= 512) & (dl % 4 == 0)).astype(np.float32)
    m += ((dl >= 0) & (dl <= 2048) & (dl % 16 == 0)).astype(np.float32)
    c["mtab"] = m.astype(ml_dtypes.bfloat16)
    half = 8
    inv_freq = (ROPE_THETA ** (-np.arange(half, dtype=np.float32) / half)).astype(np.float32)
    pos = np.arange(T, dtype=np.float32)
    ang = (pos[:, None] * inv_freq[None, :]).astype(np.float32)
    c["cost"] = np.ascontiguousarray(np.cos(ang).astype(np.float32).reshape(NT, 128, half).transpose(1, 0, 2))
    c["sint"] = np.ascontiguousarray(np.sin(ang).astype(np.float32).reshape(NT, 128, half).transpose(1, 0, 2))
    return c


def build_program(T=2048, NSEQ=2, NL=2, dbg=False, phases="nabco"):
    NT = T // 128
    QB = min(512, T)
    NQB = T // QB
    KPB = QB // 128
    OFF0 = (KPB - 1) * 128
    QA = min(256, T)
    KA = QA // 128
    nc = bass.Bass("TRN2", target_bir_lowering=False)
    cst = _consts(T)

    def din(name, shape, dt=F32):
        return nc.dram_tensor(name, list(shape), dt, kind="ExternalInput").ap()

    x_d = din("x", [NSEQ, T, D])
    w_in_d = din("w_in", [NL, D, IN_COLS])
    w_out_d = din("w_out", [NL, D, D])
    norm_w_d = din("norm_w", [NL, D])
    conv_w_d = din("conv_w", [NL, 4, 1152])
    a_log_d = din("a_log", [NL, 6])
    dt_bias_d = din("dt_bias", [NL, 6])
    gdn_w_d = din("gdn_norm_w", [NL, 64])
    qn_w_d = din("q_norm_w", [NL, 64])
    kn_w_d = din("k_norm_w", [NL, 64])
    cd = {}
    for k, v in cst.items():
        cd[k] = din("c_" + k, v.shape, BF16 if v.dtype == ml_dtypes.bfloat16 else F32)
    out_d = nc.dram_tensor("out", [NSEQ, T, D], F32, kind="ExternalOutput").ap()
    if dbg:
        dbg_d = nc.dram_tensor("dbg_mix", [128, 8, T], F32, kind="ExternalOutput").ap()

    S = Sched(nc)
    eng = S.eng

    def E(e, name, R, W, *a, **k):
        return S.op(e, lambda: getattr(eng[e], name)(*a, **k), R, W)

    def MM(out, lhsT, rhs, R, W, start=True, stop=True, inc=True):
        return S.op("pe", lambda: nc.tensor.matmul(out, lhsT=lhsT, rhs=rhs, start=start, stop=stop), R, W, inc=inc)

    def TR(out, in_, ident, R, W, inc=True):
        return S.op("pe", lambda: nc.tensor.transpose(out, in_, ident), R, W, inc=inc)

    class Tl:
        def __init__(self, name, shape, dt, nres=1):
            self.t = nc.alloc_sbuf_tensor(name, list(shape), dt)
            self.r = [Res("%s_%d" % (name, i)) for i in range(nres)]
            self.R = self.r[0]

    X = Tl("X", [128, NT, D], F32, NT)
    hT = Tl("hT", [128, KC, T], BF16, NQB)
    mixT = Tl("mixT", [128, KC, T], BF16, KC)
    wbuf = [Tl("wbuf%d" % i, [128, KC, 512], BF16, 4) for i in range(2)]
    stage = [Tl("stage%d" % i, [128, KC, 128], F32) for i in range(2)]
    C = {}
    ARENA_CONSTS = ("mtab", "incl2", "nst2", "ident2")
    for k, v in cst.items():
        if k in ARENA_CONSTS:
            continue
        C[k] = Tl("k_" + k, v.shape, BF16 if v.dtype == ml_dtypes.bfloat16 else F32)
    normw = Tl("normw", [128, NL, KC], F32)
    convw = Tl("convw", [128, NL, 9, 4], F32)
    nea = Tl("nea", [128, NL, 6], F32)
    dtb = Tl("dtb", [128, NL, 6], F32)
    gw = Tl("gw", [128, NL, 64], F32)
    wqk = Tl("wqk", [128, NL, 4, 64], F32)
    wba = Tl("wba", [128, KC, 12], BF16)
    wbas = Tl("wbas", [128, KC, 12], F32)
    ps = [nc.alloc_psum_tensor("ps%d" % i, [128, 512], F32) for i in range(8)]
    pr = [Res("ps%d" % i, excl=True) for i in range(8)]

    for k in C:
        S.dma(C[k].t[:], cd[k], writes=[C[k].R])
    for l in range(NL):
        S.dma(normw.t[:, l, :], norm_w_d[l].rearrange("(c p) -> p c", p=128), writes=[normw.R],
              allow_slow_non_contiguous=True)
        for g9 in range(9):
            S.dma(convw.t[:, l, g9, :], conv_w_d[l][:, g9 * 128:(g9 + 1) * 128].rearrange("k p -> p k"), writes=[convw.R],
                  allow_slow_non_contiguous=True)
        S.dma(nea.t[:, l, :], a_log_d[l].partition_broadcast(128), writes=[nea.R])
        S.dma(dtb.t[:, l, :], dt_bias_d[l].partition_broadcast(128), writes=[dtb.R])
        S.dma(gw.t[:, l, :], gdn_w_d[l].partition_broadcast(128), writes=[gw.R])
        for j in range(2):
            S.dma(wqk.t[:, l, j, :], qn_w_d[l].partition_broadcast(128), writes=[wqk.R])
            S.dma(wqk.t[:, l, 2 + j, :], kn_w_d[l].partition_broadcast(128), writes=[wqk.R])
    E("act", "activation", [nea.R], [nea.R], nea.t[:], nea.t[:], AF.Exp)
    E("dve", "tensor_scalar", [nea.R], [nea.R], nea.t[:], nea.t[:], -1.0, None, ALU.mult)
    E("dve", "tensor_scalar", [wqk.R], [wqk.R], wqk.t[:, :, 0:2, :], wqk.t[:, :, 0:2, :], 0.125, None, ALU.mult)

    bank_rr = [0]
    big_reg = nc.gpsimd.to_reg(1e30)

    def nbank(lo=0, hi=8):
        b = lo + bank_rr[0] % (hi - lo)
        bank_rr[0] += 1
        return b

    stg_rr = [0]

    def load_cols(slot, src2d_list):
        for g, src in src2d_list:
            st = stage[stg_rr[0] % 2]
            stg_rr[0] += 1
            S.dma(st.t[:], src.rearrange("(c p) n -> p c n", p=128), writes=[st.R])
            E("act", "copy", [st.R], [wbuf[slot].r[g]], wbuf[slot].t[:, :, g * 128:(g + 1) * 128], st.t[:])

    def unit_cols(l, offs):
        return [(g, w_in_d[l][:, o:o + 128]) for g, o in enumerate(offs)]

    def phase_norm(l, ar):
        ss, junk, xn = ar["ss"], ar["junk"], ar["xn"]
        for i in range(NT):
            E("act", "activation", [X.r[i]], [junk.R, ss.R], junk.t[:], X.t[:, i, :], AF.Square,
              accum_out=ss.t[:, i:i + 1])
        E("act", "activation", [ss.R], [ss.R], ss.t[:], ss.t[:], AF.Ln, scale=1.0 / D, bias=EPS)
        E("act", "activation", [ss.R], [ss.R], ss.t[:], ss.t[:], AF.Exp, scale=-0.5)
        for i in range(NT):
            xt = xn[i % 2]
            E("dve", "tensor_scalar", [X.r[i], ss.R], [xt.R], xt.t[:], X.t[:, i, :], ss.t[:, i:i + 1], None, ALU.mult)
            b = nbank(0, 2)
            pb = ps[b][:].bitcast(BF16)
            for c in range(KC):
                TR(pb[:, c * 128:(c + 1) * 128], xt.t[:, c * 128:(c + 1) * 128], C["identb"].t[:],
                   [xt.R, C["identb"].R], [pr[b]], inc=(c == KC - 1))
            E("dve", "tensor_tensor", [pr[b], normw.R], [hT.r[i // KPB]],
              hT.t[:, :, i * 128:(i + 1) * 128], pb.rearrange("p (c n) -> p c n", c=KC),
              normw.t[:, l, :].unsqueeze(2).broadcast_to([128, KC, 128]), ALU.mult)

    def proj_fm(slot, g, tb, b, w=None):
        w = w or QB
        for c in range(KC):
            MM(ps[b][:, 0:w], wbuf[slot].t[:, c, g * 128:(g + 1) * 128], hT.t[:, c, tb * w:(tb + 1) * w],
               [wbuf[slot].r[g], hT.r[(tb * w) // QB]], [pr[b]], start=(c == 0), stop=(c == KC - 1), inc=(c == KC - 1))

    def proj_tm(slot, c0, ncol, i, out_ap, b):
        gs = sorted(set([c0 // 128, (c0 + ncol - 1) // 128]))
        gs = list(range(gs[0], gs[-1] + 1))
        for c in range(KC):
            MM(out_ap, hT.t[:, c, i * 128:(i + 1) * 128], wbuf[slot].t[:, c, c0:c0 + ncol],
               [wbuf[slot].r[g] for g in gs] + [hT.r[i // KPB]], [pr[b]],
               start=(c == 0), stop=(c == KC - 1), inc=(c == KC - 1))

    def silu_from_psum(b, ncol, et, out_ap, out_res, extra_reads=()):
        E("act", "activation", [pr[b]], [et.R], et.t[:, 0:ncol], ps[b][:, 0:ncol], AF.Exp, scale=-1.0)
        E("act", "activation", [et.R], [et.R], et.t[:, 0:ncol], et.t[:, 0:ncol], AF.Ln, bias=1.0)
        E("act", "activation", [et.R], [et.R], et.t[:, 0:ncol], et.t[:, 0:ncol], AF.Exp, scale=-1.0)
        E("dve", "tensor_tensor", [pr[b], et.R] + list(extra_reads), [out_res], out_ap, ps[b][:, 0:ncol],
          et.t[:, 0:ncol], ALU.mult)

    def bc_inproj_common(slot, ar, kind):
        zs, et = ar["zs"], ar["et"]
        for tb in range(NQB):
            b = nbank(0, 2)
            proj_fm(slot, 3, tb, b)
            silu_from_psum(b, QB, et[tb % 2], zs.t[:, tb * QB:(tb + 1) * QB], zs.R)

    def b_unit(l, u, slot, ar):
        qT, kTp, vt, zs = ar["qT"], ar["kTp"], ar["vt"], ar["zs"]
        for tb in range(NQB):
            b = nbank(0, 2)
            proj_fm(slot, 0, tb, b)
            E("act", "activation", [pr[b]], [qT.R], qT.t[:, tb * QB:(tb + 1) * QB], ps[b][:, 0:QB], AF.Copy, scale=0.125)
            b = nbank(0, 2)
            proj_fm(slot, 1, tb, b)
            E("act", "copy", [pr[b]], [kTp.r[0]], kTp.t[0:64, 0, tb * QB:(tb + 1) * QB], ps[b][0:64, 0:QB])
            E("dve", "tensor_copy", [pr[b]], [kTp.r[1]], kTp.t[64:128, 1, tb * QB:(tb + 1) * QB], ps[b][64:128, 0:QB])
        for i in range(NT):
            b = nbank(0, 2)
            proj_tm(slot, 256, 128, i, ps[b][:, 0:128], b)
            E("act", "copy", [pr[b]], [vt.R], vt.t[:, i, :], ps[b][:, 0:128])
        bc_inproj_common(slot, ar, "b")
        sp, Pb, wts, Rb = ar["sp"], ar["P"], ar["wts"], ar["Rb"]
        ob = 7
        zb = [2, 3]
        cb = [4, 5]
        for qb in range(NQB):
            qbase = qb * QB
            nkt = (qb + 1) * KPB
            kts = list(range(nkt - 1, -1, -1))

            def stage_a(n, h):
                kt = kts[n]
                kbase = kt * 128
                diag = kbase + 127 >= qbase
                spb, Pt = sp[h][n % 2], Pb[h][n % 2]
                MM(ps[zb[h]][:, 0:QB], kTp.t[:, h, kbase:kbase + 128], qT.t[:, qbase:qbase + QB],
                   [kTp.r[h], qT.R], [pr[zb[h]]])
                E("act", "activation", [pr[zb[h]]], [spb.R], spb.t[:, 0:QB], ps[zb[h]][:, 0:QB], AF.Exp, scale=-1.0)
                E("act", "activation", [spb.R], [spb.R], spb.t[:, 0:QB], spb.t[:, 0:QB], AF.Ln, bias=1.0)
                E("dve", "tensor_tensor", [pr[zb[h]], spb.R], [Pt.R], Pt.t[:, 0:QB], ps[zb[h]][:, 0:QB],
                  spb.t[:, 0:QB], ALU.add)
                if diag:
                    E("pool", "affine_select", [Pt.R], [Pt.R], out=Pt.t[:, 0:QB], in_=Pt.t[:, 0:QB],
                      pattern=[[1, QB]], compare_op=ALU.is_gt, fill=0.0, base=qbase - kbase, channel_multiplier=-1)

            def stage_b1(n, h):
                first = (n == 0)
                Pt = Pb[h][n % 2]
                MM(ps[cb[h]][:, 0:QB], C["tri_gt"].t[:], Pt.t[:, 0:QB], [C["tri_gt"].R, Pt.R], [pr[cb[h]]],
                   start=True, stop=first, inc=first)
                if not first:
                    MM(ps[cb[h]][:, 0:QB], C["onesf"].t[:], Rb[h].t[:, 0:QB], [C["onesf"].R, Rb[h].R], [pr[cb[h]]],
                       start=False, stop=True)

            def stage_b2(n, h):
                kt = kts[n]
                kbase = kt * 128
                diag = kbase + 127 >= qbase
                first = (n == 0)
                last = (n == len(kts) - 1)
                spb, Pt, wt = sp[h][n % 2], Pb[h][n % 2], wts[h][n % 2]
                E("dve", "tensor_tensor", [pr[cb[h]], spb.R], [spb.R], spb.t[:, 0:QB], ps[cb[h]][:, 0:QB],
                  spb.t[:, 0:QB], ALU.add)
                if diag:
                    E("pool", "affine_select", [spb.R], [spb.R], out=spb.t[:, 0:QB], in_=spb.t[:, 0:QB],
                      pattern=[[1, QB]], compare_op=ALU.is_gt, fill=big_reg, base=qbase - kbase, channel_multiplier=-1)
                E("act", "activation", [spb.R], [wt.R], wt.t[:, 0:QB], spb.t[:, 0:QB], AF.Exp, scale=-1.0)
                if not last:
                    if first:
                        E("pool", "tensor_copy", [Pt.R], [Rb[h].R], Rb[h].t[:, 0:QB], Pt.t[:, 0:QB])
                    else:
                        E("pool", "tensor_tensor", [Pt.R, Rb[h].R], [Rb[h].R], out=Rb[h].t[:, 0:QB], in0=Rb[h].t[:, 0:QB],
                          in1=Pt.t[:, 0:QB], op=ALU.add)

            def stage_c(n, h):
                kt = kts[n]
                first = (n == 0)
                last = (n == len(kts) - 1)
                wt = wts[h][n % 2]
                MM(ps[ob][64 * h:64 * h + 64, 0:QB], vt.t[:, kt, 64 * h:64 * h + 64], wt.t[:, 0:QB],
                   [vt.R, wt.R], [pr[ob]], start=first, stop=last)

            for n in range(len(kts) + 2):
                if n < len(kts):
                    for h in range(2):
                        stage_a(n, h)
                if 1 <= n <= len(kts):
                    for h in range(2):
                        stage_b1(n - 1, h)
                    for h in range(2):
                        stage_b2(n - 1, h)
                if n >= 2:
                    for h in range(2):
                        stage_c(n - 2, h)
            E("dve", "tensor_tensor", [pr[ob], zs.R], [mixT.r[3 + u]], mixT.t[:, 3 + u, qbase:qbase + QB],
              ps[ob][:, 0:QB], zs.t[:, qbase:qbase + QB], ALU.mult)

    def c_unit(l, u, slot, ar):
        import os as _os
        CSTOP = int(_os.environ.get("CSTOP", "99"))
        qT, kTp, vt, zs = ar["qT"], ar["kTp"], ar["vt"], ar["zs"]
        qk, sq, s4, qkb = ar["qk"], ar["sq"], ar["s4"], ar["qkb"]
        rt = ar["rt"]
        for i in range(NT):
            b = nbank(0, 2)
            proj_tm(slot, 0, 384, i, ps[b][:, 0:384], b)
            E("act", "copy", [pr[b]], [vt.R], vt.t[:, i, :], ps[b][:, 256:384])
            q4 = qk[i % 2]
            E("act", "copy", [pr[b]], [q4.R], q4.t[:], ps[b][:, 0:256].rearrange("p (a d) -> p a d", a=4))
            if CSTOP <= 1:
                continue
            E("pool", "tensor_tensor", [q4.R], [sq.R], out=sq.t[:], in0=q4.t[:], in1=q4.t[:], op=ALU.mult)
            E("dve", "tensor_reduce", [sq.R], [s4.R], out=s4.t[:], in_=sq.t[:], axis=AX.X, op=ALU.add)
            E("act", "activation", [s4.R], [s4.R], s4.t[:], s4.t[:], AF.Ln, scale=1.0 / 64, bias=EPS)
            E("act", "activation", [s4.R], [s4.R], s4.t[:], s4.t[:], AF.Exp, scale=-0.5)
            E("dve", "tensor_tensor", [q4.R, s4.R], [q4.R], q4.t[:], q4.t[:],
              s4.t[:].unsqueeze(2).broadcast_to([128, 4, 64]), ALU.mult)
            E("pool", "tensor_tensor", [q4.R, wqk.R], [q4.R], out=q4.t[:], in0=q4.t[:], in1=wqk.t[:, l, :, :], op=ALU.mult)
            if CSTOP <= 2:
                continue
            qb_ = qkb[i % 2]
            E("act", "copy", [q4.R], [qb_.R], qb_.t[:], q4.t[:])
            cs = C["cost"].t[:, i, :].unsqueeze(1).broadcast_to([128, 4, 8])
            sn = C["sint"].t[:, i, :].unsqueeze(1).broadcast_to([128, 4, 8])
            x1 = q4.t[:, :, 0:8]
            x2 = q4.t[:, :, 8:16]
            E("dve", "tensor_tensor", [q4.R, C["cost"].R], [rt.R], rt.t[:, 0, :, :], x1, cs, ALU.mult)
            E("dve", "tensor_tensor", [q4.R, C["sint"].R], [rt.R], rt.t[:, 1, :, :], x2, sn, ALU.mult)
            E("dve", "tensor_tensor", [q4.R, C["cost"].R], [rt.R], rt.t[:, 2, :, :], x2, cs, ALU.mult)
            E("dve", "tensor_tensor", [q4.R, C["sint"].R], [rt.R], rt.t[:, 3, :, :], x1, sn, ALU.mult)
            E("dve", "tensor_tensor", [rt.R], [qb_.R], qb_.t[:, :, 0:8], rt.t[:, 0, :, :], rt.t[:, 1, :, :], ALU.subtract)
            E("dve", "tensor_tensor", [rt.R], [qb_.R], qb_.t[:, :, 8:16], rt.t[:, 2, :, :], rt.t[:, 3, :, :], ALU.add)
            if CSTOP <= 3:
                continue
            b2 = nbank(0, 2)
            pb = ps[b2][:].bitcast(BF16)
            qflat = qb_.t[:].rearrange("p a d -> p (a d)")
            TR(pb[:, 0:128], qflat[:, 0:128], C["identb"].t[:], [qb_.R, C["identb"].R], [pr[b2]], inc=False)
            TR(pb[:, 128:256], qflat[:, 128:256], C["identb"].t[:], [qb_.R, C["identb"].R], [pr[b2]])
            tq = ar["tq"][i % 2]
            E("dve", "tensor_copy", [pr[b2]], [tq.R], tq.t[:], pb[:, 0:256])
            E("act", "copy", [tq.R], [qT.R], qT.t[:, i * 128:(i + 1) * 128], tq.t[:, 0:128])
            E("pool", "tensor_copy", [tq.R], [kTp.r[0]], kTp.t[0:64, 0, i * 128:(i + 1) * 128], tq.t[0:64, 128:256])
            E("pool", "tensor_copy", [tq.R], [kTp.r[1]], kTp.t[64:128, 1, i * 128:(i + 1) * 128], tq.t[64:128, 128:256])
        if CSTOP <= 4:
            return
        bc_inproj_common(slot, ar, "c")
        if CSTOP <= 5:
            return
        Eb, EMb, rec = ar["E"], ar["EM"], ar["rec"]
        nb_, db_ = 6, 7
        sb = [2, 3, 4, 5]
        NE = len(Eb)
        for qb in range(NQB):
            qbase = qb * QB
            nkt = (qb + 1) * KPB
            pairs = [(kt, h) for kt in range(nkt - 1, -1, -1) for h in range(2)]

            def stage_a(n):
                kt, h = pairs[n]
                kbase = kt * 128
                off = qbase - kbase + OFF0
                b = sb[n % 4]
                Et, EMt = Eb[n % NE], EMb[n % NE]
                MM(ps[b][:, 0:QB], kTp.t[:, h, kbase:kbase + 128], qT.t[:, qbase:qbase + QB],
                   [kTp.r[h], qT.R], [pr[b]])
                E("act", "activation", [pr[b]], [Et.R], Et.t[:, 0:QB], ps[b][:, 0:QB], AF.Exp)
                E("pool" if n % 2 == 0 else "dve", "tensor_tensor", [Et.R, ar["mtab"].R], [EMt.R], out=EMt.t[:, 0:QB],
                  in0=Et.t[:, 0:QB], in1=ar["mtab"].t[:, off:off + QB], op=ALU.mult)

            def stage_b(n):
                kt, h = pairs[n]
                first = (kt == nkt - 1)
                last = (kt == 0)
                EMt = EMb[n % NE]
                MM(ps[nb_][64 * h:64 * h + 64, 0:QB], vt.t[:, kt, 64 * h:64 * h + 64], EMt.t[:, 0:QB],
                   [vt.R, EMt.R], [pr[nb_]], start=first, stop=last, inc=False)
                MM(ps[db_][64 * h:64 * h + 64, 0:QB], C["onesb"].t[:, 0:64], EMt.t[:, 0:QB],
                   [C["onesb"].R, EMt.R], [pr[db_]], start=first, stop=last)

            SK = 2
            for n in range(len(pairs) + SK):
                if n < len(pairs):
                    stage_a(n)
                if n >= SK:
                    stage_b(n - SK)
            E("dve", "reciprocal", [pr[db_]], [rec.R], rec.t[:, 0:QB], ps[db_][:, 0:QB])
            E("dve", "tensor_tensor", [pr[nb_], rec.R], [rec.R], rec.t[:, 0:QB], ps[nb_][:, 0:QB], rec.t[:, 0:QB], ALU.mult)
            E("pool", "tensor_tensor", [rec.R, zs.R], [mixT.r[5 + u]], out=mixT.t[:, 5 + u, qbase:qbase + QB],
              in0=rec.t[:, 0:QB], in1=zs.t[:, qbase:qbase + QB], op=ALU.mult)

    def a_prep(l, ar):
        beta, gg, t12 = ar["beta"], ar["g"], ar["t12"]
        S.dma(wbas.t[:], w_in_d[l][:, OFF_BETA:OFF_BETA + 12].rearrange("(c p) n -> p c n", p=128), writes=[wbas.R])
        E("pool", "tensor_copy", [wbas.R], [wba.R], wba.t[:], wbas.t[:])
        b = nbank(0, 2)
        for i in range(NT):
            for c in range(KC):
                MM(ps[b][:, i * 12:(i + 1) * 12], hT.t[:, c, i * 128:(i + 1) * 128], wba.t[:, c, :],
                   [wba.R, hT.r[i // KPB]], [pr[b]], start=(c == 0), stop=(c == KC - 1),
                   inc=(c == KC - 1 and i == NT - 1))
        pv = ps[b][:, 0:NT * 12].rearrange("p (n k) -> p n k", k=12)
        E("act", "activation", [pr[b]], [beta.R], beta.t[:], pv[:, :, 0:6], AF.Exp, scale=-1.0)
        E("act", "activation", [beta.R], [beta.R], beta.t[:], beta.t[:], AF.Ln, bias=1.0)
        E("act", "activation", [beta.R], [beta.R], beta.t[:], beta.t[:], AF.Exp, scale=-1.0)
        E("dve", "tensor_tensor", [pr[b], dtb.R], [t12.R], t12.t[:], pv[:, :, 6:12],
          dtb.t[:, l, :].unsqueeze(1).broadcast_to([128, NT, 6]), ALU.add)
        E("act", "activation", [t12.R], [t12.R], t12.t[:], t12.t[:], AF.Exp)
        E("act", "activation", [t12.R], [t12.R], t12.t[:], t12.t[:], AF.Ln, bias=1.0)
        E("dve", "tensor_tensor", [t12.R, nea.R], [gg.R], gg.t[:], t12.t[:],
          nea.t[:, l, :].unsqueeze(1).broadcast_to([128, NT, 6]), ALU.mult)

    def a_unit(l, u, slot, ar):
        beta, gg = ar["beta"], ar["g"]
        xb, acc, et, yq, qkvT = ar["xb"], ar["acc"], ar["et"], ar["yq"], ar["qkvT"]
        gz = ar["gz"]
        Ssb = ar["S"]
        identf, tri_le, onesf = C["identf"], C["tri_le"], C["onesf"]
        E("pool", "memset", [], [Ssb.R], Ssb.t[:], 0.0)
        for g in range(3):
            E("pool", "memset", [], [xb[g].R], xb[g].t[:, 0:3], 0.0)
        for tb in range(T // QA):
            for g in range(3):
                b = nbank(0, 2)
                proj_fm(slot, g, tb, b, QA)
                gi = g * 3 + u
                xg = xb[g]
                E("act", "copy", [pr[b]], [xg.R], xg.t[:, 3:3 + QA], ps[b][:, 0:QA])
                a_ = acc[g % 2]
                E("dve", "tensor_scalar", [xg.R, convw.R], [a_.R], a_.t[:, 0:QA], xg.t[:, 0:QA],
                  convw.t[:, l, gi, 0:1], None, ALU.mult)
                for k in range(1, 4):
                    E("dve", "scalar_tensor_tensor", [xg.R, convw.R, a_.R], [a_.R], a_.t[:, 0:QA], xg.t[:, k:k + QA],
                      convw.t[:, l, gi, k:k + 1], a_.t[:, 0:QA], ALU.mult, ALU.add)
                E("pool", "tensor_copy", [xg.R], [xg.R], xg.t[:, 0:3], xg.t[:, QA:QA + 3])
                e_ = et[g % 2]
                E("act", "activation", [a_.R], [e_.R], e_.t[:, 0:QA], a_.t[:, 0:QA], AF.Exp, scale=-1.0)
                E("act", "activation", [e_.R], [e_.R], e_.t[:, 0:QA], e_.t[:, 0:QA], AF.Ln, bias=1.0)
                E("act", "activation", [e_.R], [e_.R], e_.t[:, 0:QA], e_.t[:, 0:QA], AF.Exp, scale=-1.0)
                if g == 2:
                    E("dve", "tensor_tensor", [a_.R, e_.R], [qkvT[2].R], qkvT[2].t[:, 0:QA], a_.t[:, 0:QA], e_.t[:, 0:QA], ALU.mult)
                else:
                    y = yq[g]
                    E("dve", "tensor_tensor", [a_.R, e_.R], [y.R], y.t[:, 0:QA], a_.t[:, 0:QA], e_.t[:, 0:QA], ALU.mult)
                    E("pool", "tensor_tensor", [y.R], [e_.R], out=e_.t[:, 0:QA], in0=y.t[:, 0:QA], in1=y.t[:, 0:QA], op=ALU.mult)
                    b2 = nbank(0, 2)
                    MM(ps[b2][:, 0:QA], C["blockones"].t[:], e_.t[:, 0:QA], [C["blockones"].R, e_.R], [pr[b2]])
                    E("act", "activation", [pr[b2]], [e_.R], e_.t[:, 0:QA], ps[b2][:, 0:QA], AF.Ln, bias=EPS)
                    E("act", "activation", [e_.R], [e_.R], e_.t[:, 0:QA], e_.t[:, 0:QA], AF.Exp, scale=-0.5)
                    E("dve", "scalar_tensor_tensor", [y.R, e_.R], [qkvT[g].R], qkvT[g].t[:, 0:QA], y.t[:, 0:QA],
                      0.125 if g == 0 else 1.0, e_.t[:, 0:QA], ALU.mult, ALU.mult)
            b = nbank(0, 2)
            for j in range(KA):
                i = tb * KA + j
                proj_tm(slot, 384, 128, i, ps[b][:, j * 128:(j + 1) * 128], b)
            silu_from_psum(b, QA, et[0], gz.t[:].rearrange("p a d -> p (a d)"), gz.R)
            E("pool", "tensor_tensor", [gz.R, gw.R], [gz.R], out=gz.t[:],
              in0=gz.t[:],
              in1=gw.t[:, l, :].unsqueeze(1).broadcast_to([128, KA * 2, 64]), op=ALU.mult)
            for j in range(KA):
                a_tile(l, u, tb * KA + j, j, ar)

    def a_tile(l, u, i, j, ar):
        import os as _os
        ASTOP = int(_os.environ.get("ASTOP", "99"))
        if ASTOP <= 1:
            return
        beta, gg = ar["beta"], ar["g"]
        qkvT, gz, Ssb = ar["qkvT"], ar["gz"], ar["S"]
        identf, tri_le, onesf = C["identf"], C["tri_le"], C["onesf"]
        k = i % 2
        kv, gcs, egc, sbeg, gb, d0, DT, DTs, DTi = (ar[n][k] for n in ("kv", "gcs", "egc", "sbeg", "gb", "d0", "DT", "DTs", "DTi"))
        egr, kTp, M0p, qkT, MN, Pm, bv, usb, wTp, qgTp, vnew = (ar[n][k] for n in ("egr", "kTp", "M0p", "qkT", "MN", "Pm", "bv", "usb", "wTp", "qgTp", "vnew"))
        osb, osq, oss, ot, obf = (ar[n][k] for n in ("osb", "osq", "oss", "ot", "obf"))
        cs = slice(j * 128, (j + 1) * 128)
        bi = beta.t[:, i, 2 * u:2 * u + 2]
        gi = gg.t[:, i, 2 * u:2 * u + 2]
        qT_, kT_, vT_ = qkvT[0], qkvT[1], qkvT[2]
        b = 2
        TR(ps[b][:, 0:128], kT_.t[:, cs], identf.t[:], [kT_.R, identf.R], [pr[b]], inc=False)
        TR(ps[b][:, 128:256], vT_.t[:, cs], identf.t[:], [vT_.R, identf.R], [pr[b]], inc=False)
        MM(ps[b][:, 256:258], tri_le.t[:], gi, [tri_le.R, gg.R], [pr[b]])
        E("act", "copy", [pr[b]], [kv.R], kv.t[:], ps[b][:, 0:256].rearrange("p (a h d) -> p a h d", a=2, h=2))
        E("dve", "tensor_copy", [pr[b]], [gcs.R], gcs.t[:], ps[b][:, 256:258])
        E("act", "activation", [gcs.R], [egc.R], egc.t[:], gcs.t[:], AF.Exp)
        E("pool", "tensor_tensor", [egc.R, beta.R], [sbeg.R], out=sbeg.t[:], in0=egc.t[:], in1=bi, op=ALU.mult)
        if ASTOP <= 2:
            return
        for h in range(2):
            E("dve", "tensor_scalar", [onesf.R, gg.R], [gb.R], gb.t[:, h, :], onesf.t[:], gi[:, h:h + 1], None, ALU.mult)
            E("dve", "tensor_scalar", [onesf.R, gg.R], [gb.R], gb.t[:, 2, 64 * h:64 * h + 64], onesf.t[:, 0:64], gi[:, h:h + 1], None, ALU.mult)
        b = 3
        for h in range(3):
            MM(ps[b][:, h * 128:(h + 1) * 128], gb.t[:, h, :], tri_le.t[:], [gb.R, tri_le.R], [pr[b]], inc=(h == 2))
        for h in range(2):
            E("dve", "tensor_scalar", [pr[b], gcs.R], [d0.R], d0.t[:, h, :], ps[b][:, h * 128:(h + 1) * 128],
              gcs.t[:, h:h + 1], 0.0, ALU.subtract, ALU.min)
        E("act", "activation", [d0.R], [DT.R], DT.t[:], d0.t[:], AF.Exp)
        E("act", "activation", [pr[b]], [egr.R], egr.t[:], ps[b][:, 256:384], AF.Exp)
        E("pool", "tensor_tensor", [DT.R, ar["nst2"].R], [DTs.R], out=DTs.t[:], in0=DT.t[:], in1=ar["nst2"].t[:], op=ALU.mult)
        E("pool", "tensor_tensor", [DT.R, ar["incl2"].R], [DTi.R], out=DTi.t[:], in0=DT.t[:], in1=ar["incl2"].t[:], op=ALU.mult)
        if ASTOP <= 3:
            return
        E("pool", "tensor_copy", [kT_.R], [kTp.R], kTp.t[0:64, 0, :], kT_.t[0:64, cs])
        E("pool", "tensor_copy", [kT_.R], [kTp.R], kTp.t[64:128, 1, :], kT_.t[64:128, cs])
        A4 = int(_os.environ.get("A4", "99"))
        if A4 <= 1:
            return
        b = 4
        AV = _os.environ.get("AV", "kq")
        for h in range(2):
            if "x" in AV:
                MM(ps[b][:, h * 128:(h + 1) * 128], kT_.t[:, cs], kT_.t[:, cs], [kTp.R, kT_.R], [pr[b]], inc=(h == 1))
            if "z" in AV:
                MM(ps[b][:, h * 128:(h + 1) * 128], gb.t[:, h, :], C["tri_le"].t[:], [gb.R, tri_le.R], [pr[b]], inc=(h == 1))
            if "y" in AV:
                MM(ps[b][:, h * 128:(h + 1) * 128], kTp.t[:, h, :], C["tri_le"].t[:], [kTp.R, kT_.R], [pr[b]], inc=(h == 1))
            if "k" in AV:
                MM(ps[b][:, h * 128:(h + 1) * 128], kTp.t[:, h, :], kT_.t[:, cs], [kTp.R, kT_.R], [pr[b]], inc=("q" not in AV and h == 1))
        for h in range(2):
            if "q" in AV:
                MM(ps[b][:, 256 + h * 128:256 + (h + 1) * 128], kTp.t[:, h, :], qT_.t[:, cs], [kTp.R, qT_.R], [pr[b]], inc=(h == 1))
        if A4 <= 2:
            return
        E("dve", "tensor_tensor", [pr[b], DTs.R], [M0p.R], M0p.t[:], ps[b][:, 0:256].rearrange("p (h n) -> p h n", h=2), DTs.t[:], ALU.mult)
        E("dve", "tensor_tensor", [pr[b], DTi.R], [qkT.R], qkT.t[:], ps[b][:, 256:512].rearrange("p (h n) -> p h n", h=2), DTi.t[:], ALU.mult)
        if ASTOP <= 4:
            return
        b = 5
        for h in range(2):
            TR(ps[b][:, h * 128:(h + 1) * 128], M0p.t[:, h, :], identf.t[:], [M0p.R, identf.R], [pr[b]], inc=(h == 1))
        mn = MN[0]
        for h in range(2):
            E("dve", "tensor_scalar", [pr[b], beta.R], [mn.R], mn.t[:, 2 + h, :], ps[b][:, h * 128:(h + 1) * 128], bi[:, h:h + 1], None, ALU.mult)
        b = 5
        for h in range(2):
            TR(ps[b][:, h * 128:(h + 1) * 128], mn.t[:, 2 + h, :], identf.t[:], [mn.R, identf.R], [pr[b]], inc=(h == 1))
        E("act", "copy", [pr[b]], [mn.R], mn.t[:, 0:2, :], ps[b][:, 0:256].rearrange("p (h n) -> p h n", h=2))
        E("dve", "tensor_tensor", [pr[b], ar["ident2"].R], [Pm.R], Pm.t[:], ps[b][:, 0:256].rearrange("p (h n) -> p h n", h=2), ar["ident2"].t[:], ALU.add)
        if ASTOP <= 5:
            return
        for lv in range(1, 7):
            mo = MN[(lv - 1) % 2]
            mw = MN[lv % 2]
            b = 6
            if lv < 6:
                for h in range(2):
                    MM(ps[b][:, h * 128:(h + 1) * 128], mo.t[:, 2 + h, :], mo.t[:, h, :], [mo.R], [pr[b]], inc=False)
            for h in range(2):
                MM(ps[b][:, 256 + h * 128:256 + (h + 1) * 128], mo.t[:, h, :], mo.t[:, 2 + h, :], [mo.R], [pr[b]], inc=(h == 1))
            if lv < 6:
                E("act", "copy", [pr[b]], [mw.R], mw.t[:, 0:2, :], ps[b][:, 0:256].rearrange("p (h n) -> p h n", h=2))
            E("dve", "tensor_copy", [pr[b]], [mw.R], mw.t[:, 2:4, :], ps[b][:, 256:512].rearrange("p (h n) -> p h n", h=2))
            b = 7
            for h in range(2):
                MM(ps[b][:, h * 128:(h + 1) * 128], mw.t[:, 2 + h, :], Pm.t[:, h, :], [mw.R, Pm.R], [pr[b]], inc=(h == 1))
            E("dve", "tensor_tensor", [pr[b], Pm.R], [Pm.R], Pm.t[:], Pm.t[:], ps[b][:, 0:256].rearrange("p (h n) -> p h n", h=2), ALU.add)
        if ASTOP <= 6:
            return
        E("pool", "tensor_tensor", [kv.R, beta.R], [bv.R], out=bv.t[:, 0, :, :], in0=kv.t[:, 1, :, :],
          in1=bi.unsqueeze(2).broadcast_to([128, 2, 64]), op=ALU.mult)
        E("pool", "tensor_tensor", [kv.R, sbeg.R], [bv.R], out=bv.t[:, 1, :, :], in0=kv.t[:, 0, :, :],
          in1=sbeg.t[:].unsqueeze(2).broadcast_to([128, 2, 64]), op=ALU.mult)
        E("pool", "tensor_tensor", [kv.R, DT.R], [bv.R], out=bv.t[:, 2, :, :], in0=kv.t[:, 0, :, :],
          in1=DT.t[:, :, 127:128].broadcast_to([128, 2, 64]), op=ALU.mult)
        b = 3
        for h in range(2):
            MM(ps[b][:, h * 64:(h + 1) * 64], Pm.t[:, h, :], bv.t[:, 0, h, :], [Pm.R, bv.R], [pr[b]], inc=False)
        for h in range(2):
            MM(ps[b][64 * h:64 * h + 64, 128:256], bv.t[:, 1, h, :], Pm.t[:, h, :], [Pm.R, bv.R], [pr[b]], inc=(h == 1))
        E("act", "copy", [pr[b]], [usb.R], usb.t[:], ps[b][:, 0:128].rearrange("p (h d) -> p h d", h=2))
        E("dve", "tensor_copy", [pr[b]], [wTp.R], wTp.t[0:64, 0, :], ps[b][0:64, 128:256])
        E("dve", "tensor_copy", [pr[b]], [wTp.R], wTp.t[64:128, 1, :], ps[b][64:128, 128:256])
        E("pool", "tensor_tensor", [qT_.R, egr.R], [qgTp.R], out=qgTp.t[0:64, 0, :], in0=qT_.t[0:64, cs], in1=egr.t[0:64, :], op=ALU.mult)
        E("pool", "tensor_tensor", [qT_.R, egr.R], [qgTp.R], out=qgTp.t[64:128, 1, :], in0=qT_.t[64:128, cs], in1=egr.t[64:128, :], op=ALU.mult)
        if ASTOP <= 7:
            return
        b = 4
        for h in range(2):
            MM(ps[b][:, h * 64:(h + 1) * 64], wTp.t[:, h, :], Ssb.t[:], [wTp.R, Ssb.R], [pr[b]], inc=(h == 1))
        E("dve", "tensor_tensor", [usb.R, pr[b]], [vnew.R], vnew.t[:], usb.t[:], ps[b][:, 0:128].rearrange("p (h d) -> p h d", h=2), ALU.subtract)
        bo = 6
        for h in range(2):
            MM(ps[bo][:, h * 64:(h + 1) * 64], qgTp.t[:, h, :], Ssb.t[:], [qgTp.R, Ssb.R], [pr[bo]], start=True, stop=False, inc=False)
            MM(ps[bo][:, h * 64:(h + 1) * 64], qkT.t[:, h, :], vnew.t[:, h, :], [qkT.R, vnew.R], [pr[bo]], start=False, stop=True, inc=(h == 1))
        b = 7
        for h in range(2):
            MM(ps[b][64 * h:64 * h + 64, 0:64], bv.t[:, 2, h, :], vnew.t[:, h, :], [bv.R, vnew.R], [pr[b]], inc=(h == 1))
        E("dve", "scalar_tensor_tensor", [Ssb.R, egr.R, pr[b]], [Ssb.R], Ssb.t[:], Ssb.t[:], egr.t[:, 127:128], ps[b][:, 0:64], ALU.mult, ALU.add)
        if ASTOP <= 8:
            return
        E("act", "copy", [pr[bo]], [osb.R], osb.t[:], ps[bo][:, 0:128].rearrange("p (h d) -> p h d", h=2))
        E("pool", "tensor_tensor", [osb.R], [osq.R], out=osq.t[:], in0=osb.t[:], in1=osb.t[:], op=ALU.mult)
        E("dve", "tensor_reduce", [osq.R], [oss.R], out=oss.t[:], in_=osq.t[:], axis=AX.X, op=ALU.add)
        E("act", "activation", [oss.R], [oss.R], oss.t[:], oss.t[:], AF.Ln, scale=1.0 / 64, bias=EPS)
        E("act", "activation", [oss.R], [oss.R], oss.t[:], oss.t[:], AF.Exp, scale=-0.5)
        E("dve", "tensor_tensor", [osb.R, oss.R], [ot.R], ot.t[:], osb.t[:], oss.t[:].unsqueeze(2).broadcast_to([128, 2, 64]), ALU.mult)
        E("dve", "tensor_tensor", [ot.R, gz.R], [obf.R], obf.t[:], ot.t[:], gz.t[:, 2 * j:2 * j + 2, :], ALU.mult)
        b = nbank(0, 2)
        pb = ps[b][:].bitcast(BF16)
        TR(pb[:, 0:128], obf.t[:].rearrange("p h d -> p (h d)"), C["identb"].t[:], [obf.R, C["identb"].R], [pr[b]])
        E("dve", "tensor_copy", [pr[b]], [mixT.r[u]], mixT.t[:, u, i * 128:(i + 1) * 128], pb[:, 0:128])

    def phase_out(l):
        for hf in range(2):
            for i in range(NT):
                b = nbank(0, 4)
                for c in range(KC):
                    MM(ps[b][:, 0:512], mixT.t[:, c, i * 128:(i + 1) * 128], wbuf[hf].t[:, c, :],
                       [mixT.r[c]] + wbuf[hf].r, [pr[b]], start=(c == 0), stop=(c == KC - 1), inc=(c == KC - 1))
                E("dve", "tensor_tensor", [pr[b], X.r[i]], [X.r[i]], X.t[:, i, hf * 512:(hf + 1) * 512],
                  X.t[:, i, hf * 512:(hf + 1) * 512], ps[b][:, 0:512], ALU.add)

    import contextlib

    class Arena:
        def __init__(self):
            self.stack = contextlib.ExitStack()

        def tl(self, name, shape, dt):
            t = Tl.__new__(Tl)
            t.t = self.stack.enter_context(nc.sbuf_tensor(name, list(shape), dt))
            t.r = [Res(name)]
            t.R = t.r[0]
            return t

        def const(self, k):
            v = cst[k]
            t = self.tl(nm("ac_" + k), v.shape, BF16 if v.dtype == ml_dtypes.bfloat16 else F32)
            S.dma(t.t[:], cd[k], writes=[t.R])
            self.consts = getattr(self, "consts", []) + [t]
            return t

        def close(self):
            for t_ in getattr(self, "consts", []):
                for e_ in ("pe", "act", "dve", "pool"):
                    S._wait(e_, *t_.R.w)
            S.barrier()
            for k_ in ("pe", "act", "dve", "pool"):
                S._wait("sp", k_, S.cnt[k_])
            self.stack.close()

    uid = [0]

    def nm(s):
        uid[0] += 1
        return "%s_%d" % (s, uid[0])

    for s in range(NSEQ):
        for i0 in range(0, NT, 4):
            n = min(4, NT - i0)
            S.dma(X.t[:, i0:i0 + n, :], x_d[s, i0 * 128:(i0 + n) * 128, :].rearrange("(n p) d -> p n d", p=128),
                  writes=[X.r[i] for i in range(i0, i0 + n)])
        for l in range(NL):
            units = [("a", u, [OFF_QA + 128 * u, OFF_KA + 128 * u, OFF_VA + 128 * u, OFF_ZA + 128 * u]) for u in range(3)]
            units += [("b", u, [OFF_QB + 128 * u, OFF_KB + 128 * u, OFF_VB + 128 * u, OFF_ZB + 128 * u]) for u in range(2)]
            units += [("c", u, [OFF_QC + 128 * u, OFF_KC + 128 * u, OFF_VC + 128 * u, OFF_ZC + 128 * u]) for u in range(3)]
            load_cols(0, unit_cols(l, units[0][2]))
            A = Arena()
            ar = {"ss": A.tl(nm("ss"), [128, NT], F32), "junk": A.tl(nm("junk"), [128, D], BF16),
                  "xn": [A.tl(nm("xn"), [128, D], BF16) for _ in range(2)]}
            phase_norm(l, ar)
            A.close()
            A = Arena()
            ar = {"beta": A.tl(nm("beta"), [128, NT, 6], F32), "g": A.tl(nm("g"), [128, NT, 6], F32),
                  "t12": A.tl(nm("t12"), [128, NT, 6], F32),
                  "xb": [A.tl(nm("xb"), [128, QA + 3], F32) for _ in range(3)],
                  "qkvT": [A.tl(nm("qkvT"), [128, QA], F32) for _ in range(3)],
                  "gz": A.tl(nm("gz"), [128, 2 * KA, 64], F32),
                  "S": A.tl(nm("S"), [128, 64], F32)}
            _acc = A.tl(nm("acc"), [128, QA], F32)
            ar["acc"] = [_acc, _acc]
            ar["et"] = [A.tl(nm("et"), [128, QA], F32) for _ in range(2)]
            _yq = A.tl(nm("yq"), [128, QA], F32)
            ar["yq"] = [_yq, _yq]

            def two(n_, shp, dt=F32):
                t_ = A.tl(nm(n_), shp, dt)
                return [t_, t_]
            ar.update({"kv": two("kv", [128, 2, 2, 64]), "gcs": two("gcs", [128, 2]), "egc": two("egc", [128, 2]),
                       "sbeg": two("sbeg", [128, 2]), "gb": two("gb", [128, 3, 128]), "d0": two("d0", [128, 2, 128]),
                       "DTs": two("DTs", [128, 2, 128]), "DTi": two("DTi", [128, 2, 128]),
                       "egr": two("egr", [128, 128]), "kTp": two("kTpA", [128, 2, 128]), "M0p": two("M0p", [128, 2, 128]),
                       "qkT": two("qkT", [128, 2, 128]),
                       "Pm": two("Pm", [128, 2, 128]), "bv": two("bv", [128, 3, 2, 64]), "usb": two("usb", [128, 2, 64]),
                       "wTp": two("wTp", [128, 2, 128]), "qgTp": two("qgTp", [128, 2, 128]), "vnew": two("vnew", [128, 2, 64]),
                       "osb": two("osb", [128, 2, 64]), "osq": two("osq", [128, 2, 64]), "oss": two("oss", [128, 2]),
                       "ot": two("ot", [128, 2, 64]), "obf": two("obf", [128, 2, 64], BF16)})
            ar["DT"] = ar["d0"]
            for k_ in ("incl2", "nst2", "ident2"):
                ar[k_] = A.const(k_)
            _mn = [A.tl(nm("MN"), [128, 4, 128], F32) for _ in range(2)]
            ar["MN"] = [_mn, _mn]
            for n_ in ("kTp", "wTp", "qgTp"):
                E("pool", "memset", [], [ar[n_][0].R], ar[n_][0].t[:], 0.0)
            if "a" in phases:
                a_prep(l, ar)
            ui = 0
            for (kind, u, offs) in units[0:3]:
                load_cols((ui + 1) % 2, unit_cols(l, units[ui + 1][2]))
                if "a" in phases:
                    a_unit(l, u, ui % 2, ar)
                ui += 1
            A.close()
            for kind_, ulist in (("b", units[3:5]), ("c", units[5:8])):
                A = Arena()
                ar = {"qT": A.tl(nm("qT"), [128, T], BF16), "kTp": A.tl(nm("kTp"), [128, 2, T], BF16),
                      "vt": A.tl(nm("vt"), [128, NT, 128], BF16), "zs": A.tl(nm("zs"), [128, T], BF16)}
                _et = A.tl(nm("etb"), [128, QB], F32)
                ar["et"] = [_et, _et]
                if kind_ == "b":
                    ar["sp"] = [[A.tl(nm("sp"), [128, QB], F32) for _ in range(2)] for _ in range(2)]
                    ar["P"] = [[A.tl(nm("P"), [128, QB], F32) for _ in range(2)] for _ in range(2)]
                    ar["wts"] = [[A.tl(nm("wts"), [128, QB], BF16) for _ in range(2)] for _ in range(2)]
                    ar["Rb"] = [A.tl(nm("Rb"), [128, QB], F32) for _ in range(2)]
                else:
                    ar.update({"E": [A.tl(nm("E"), [128, QB], BF16) for _ in range(3)],
                               "EM": [A.tl(nm("EM"), [128, QB], BF16) for _ in range(3)],
                               "rec": A.tl(nm("rec"), [128, QB], F32),
                               "qk": [A.tl(nm("qk"), [128, 4, 64], F32) for _ in range(2)],
                               "qkb": [A.tl(nm("qkb"), [128, 4, 64], BF16) for _ in range(2)],
                               "sq": A.tl(nm("sq"), [128, 4, 64], F32), "s4": A.tl(nm("s4"), [128, 4], F32),
                               "rt": A.tl(nm("rt"), [128, 4, 4, 8], F32),
                               "tq": [A.tl(nm("tq"), [128, 256], BF16) for _ in range(2)]})
                    ar["mtab"] = A.const("mtab")
                ar["kTp"].r = [Res("kTp0"), Res("kTp1")]
                E("pool", "memset", [], ar["kTp"].r, ar["kTp"].t[:], 0.0)
                for (kind, u, offs) in ulist:
                    if ui + 1 < len(units):
                        load_cols((ui + 1) % 2, unit_cols(l, units[ui + 1][2]))
                    else:
                        load_cols((ui + 1) % 2, [(g, w_out_d[l][:, g * 128:(g + 1) * 128]) for g in range(4)])
                    if kind == "b" and "b" in phases:
                        b_unit(l, u, ui % 2, ar)
                    if kind == "c" and "c" in phases:
                        c_unit(l, u, ui % 2, ar)
                    ui += 1
                A.close()
            load_cols(1, [(g, w_out_d[l][:, 512 + g * 128:512 + (g + 1) * 128]) for g in range(4)])
            if dbg and l == NL - 1 and s == NSEQ - 1:
                A = Arena()
                dm = A.tl(nm("dm"), [128, T], F32)
                for c in range(KC):
                    E("dve", "tensor_copy", [mixT.r[c]], [dm.R], dm.t[:], mixT.t[:, c, :])
                    S.dma(dbg_d[:, c, :], dm.t[:], reads=[dm.R], writes=[Res("dbgout")])
                E("pool", "memset", [], [dm.R], dm.t[:, 0:1], 0.0)
                A.close()
            phase_out(l)
            S.barrier()
        outres = Res("out")
        for i0 in range(0, NT, 4):
            n = min(4, NT - i0)
            S.dma(out_d[s, i0 * 128:(i0 + n) * 128, :].rearrange("(n p) d -> p n d", p=128), X.t[:, i0:i0 + n, :],
                  reads=[X.r[i] for i in range(i0, i0 + n)], writes=[outres])
        S.finish([outres])
    return nc, cst, S


_CACHE = {}


def kernel(x, norm_w, w_in, conv_w, a_log, dt_bias, gdn_norm_w, q_norm_w, k_norm_w, w_out):
    n_cores = 8
    B, T, _ = x.shape
    NSEQ = B // n_cores
    NL = w_in.shape[0]
    key = (T, NSEQ, NL)
    if key not in _CACHE:
        _CACHE[key] = build_program(T, NSEQ, NL)
    nc, cst, _ = _CACHE[key]
    f = lambda a: np.ascontiguousarray(np.asarray(a, dtype=np.float32))
    shared = {"w_in": f(w_in), "w_out": f(w_out), "norm_w": f(norm_w), "conv_w": f(conv_w), "a_log": f(a_log),
              "dt_bias": f(dt_bias), "gdn_norm_w": f(gdn_norm_w), "q_norm_w": f(q_norm_w), "k_norm_w": f(k_norm_w)}
    for k, v in cst.items():
        shared["c_" + k] = v
    xs = f(x)
    in_maps = []
    for c in range(n_cores):
        m = dict(shared)
        m["x"] = xs[c * NSEQ:(c + 1) * NSEQ]
        in_maps.append(m)
    res = run_bass_kernel_spmd(nc, in_maps, core_ids=list(range(n_cores)))
    return np.concatenate([np.asarray(r["out"], dtype=np.float32) for r in res.results], axis=0)
```

```python
import numpy as np
import ml_dtypes
import concourse.bass as bass
import concourse.mybir as mybir
from concourse.bass_utils import run_bass_kernel_spmd

F32 = mybir.dt.float32
BF16 = mybir.dt.bfloat16
F32R = mybir.dt.float32r


def rr(ap):
    return ap.bitcast(F32R)
AF = mybir.ActivationFunctionType
ALU = mybir.AluOpType
AX = mybir.AxisListType

D = 1024
KC = 8
IN_COLS = 4108
OFF_QA, OFF_KA, OFF_VA, OFF_ZA, OFF_BETA, OFF_ALPHA = 0, 384, 768, 1152, 1536, 1542
OFF_QB, OFF_KB, OFF_VB, OFF_ZB = 1548, 1804, 2060, 2316
OFF_QC, OFF_KC, OFF_VC, OFF_ZC = 2572, 2956, 3340, 3724
EPS = 1e-6
ROPE_THETA = 500000.0


class Res:
    __slots__ = ("name", "w", "r", "excl")

    def __init__(self, name, excl=False):
        self.name = name
        self.w = None
        self.r = []
        self.excl = excl


class Sched:
    def __init__(self, nc, n_dma_sems=24):
        self.nc = nc
        self.eng = {"pe": nc.tensor, "act": nc.scalar, "dve": nc.vector, "pool": nc.gpsimd, "sp": nc.sync}
        self.sems = {}
        self.cnt = {}
        for e in ("pe", "act", "dve", "pool"):
            self.sems[e] = nc.alloc_semaphore("s_" + e)
            self.cnt[e] = 0
        self.dma_sems = []
        for i in range(n_dma_sems):
            k = "d%d" % i
            self.sems[k] = nc.alloc_semaphore("s_" + k)
            self.cnt[k] = 0
            self.dma_sems.append(k)
        self.dma_rr = 0
        self.waited = {}
        self.pe_pending = False
        self.nwaits = 0
        self.nins = 0

    def _wait(self, e, key, val):
        if val <= 0:
            return
        if e == "pe" and key == "pe":
            return
        if key == "pe" and val > self.cnt["pe"]:
            raise RuntimeError("wait on a PE group that has not been closed with inc=True")
        if self.waited.get((e, key), 0) >= val:
            return
        self.waited[(e, key)] = val
        self.eng[e].wait_ge(self.sems[key], val)
        self.nwaits += 1

    def _deps(self, e, reads, writes):
        for t in reads:
            if t.w is not None:
                self._wait(e, *t.w)
        for t in writes:
            if t.w is not None:
                self._wait(e, *t.w)
            for rr in t.r:
                self._wait(e, *rr)

    @staticmethod
    def _compact(lst):
        best = {}
        for k, v in lst:
            if best.get(k, 0) < v:
                best[k] = v
        return list(best.items())

    def _record(self, key, val, reads, writes):
        for t in reads:
            t.r.append((key, val))
            if len(t.r) > 16:
                t.r = self._compact(t.r)
        for t in writes:
            t.w = (key, val)
            t.r = []

    def op(self, e, ins_fn, reads=(), writes=(), inc=True):
        if any(t.excl for t in reads):
            writes = list(writes) + [t for t in reads if t.excl]
            reads = [t for t in reads if not t.excl]
        self._deps(e, reads, writes)
        ins = ins_fn()
        self.nins += 1
        if e == "pe" and not inc:
            self.pe_pending = True
            self._record("pe", self.cnt["pe"] + 1, reads, writes)
            return ins
        if e == "pe":
            self.pe_pending = False
        self.cnt[e] += 1
        ins.then_inc(self.sems[e], 1)
        self._record(e, self.cnt[e], reads, writes)
        return ins

    def dma(self, out, in_, reads=(), writes=(), q="sp", **kw):
        key = self.dma_sems[self.dma_rr % len(self.dma_sems)]
        self.dma_rr += 1
        self._wait(q, key, self.cnt[key])
        self._deps(q, reads, writes)
        ins = self.eng[q].dma_start(out=out, in_=in_, **kw)
        self.nins += 1
        self.cnt[key] += 16
        ins.then_inc(self.sems[key], 16)
        self._record(key, self.cnt[key], reads, writes)
        return ins

    def barrier(self):
        assert not self.pe_pending
        for e in ("pe", "act", "dve", "pool"):
            for k in ("pe", "act", "dve", "pool"):
                self._wait(e, k, self.cnt[k])

    def finish(self, out_res):
        for t in out_res:
            if t.w is not None:
                self._wait("sp", *t.w)


def _consts(T):
    NT = T // 128
    QB = min(512, T)
    KPB = QB // 128
    c = {}
    idx = np.arange(128)
    c["identf"] = np.eye(128, dtype=np.float32)
    c["identb"] = np.eye(128, dtype=np.float32).astype(ml_dtypes.bfloat16)
    c["tri_gt"] = (idx[:, None] > idx[None, :]).astype(np.float32)
    c["tri_le"] = (idx[:, None] <= idx[None, :]).astype(np.float32)
    c["onesf"] = np.ones((128, 128), np.float32)
    c["onesb"] = np.ones((128, 64), np.float32).astype(ml_dtypes.bfloat16)
    bo = np.zeros((128, 128), np.float32)
    bo[:64, :64] = 1.0
    bo[64:, 64:] = 1.0
    c["blockones"] = bo
    incl = (idx[:, None] <= idx[None, :]).astype(np.float32)
    nst = -(idx[:, None] < idx[None, :]).astype(np.float32)
    c["incl2"] = np.stack([incl, incl], 1)
    c["nst2"] = np.stack([nst, nst], 1)
    c["ident2"] = np.stack([np.eye(128, dtype=np.float32)] * 2, 1)
    OFF0 = (KPB - 1) * 128
    W = OFF0 + (T - QB) + QB
    dl = np.arange(W)[None, :] - idx[:, None] - OFF0
    m = ((dl >= 0) & (dl <= 128)).astype(np.float32)
    m += ((dl >= 0) & (dl <= 512) & (dl % 4 == 0)).astype(np.float32)
    m += ((dl >= 0) & (dl <= 2048) & (dl % 16 == 0)).astype(np.float32)
    c["mtab"] = m.astype(ml_dtypes.bfloat16)
    half = 8
    inv_freq = (ROPE_THETA ** (-np.arange(half, dtype=np.float32) / half)).astype(np.float32)
    pos = np.arange(T, dtype=np.float32)
    ang = (pos[:, None] * inv_freq[None, :]).astype(np.float32)
    c["cost"] = np.ascontiguousarray(np.cos(ang).astype(np.float32).reshape(NT, 128, half).transpose(1, 0, 2))
    c["sint"] = np.ascontiguousarray(np.sin(ang).astype(np.float32).reshape(NT, 128, half).transpose(1, 0, 2))
    return c


def build_program(T=2048, NSEQ=2, NL=2, dbg=False, phases="nabco"):
    NT = T // 128
    QB = min(512, T)
    NQB = T // QB
    KPB = QB // 128
    OFF0 = (KPB - 1) * 128
    QA = min(256, T)
    KA = QA // 128
    nc = bass.Bass("TRN2", target_bir_lowering=False)
    cst = _consts(T)

    def din(name, shape, dt=F32):
        return nc.dram_tensor(name, list(shape), dt, kind="ExternalInput").ap()

    x_d = din("x", [NSEQ, T, D])
    w_in_d = din("w_in", [NL, D, IN_COLS])
    w_out_d = din("w_out", [NL, D, D])
    norm_w_d = din("norm_w", [NL, D])
    conv_w_d = din("conv_w", [NL, 4, 1152])
    a_log_d = din("a_log", [NL, 6])
    dt_bias_d = din("dt_bias", [NL, 6])
    gdn_w_d = din("gdn_norm_w", [NL, 64])
    qn_w_d = din("q_norm_w", [NL, 64])
    kn_w_d = din("k_norm_w", [NL, 64])
    cd = {}
    for k, v in cst.items():
        cd[k] = din("c_" + k, v.shape, BF16 if v.dtype == ml_dtypes.bfloat16 else F32)
    out_d = nc.dram_tensor("out", [NSEQ, T, D], F32, kind="ExternalOutput").ap()
    if dbg:
        dbg_d = nc.dram_tensor("dbg_mix", [128, 8, T], F32, kind="ExternalOutput").ap()

    S = Sched(nc)
    eng = S.eng

    def E(e, name, R, W, *a, **k):
        return S.op(e, lambda: getattr(eng[e], name)(*a, **k), R, W)

    def MM(out, lhsT, rhs, R, W, start=True, stop=True, inc=True):
        return S.op("pe", lambda: nc.tensor.matmul(out, lhsT=lhsT, rhs=rhs, start=start, stop=stop), R, W, inc=inc)

    def TR(out, in_, ident, R, W, inc=True):
        return S.op("pe", lambda: nc.tensor.transpose(out, in_, ident), R, W, inc=inc)

    class Tl:
        def __init__(self, name, shape, dt, nres=1):
            self.t = nc.alloc_sbuf_tensor(name, list(shape), dt)
            self.r = [Res("%s_%d" % (name, i)) for i in range(nres)]
            self.R = self.r[0]

    X = Tl("X", [128, NT, D], F32, NT)
    hT = Tl("hT", [128, KC, T], BF16, NQB)
    mixT = Tl("mixT", [128, KC, T], BF16, KC)
    wbuf = [Tl("wbuf%d" % i, [128, KC, 512], BF16, 4) for i in range(2)]
    stage = [Tl("stage%d" % i, [128, KC, 128], F32) for i in range(2)]
    C = {}
    ARENA_CONSTS = ("mtab", "incl2", "nst2", "ident2", "tri_le", "tri_gt", "onesf", "blockones")
    for k, v in cst.items():
        if k in ARENA_CONSTS:
            continue
        C[k] = Tl("k_" + k, v.shape, BF16 if v.dtype == ml_dtypes.bfloat16 else F32)
    normw = Tl("normw", [128, NL, KC], F32)
    convw = Tl("convw", [128, NL, 9, 4], F32)
    nea = Tl("nea", [128, NL, 6], F32)
    dtb = Tl("dtb", [128, NL, 6], F32)
    gw = Tl("gw", [128, NL, 64], F32)
    wqk = Tl("wqk", [128, NL, 4, 64], F32)
    wba = Tl("wba", [128, KC, 12], BF16)
    wbas = Tl("wbas", [128, KC, 12], F32)
    ps = [nc.alloc_psum_tensor("ps%d" % i, [128, 512], F32) for i in range(8)]
    pr = [Res("ps%d" % i, excl=True) for i in range(8)]

    for k in C:
        S.dma(C[k].t[:], cd[k], writes=[C[k].R])
    for l in range(NL):
        S.dma(normw.t[:, l, :], norm_w_d[l].rearrange("(c p) -> p c", p=128), writes=[normw.R],
              allow_slow_non_contiguous=True)
        for g9 in range(9):
            S.dma(convw.t[:, l, g9, :], conv_w_d[l][:, g9 * 128:(g9 + 1) * 128].rearrange("k p -> p k"), writes=[convw.R],
                  allow_slow_non_contiguous=True)
        S.dma(nea.t[:, l, :], a_log_d[l].partition_broadcast(128), writes=[nea.R])
        S.dma(dtb.t[:, l, :], dt_bias_d[l].partition_broadcast(128), writes=[dtb.R])
        S.dma(gw.t[:, l, :], gdn_w_d[l].partition_broadcast(128), writes=[gw.R])
        for j in range(2):
            S.dma(wqk.t[:, l, j, :], qn_w_d[l].partition_broadcast(128), writes=[wqk.R])
            S.dma(wqk.t[:, l, 2 + j, :], kn_w_d[l].partition_broadcast(128), writes=[wqk.R])
    E("act", "activation", [nea.R], [nea.R], nea.t[:], nea.t[:], AF.Exp)
    E("dve", "tensor_scalar", [nea.R], [nea.R], nea.t[:], nea.t[:], -1.0, None, ALU.mult)
    E("dve", "tensor_scalar", [wqk.R], [wqk.R], wqk.t[:, :, 0:2, :], wqk.t[:, :, 0:2, :], 0.125, None, ALU.mult)

    zerof = Tl("zerof", [128, 256], F32)
    E("pool", "memset", [], [zerof.R], zerof.t[:], 0.0)
    for i_, k_ in enumerate(("tri_le", "tri_gt", "onesf", "blockones")):
        C[k_ + "_r"] = Tl("k_" + k_ + "_r", [128, 128], F32)
        S.dma(stage[i_ % 2].t[:, 0, :], cd[k_], writes=[stage[i_ % 2].R])
        E("dve", "tensor_copy", [stage[i_ % 2].R], [C[k_ + "_r"].R], rr(C[k_ + "_r"].t[:]), stage[i_ % 2].t[:, 0, :])
    C["onesf"] = C["onesf_r"]
    C["tri_le"] = C["tri_le_r"]
    bank_rr = [0]
    big_reg = nc.gpsimd.to_reg(1e30)

    def nbank(lo=0, hi=8):
        b = lo + bank_rr[0] % (hi - lo)
        bank_rr[0] += 1
        return b

    stg_rr = [0]

    def load_cols(slot, src2d_list):
        for g, src in src2d_list:
            st = stage[stg_rr[0] % 2]
            stg_rr[0] += 1
            S.dma(st.t[:], src.rearrange("(c p) n -> p c n", p=128), writes=[st.R])
            E("act", "copy", [st.R], [wbuf[slot].r[g]], wbuf[slot].t[:, :, g * 128:(g + 1) * 128], st.t[:])

    def unit_cols(l, offs):
        return [(g, w_in_d[l][:, o:o + 128]) for g, o in enumerate(offs)]

    def phase_norm(l, ar):
        ss, junk, xn = ar["ss"], ar["junk"], ar["xn"]
        for i in range(NT):
            E("act", "activation", [X.r[i]], [junk.R, ss.R], junk.t[:], X.t[:, i, :], AF.Square,
              accum_out=ss.t[:, i:i + 1])
        E("act", "activation", [ss.R], [ss.R], ss.t[:], ss.t[:], AF.Ln, scale=1.0 / D, bias=EPS)
        E("act", "activation", [ss.R], [ss.R], ss.t[:], ss.t[:], AF.Exp, scale=-0.5)
        for i in range(NT):
            xt = xn[i % 2]
            E("dve", "tensor_scalar", [X.r[i], ss.R], [xt.R], xt.t[:], X.t[:, i, :], ss.t[:, i:i + 1], None, ALU.mult)
            b = nbank(0, 2)
            pb = ps[b][:].bitcast(BF16)
            for c in range(KC):
                TR(pb[:, c * 128:(c + 1) * 128], xt.t[:, c * 128:(c + 1) * 128], C["identb"].t[:],
                   [xt.R, C["identb"].R], [pr[b]], inc=(c == KC - 1))
            E("dve", "tensor_tensor", [pr[b], normw.R], [hT.r[i // KPB]],
              hT.t[:, :, i * 128:(i + 1) * 128], pb.rearrange("p (c n) -> p c n", c=KC),
              normw.t[:, l, :].unsqueeze(2).broadcast_to([128, KC, 128]), ALU.mult)

    def proj_fm(slot, g, tb, b, w=None):
        w = w or QB
        for c in range(KC):
            MM(ps[b][:, 0:w], wbuf[slot].t[:, c, g * 128:(g + 1) * 128], hT.t[:, c, tb * w:(tb + 1) * w],
               [wbuf[slot].r[g], hT.r[(tb * w) // QB]], [pr[b]], start=(c == 0), stop=(c == KC - 1), inc=(c == KC - 1))

    def proj_tm(slot, c0, ncol, i, out_ap, b):
        gs = sorted(set([c0 // 128, (c0 + ncol - 1) // 128]))
        gs = list(range(gs[0], gs[-1] + 1))
        for c in range(KC):
            MM(out_ap, hT.t[:, c, i * 128:(i + 1) * 128], wbuf[slot].t[:, c, c0:c0 + ncol],
               [wbuf[slot].r[g] for g in gs] + [hT.r[i // KPB]], [pr[b]],
               start=(c == 0), stop=(c == KC - 1), inc=(c == KC - 1))

    def silu_from_psum(b, ncol, et, out_ap, out_res, extra_reads=()):
        E("act", "activation", [pr[b]], [et.R], et.t[:, 0:ncol], ps[b][:, 0:ncol], AF.Exp, scale=-1.0)
        E("act", "activation", [et.R], [et.R], et.t[:, 0:ncol], et.t[:, 0:ncol], AF.Ln, bias=1.0)
        E("act", "activation", [et.R], [et.R], et.t[:, 0:ncol], et.t[:, 0:ncol], AF.Exp, scale=-1.0)
        E("dve", "tensor_tensor", [pr[b], et.R] + list(extra_reads), [out_res], out_ap, ps[b][:, 0:ncol],
          et.t[:, 0:ncol], ALU.mult)

    def bc_inproj_common(slot, ar, kind):
        zs, et = ar["zs"], ar["et"]
        for tb in range(NQB):
            b = nbank(0, 2)
            proj_fm(slot, 3, tb, b)
            silu_from_psum(b, QB, et[tb % 2], zs.t[:, tb * QB:(tb + 1) * QB], zs.R)

    def b_unit(l, u, slot, ar):
        qT, kTp, vt, zs = ar["qT"], ar["kTp"], ar["vt"], ar["zs"]
        for tb in range(NQB):
            b = nbank(0, 2)
            proj_fm(slot, 0, tb, b)
            E("act", "activation", [pr[b]], [qT.R], qT.t[:, tb * QB:(tb + 1) * QB], ps[b][:, 0:QB], AF.Copy, scale=0.125)
            b = nbank(0, 2)
            proj_fm(slot, 1, tb, b)
            E("act", "copy", [pr[b]], [kTp.r[0]], kTp.t[0:64, 0, tb * QB:(tb + 1) * QB], ps[b][0:64, 0:QB])
            E("dve", "tensor_copy", [pr[b]], [kTp.r[1]], kTp.t[64:128, 1, tb * QB:(tb + 1) * QB], ps[b][64:128, 0:QB])
        for i in range(NT):
            b = nbank(0, 2)
            proj_tm(slot, 256, 128, i, ps[b][:, 0:128], b)
            E("act", "copy", [pr[b]], [vt.R], vt.t[:, i, :], ps[b][:, 0:128])
        bc_inproj_common(slot, ar, "b")
        sp, Pb, wts, Rb = ar["sp"], ar["P"], ar["wts"], ar["Rb"]
        ob = 7
        zb = [2, 3]
        cb = [4, 5]
        for qb in range(NQB):
            qbase = qb * QB
            nkt = (qb + 1) * KPB
            kts = list(range(nkt - 1, -1, -1))

            def stage_a(n, h):
                kt = kts[n]
                kbase = kt * 128
                diag = kbase + 127 >= qbase
                spb, Pt = sp[h][n % 2], Pb[h][n % 2]
                MM(ps[zb[h]][:, 0:QB], kTp.t[:, h, kbase:kbase + 128], qT.t[:, qbase:qbase + QB],
                   [kTp.r[h], qT.R], [pr[zb[h]]])
                E("act", "activation", [pr[zb[h]]], [spb.R], spb.t[:, 0:QB], ps[zb[h]][:, 0:QB], AF.Exp, scale=-1.0)
                E("act", "activation", [spb.R], [spb.R], spb.t[:, 0:QB], spb.t[:, 0:QB], AF.Ln, bias=1.0)
                E("dve", "tensor_tensor", [pr[zb[h]], spb.R], [Pt.R], rr(Pt.t[:, 0:QB]), ps[zb[h]][:, 0:QB],
                  spb.t[:, 0:QB], ALU.add)
                if diag:
                    E("pool", "affine_select", [Pt.R], [Pt.R], out=rr(Pt.t[:, 0:QB]), in_=rr(Pt.t[:, 0:QB]),
                      pattern=[[1, QB]], compare_op=ALU.is_gt, fill=0.0, base=qbase - kbase, channel_multiplier=-1)

            def stage_b1(n, h):
                first = (n == 0)
                Pt = Pb[h][n % 2]
                MM(ps[cb[h]][:, 0:QB], rr(C["tri_gt_r"].t[:]), rr(Pt.t[:, 0:QB]), [C["tri_gt_r"].R, Pt.R], [pr[cb[h]]],
                   start=True, stop=first, inc=first)
                if not first:
                    MM(ps[cb[h]][:, 0:QB], rr(C["onesf_r"].t[:]), rr(Rb[h].t[:, 0:QB]), [C["onesf_r"].R, Rb[h].R], [pr[cb[h]]],
                       start=False, stop=True)

            def stage_b2(n, h):
                kt = kts[n]
                kbase = kt * 128
                diag = kbase + 127 >= qbase
                first = (n == 0)
                last = (n == len(kts) - 1)
                spb, Pt, wt = sp[h][n % 2], Pb[h][n % 2], wts[h][n % 2]
                E("dve", "tensor_tensor", [pr[cb[h]], spb.R], [spb.R], spb.t[:, 0:QB], ps[cb[h]][:, 0:QB],
                  spb.t[:, 0:QB], ALU.add)
                if diag:
                    E("pool", "affine_select", [spb.R], [spb.R], out=spb.t[:, 0:QB], in_=spb.t[:, 0:QB],
                      pattern=[[1, QB]], compare_op=ALU.is_gt, fill=big_reg, base=qbase - kbase, channel_multiplier=-1)
                E("act", "activation", [spb.R], [wt.R], wt.t[:, 0:QB], spb.t[:, 0:QB], AF.Exp, scale=-1.0)
                if not last:
                    if first:
                        E("pool", "tensor_copy", [Pt.R], [Rb[h].R], rr(Rb[h].t[:, 0:QB]), rr(Pt.t[:, 0:QB]))
                    else:
                        E("pool", "tensor_tensor", [Pt.R, Rb[h].R], [Rb[h].R], out=rr(Rb[h].t[:, 0:QB]), in0=Rb[h].t[:, 0:QB],
                          in1=Pt.t[:, 0:QB], op=ALU.add)

            def stage_c(n, h):
                kt = kts[n]
                first = (n == 0)
                last = (n == len(kts) - 1)
                wt = wts[h][n % 2]
                MM(ps[ob][64 * h:64 * h + 64, 0:QB], vt.t[:, kt, 64 * h:64 * h + 64], wt.t[:, 0:QB],
                   [vt.R, wt.R], [pr[ob]], start=first, stop=last)

            for n in range(len(kts) + 2):
                if n < len(kts):
                    for h in range(2):
                        stage_a(n, h)
                if 1 <= n <= len(kts):
                    for h in range(2):
                        stage_b1(n - 1, h)
                    for h in range(2):
                        stage_b2(n - 1, h)
                if n >= 2:
                    for h in range(2):
                        stage_c(n - 2, h)
            E("dve", "tensor_tensor", [pr[ob], zs.R], [mixT.r[3 + u]], mixT.t[:, 3 + u, qbase:qbase + QB],
              ps[ob][:, 0:QB], zs.t[:, qbase:qbase + QB], ALU.mult)

    def c_unit(l, u, slot, ar):
        import os as _os
        CSTOP = int(_os.environ.get("CSTOP", "99"))
        qT, kTp, vt, zs = ar["qT"], ar["kTp"], ar["vt"], ar["zs"]
        qk, sq, s4, qkb = ar["qk"], ar["sq"], ar["s4"], ar["qkb"]
        rt = ar["rt"]
        for i in range(NT):
            b = nbank(0, 2)
            proj_tm(slot, 0, 384, i, ps[b][:, 0:384], b)
            E("act", "copy", [pr[b]], [vt.R], vt.t[:, i, :], ps[b][:, 256:384])
            q4 = qk[i % 2]
            E("act", "copy", [pr[b]], [q4.R], q4.t[:], ps[b][:, 0:256].rearrange("p (a d) -> p a d", a=4))
            if CSTOP <= 1:
                continue
            E("pool", "tensor_tensor", [q4.R], [sq.R], out=sq.t[:], in0=q4.t[:], in1=q4.t[:], op=ALU.mult)
            E("dve", "tensor_reduce", [sq.R], [s4.R], out=s4.t[:], in_=sq.t[:], axis=AX.X, op=ALU.add)
            E("act", "activation", [s4.R], [s4.R], s4.t[:], s4.t[:], AF.Ln, scale=1.0 / 64, bias=EPS)
            E("act", "activation", [s4.R], [s4.R], s4.t[:], s4.t[:], AF.Exp, scale=-0.5)
            E("dve", "tensor_tensor", [q4.R, s4.R], [q4.R], q4.t[:], q4.t[:],
              s4.t[:].unsqueeze(2).broadcast_to([128, 4, 64]), ALU.mult)
            E("pool", "tensor_tensor", [q4.R, wqk.R], [q4.R], out=q4.t[:], in0=q4.t[:], in1=wqk.t[:, l, :, :], op=ALU.mult)
            if CSTOP <= 2:
                continue
            qb_ = qkb[i % 2]
            E("act", "copy", [q4.R], [qb_.R], qb_.t[:], q4.t[:])
            cs = C["cost"].t[:, i, :].unsqueeze(1).broadcast_to([128, 4, 8])
            sn = C["sint"].t[:, i, :].unsqueeze(1).broadcast_to([128, 4, 8])
            x1 = q4.t[:, :, 0:8]
            x2 = q4.t[:, :, 8:16]
            E("dve", "tensor_tensor", [q4.R, C["cost"].R], [rt.R], rt.t[:, 0, :, :], x1, cs, ALU.mult)
            E("dve", "tensor_tensor", [q4.R, C["sint"].R], [rt.R], rt.t[:, 1, :, :], x2, sn, ALU.mult)
            E("dve", "tensor_tensor", [q4.R, C["cost"].R], [rt.R], rt.t[:, 2, :, :], x2, cs, ALU.mult)
            E("dve", "tensor_tensor", [q4.R, C["sint"].R], [rt.R], rt.t[:, 3, :, :], x1, sn, ALU.mult)
            E("dve", "tensor_tensor", [rt.R], [qb_.R], qb_.t[:, :, 0:8], rt.t[:, 0, :, :], rt.t[:, 1, :, :], ALU.subtract)
            E("dve", "tensor_tensor", [rt.R], [qb_.R], qb_.t[:, :, 8:16], rt.t[:, 2, :, :], rt.t[:, 3, :, :], ALU.add)
            if CSTOP <= 3:
                continue
            b2 = nbank(0, 2)
            pb = ps[b2][:].bitcast(BF16)
            qflat = qb_.t[:].rearrange("p a d -> p (a d)")
            TR(pb[:, 0:128], qflat[:, 0:128], C["identb"].t[:], [qb_.R, C["identb"].R], [pr[b2]], inc=False)
            TR(pb[:, 128:256], qflat[:, 128:256], C["identb"].t[:], [qb_.R, C["identb"].R], [pr[b2]])
            tq = ar["tq"][i % 2]
            E("dve", "tensor_copy", [pr[b2]], [tq.R], tq.t[:], pb[:, 0:256])
            E("act", "copy", [tq.R], [qT.R], qT.t[:, i * 128:(i + 1) * 128], tq.t[:, 0:128])
            E("pool", "tensor_copy", [tq.R], [kTp.r[0]], kTp.t[0:64, 0, i * 128:(i + 1) * 128], tq.t[0:64, 128:256])
            E("pool", "tensor_copy", [tq.R], [kTp.r[1]], kTp.t[64:128, 1, i * 128:(i + 1) * 128], tq.t[64:128, 128:256])
        if CSTOP <= 4:
            return
        bc_inproj_common(slot, ar, "c")
        if CSTOP <= 5:
            return
        Eb, EMb, rec = ar["E"], ar["EM"], ar["rec"]
        nb_, db_ = 6, 7
        sb = [2, 3, 4, 5]
        NE = len(Eb)
        for qb in range(NQB):
            qbase = qb * QB
            nkt = (qb + 1) * KPB
            pairs = [(kt, h) for kt in range(nkt - 1, -1, -1) for h in range(2)]

            def stage_a(n):
                kt, h = pairs[n]
                kbase = kt * 128
                off = qbase - kbase + OFF0
                b = sb[n % 4]
                Et, EMt = Eb[n % NE], EMb[n % NE]
                MM(ps[b][:, 0:QB], kTp.t[:, h, kbase:kbase + 128], qT.t[:, qbase:qbase + QB],
                   [kTp.r[h], qT.R], [pr[b]])
                E("act", "activation", [pr[b]], [Et.R], Et.t[:, 0:QB], ps[b][:, 0:QB], AF.Exp)
                E("pool" if n % 2 == 0 else "dve", "tensor_tensor", [Et.R, ar["mtab"].R], [EMt.R], out=EMt.t[:, 0:QB],
                  in0=Et.t[:, 0:QB], in1=ar["mtab"].t[:, off:off + QB], op=ALU.mult)

            def stage_b(n):
                kt, h = pairs[n]
                first = (kt == nkt - 1)
                last = (kt == 0)
                EMt = EMb[n % NE]
                MM(ps[nb_][64 * h:64 * h + 64, 0:QB], vt.t[:, kt, 64 * h:64 * h + 64], EMt.t[:, 0:QB],
                   [vt.R, EMt.R], [pr[nb_]], start=first, stop=last, inc=False)
                MM(ps[db_][64 * h:64 * h + 64, 0:QB], C["onesb"].t[:, 0:64], EMt.t[:, 0:QB],
                   [C["onesb"].R, EMt.R], [pr[db_]], start=first, stop=last)

            SK = 2
            for n in range(len(pairs) + SK):
                if n < len(pairs):
                    stage_a(n)
                if n >= SK:
                    stage_b(n - SK)
            E("dve", "reciprocal", [pr[db_]], [rec.R], rec.t[:, 0:QB], ps[db_][:, 0:QB])
            E("dve", "tensor_tensor", [pr[nb_], rec.R], [rec.R], rec.t[:, 0:QB], ps[nb_][:, 0:QB], rec.t[:, 0:QB], ALU.mult)
            E("pool", "tensor_tensor", [rec.R, zs.R], [mixT.r[5 + u]], out=mixT.t[:, 5 + u, qbase:qbase + QB],
              in0=rec.t[:, 0:QB], in1=zs.t[:, qbase:qbase + QB], op=ALU.mult)

    def a_prep(l, ar):
        beta, gg, t12 = ar["beta"], ar["g"], ar["t12"]
        S.dma(wbas.t[:], w_in_d[l][:, OFF_BETA:OFF_BETA + 12].rearrange("(c p) n -> p c n", p=128), writes=[wbas.R])
        E("pool", "tensor_copy", [wbas.R], [wba.R], wba.t[:], wbas.t[:])
        b = nbank(0, 2)
        for i in range(NT):
            for c in range(KC):
                MM(ps[b][:, i * 12:(i + 1) * 12], hT.t[:, c, i * 128:(i + 1) * 128], wba.t[:, c, :],
                   [wba.R, hT.r[i // KPB]], [pr[b]], start=(c == 0), stop=(c == KC - 1),
                   inc=(c == KC - 1 and i == NT - 1))
        pv = ps[b][:, 0:NT * 12].rearrange("p (n k) -> p n k", k=12)
        E("act", "activation", [pr[b]], [beta.R], beta.t[:], pv[:, :, 0:6], AF.Exp, scale=-1.0)
        E("act", "activation", [beta.R], [beta.R], beta.t[:], beta.t[:], AF.Ln, bias=1.0)
        E("act", "activation", [beta.R], [beta.R], beta.t[:], beta.t[:], AF.Exp, scale=-1.0)
        E("dve", "tensor_tensor", [pr[b], dtb.R], [t12.R], t12.t[:], pv[:, :, 6:12],
          dtb.t[:, l, :].unsqueeze(1).broadcast_to([128, NT, 6]), ALU.add)
        E("act", "activation", [t12.R], [t12.R], t12.t[:], t12.t[:], AF.Exp)
        E("act", "activation", [t12.R], [t12.R], t12.t[:], t12.t[:], AF.Ln, bias=1.0)
        E("dve", "tensor_tensor", [t12.R, nea.R], [gg.R], rr(gg.t[:]), t12.t[:],
          nea.t[:, l, :].unsqueeze(1).broadcast_to([128, NT, 6]), ALU.mult)

    def a_unit(l, u, slot, ar):
        beta, gg = ar["beta"], ar["g"]
        xb, acc, et, yq, qkvT = ar["xb"], ar["acc"], ar["et"], ar["yq"], ar["qkvT"]
        sqr = ar["sqr"]
        gz = ar["gz"]
        Ssb = ar["S"]
        identf, tri_le, onesf = C["identf"], C["tri_le"], C["onesf"]
        E("pool", "tensor_copy", [zerof.R], [Ssb.R], rr(Ssb.t[:]), zerof.t[:, 0:64])
        for g in range(3):
            E("pool", "memset", [], [xb[g].R], xb[g].t[:, 0:3], 0.0)
        for tb in range(T // QA):
            for g in range(3):
                b = nbank(0, 2)
                proj_fm(slot, g, tb, b, QA)
                gi = g * 3 + u
                xg = xb[g]
                E("act", "copy", [pr[b]], [xg.R], xg.t[:, 3:3 + QA], ps[b][:, 0:QA])
                a_ = acc[g % 2]
                E("dve", "tensor_scalar", [xg.R, convw.R], [a_.R], a_.t[:, 0:QA], xg.t[:, 0:QA],
                  convw.t[:, l, gi, 0:1], None, ALU.mult)
                for k in range(1, 4):
                    E("dve", "scalar_tensor_tensor", [xg.R, convw.R, a_.R], [a_.R], a_.t[:, 0:QA], xg.t[:, k:k + QA],
                      convw.t[:, l, gi, k:k + 1], a_.t[:, 0:QA], ALU.mult, ALU.add)
                E("pool", "tensor_copy", [xg.R], [xg.R], xg.t[:, 0:3], xg.t[:, QA:QA + 3])
                e_ = et[g % 2]
                E("act", "activation", [a_.R], [e_.R], e_.t[:, 0:QA], a_.t[:, 0:QA], AF.Exp, scale=-1.0)
                E("act", "activation", [e_.R], [e_.R], e_.t[:, 0:QA], e_.t[:, 0:QA], AF.Ln, bias=1.0)
                E("act", "activation", [e_.R], [e_.R], e_.t[:, 0:QA], e_.t[:, 0:QA], AF.Exp, scale=-1.0)
                if g == 2:
                    E("dve", "tensor_tensor", [a_.R, e_.R], [qkvT[2].R], qkvT[2].t[:, 0:QA], a_.t[:, 0:QA], e_.t[:, 0:QA], ALU.mult)
                else:
                    y = yq[g]
                    E("dve", "tensor_tensor", [a_.R, e_.R], [y.R], y.t[:, 0:QA], a_.t[:, 0:QA], e_.t[:, 0:QA], ALU.mult)
                    E("pool", "tensor_tensor", [y.R], [sqr.R], out=rr(sqr.t[:, 0:QA]), in0=y.t[:, 0:QA], in1=y.t[:, 0:QA], op=ALU.mult)
                    b2 = nbank(0, 2)
                    MM(ps[b2][:, 0:QA], rr(C["blockones_r"].t[:]), rr(sqr.t[:, 0:QA]), [C["blockones_r"].R, sqr.R], [pr[b2]])
                    E("act", "activation", [pr[b2]], [e_.R], e_.t[:, 0:QA], ps[b2][:, 0:QA], AF.Ln, bias=EPS)
                    E("act", "activation", [e_.R], [e_.R], e_.t[:, 0:QA], e_.t[:, 0:QA], AF.Exp, scale=-0.5)
                    E("dve", "scalar_tensor_tensor", [y.R, e_.R], [qkvT[g].R], rr(qkvT[g].t[:, 0:QA]), y.t[:, 0:QA],
                      0.125 if g == 0 else 1.0, e_.t[:, 0:QA], ALU.mult, ALU.mult)
            b = nbank(0, 2)
            for j in range(KA):
                i = tb * KA + j
                proj_tm(slot, 384, 128, i, ps[b][:, j * 128:(j + 1) * 128], b)
            silu_from_psum(b, QA, et[0], gz.t[:].rearrange("p a d -> p (a d)"), gz.R)
            E("pool", "tensor_tensor", [gz.R, gw.R], [gz.R], out=gz.t[:],
              in0=gz.t[:],
              in1=gw.t[:, l, :].unsqueeze(1).broadcast_to([128, KA * 2, 64]), op=ALU.mult)
            gens = [a_tile(l, u, tb * KA + j, j, ar) for j in range(KA)]
            front = [True] * KA
            while any(front):
                for j, g_ in enumerate(gens):
                    if front[j]:
                        if next(g_) == "BACK":
                            front[j] = False
            for g_ in gens:
                for _ in g_:
                    pass

    def a_tile(l, u, i, j, ar):
        beta, gg = ar["beta"], ar["g"]
        qkvT, gz, Ssb = ar["qkvT"], ar["gz"], ar["S"]
        identf, tri_le, onesf = C["identf"], C["tri_le"], C["onesf"]
        k = j % 2
        BM = [[2, 3, 4, 5, 6, 7], [5, 6, 7, 2, 3, 4]][k]
        kv, gcs, egc, sbeg, gb, d0, DT, DTs, DTi = (ar[n][k] for n in ("kv", "gcs", "egc", "sbeg", "gb", "d0", "DT", "DTs", "DTi"))
        egr, kTp, M0p, qkT, MN, Pm, bv, usb, wTp, qgTp, vnew = (ar[n][k] for n in ("egr", "kTp", "M0p", "qkT", "MN", "Pm", "bv", "usb", "wTp", "qgTp", "vnew"))
        osb, osq, oss, ot, obf = (ar[n][k] for n in ("osb", "osq", "oss", "ot", "obf"))
        cs = slice(j * 128, (j + 1) * 128)
        bi = beta.t[:, i, 2 * u:2 * u + 2]
        gi = gg.t[:, i, 2 * u:2 * u + 2]
        qT_, kT_, vT_ = qkvT[0], qkvT[1], qkvT[2]
        b = BM[0]
        TR(ps[b][:, 0:128], kT_.t[:, cs], identf.t[:], [kT_.R, identf.R], [pr[b]], inc=False)
        TR(ps[b][:, 128:256], vT_.t[:, cs], identf.t[:], [vT_.R, identf.R], [pr[b]], inc=False)
        MM(ps[b][:, 256:258], rr(C["tri_le_r"].t[:]), rr(gi), [C["tri_le_r"].R, gg.R], [pr[b]])
        yield
        E("act", "copy", [pr[b]], [kv.R], kv.t[:], ps[b][:, 0:256].rearrange("p (a h d) -> p a h d", a=2, h=2))
        E("dve", "tensor_copy", [pr[b]], [gcs.R], gcs.t[:], ps[b][:, 256:258])
        E("act", "activation", [gcs.R], [egc.R], egc.t[:], gcs.t[:], AF.Exp)
        E("pool", "tensor_tensor", [egc.R, beta.R], [sbeg.R], out=sbeg.t[:], in0=egc.t[:], in1=bi, op=ALU.mult)
        yield
        for h in range(2):
            E("dve", "tensor_scalar", [onesf.R, gg.R], [gb.R], rr(gb.t[:, h, :]), onesf.t[:], gi[:, h:h + 1], None, ALU.mult)
            E("dve", "tensor_scalar", [onesf.R, gg.R], [gb.R], rr(gb.t[:, 2, 64 * h:64 * h + 64]), onesf.t[:, 0:64], gi[:, h:h + 1], None, ALU.mult)
        b = BM[1]
        for h in range(3):
            MM(ps[b][:, h * 128:(h + 1) * 128], rr(gb.t[:, h, :]), rr(C["tri_le_r"].t[:]), [gb.R, C["tri_le_r"].R], [pr[b]], inc=(h == 2))
        yield
        for h in range(2):
            E("dve", "tensor_scalar", [pr[b], gcs.R], [d0.R], d0.t[:, h, :], ps[b][:, h * 128:(h + 1) * 128],
              gcs.t[:, h:h + 1], 0.0, ALU.subtract, ALU.min)
        E("act", "activation", [d0.R], [DT.R], DT.t[:], d0.t[:], AF.Exp)
        E("act", "activation", [pr[b]], [egr.R], egr.t[:], ps[b][:, 256:384], AF.Exp)
        E("pool", "tensor_tensor", [DT.R, ar["nst2"].R], [DTs.R], out=DTs.t[:], in0=DT.t[:], in1=ar["nst2"].t[:], op=ALU.mult)
        E("pool", "tensor_tensor", [DT.R, ar["incl2"].R], [DTi.R], out=DTi.t[:], in0=DT.t[:], in1=ar["incl2"].t[:], op=ALU.mult)
        yield
        E("pool", "tensor_copy", [kT_.R], [kTp.R], rr(kTp.t[0:64, 0, :]), rr(kT_.t[0:64, cs]))
        E("pool", "tensor_copy", [kT_.R], [kTp.R], rr(kTp.t[64:128, 1, :]), rr(kT_.t[64:128, cs]))
        b = BM[2]
        for h in range(2):
            MM(ps[b][:, h * 128:(h + 1) * 128], rr(kTp.t[:, h, :]), rr(kT_.t[:, cs]), [kTp.R, kT_.R], [pr[b]], inc=False)
        for h in range(2):
            MM(ps[b][:, 256 + h * 128:256 + (h + 1) * 128], rr(kTp.t[:, h, :]), rr(qT_.t[:, cs]), [kTp.R, qT_.R], [pr[b]], inc=(h == 1))
        yield
        E("dve", "tensor_tensor", [pr[b], DTs.R], [M0p.R], M0p.t[:], ps[b][:, 0:256].rearrange("p (h n) -> p h n", h=2), DTs.t[:], ALU.mult)
        E("dve", "tensor_tensor", [pr[b], DTi.R], [qkT.R], rr(qkT.t[:]), ps[b][:, 256:512].rearrange("p (h n) -> p h n", h=2), DTi.t[:], ALU.mult)
        yield
        b = BM[3]
        for h in range(2):
            TR(ps[b][:, h * 128:(h + 1) * 128], M0p.t[:, h, :], identf.t[:], [M0p.R, identf.R], [pr[b]], inc=(h == 1))
        mn = MN[0]
        yield
        for h in range(2):
            E("dve", "tensor_scalar", [pr[b], beta.R], [mn.R], rr(mn.t[:, 2 + h, :]), ps[b][:, h * 128:(h + 1) * 128], bi[:, h:h + 1], None, ALU.mult)
        b = BM[3]
        for h in range(2):
            TR(ps[b][:, h * 128:(h + 1) * 128], mn.t[:, 2 + h, :], identf.t[:], [mn.R, identf.R], [pr[b]], inc=(h == 1))
        yield
        E("act", "copy", [pr[b]], [mn.R], rr(mn.t[:, 0:2, :]), ps[b][:, 0:256].rearrange("p (h n) -> p h n", h=2))
        E("dve", "tensor_tensor", [pr[b], ar["ident2"].R], [Pm.R], rr(Pm.t[:]), ps[b][:, 0:256].rearrange("p (h n) -> p h n", h=2), ar["ident2"].t[:], ALU.add)
        yield
        for lv in range(1, 7):
            mo = MN[(lv - 1) % 2]
            mw = MN[lv % 2]
            b = BM[4]
            if lv < 6:
                for h in range(2):
                    MM(ps[b][:, h * 128:(h + 1) * 128], rr(mo.t[:, 2 + h, :]), rr(mo.t[:, h, :]), [mo.R], [pr[b]], inc=False)
            for h in range(2):
                MM(ps[b][:, 256 + h * 128:256 + (h + 1) * 128], rr(mo.t[:, h, :]), rr(mo.t[:, 2 + h, :]), [mo.R], [pr[b]], inc=(h == 1))
            yield
            if lv < 6:
                E("act", "copy", [pr[b]], [mw.R], rr(mw.t[:, 0:2, :]), ps[b][:, 0:256].rearrange("p (h n) -> p h n", h=2))
            E("dve", "tensor_copy", [pr[b]], [mw.R], rr(mw.t[:, 2:4, :]), ps[b][:, 256:512].rearrange("p (h n) -> p h n", h=2))
            yield
            b = BM[5]
            for h in range(2):
                MM(ps[b][:, h * 128:(h + 1) * 128], rr(mw.t[:, 2 + h, :]), rr(Pm.t[:, h, :]), [mw.R, Pm.R], [pr[b]], inc=(h == 1))
            yield
            E("dve", "tensor_tensor", [pr[b], Pm.R], [Pm.R], rr(Pm.t[:]), Pm.t[:], ps[b][:, 0:256].rearrange("p (h n) -> p h n", h=2), ALU.add)
            yield
        yield
        E("pool", "tensor_tensor", [kv.R, beta.R], [bv.R], out=rr(bv.t[:, 0, :, :]), in0=kv.t[:, 1, :, :],
          in1=bi.unsqueeze(2).broadcast_to([128, 2, 64]), op=ALU.mult)
        E("pool", "tensor_tensor", [kv.R, sbeg.R], [bv.R], out=rr(bv.t[:, 1, :, :]), in0=kv.t[:, 0, :, :],
          in1=sbeg.t[:].unsqueeze(2).broadcast_to([128, 2, 64]), op=ALU.mult)
        E("pool", "tensor_tensor", [kv.R, DT.R], [bv.R], out=rr(bv.t[:, 2, :, :]), in0=kv.t[:, 0, :, :],
          in1=DT.t[:, :, 127:128].broadcast_to([128, 2, 64]), op=ALU.mult)
        b = BM[1]
        for h in range(2):
            MM(ps[b][:, h * 64:(h + 1) * 64], rr(Pm.t[:, h, :]), rr(bv.t[:, 0, h, :]), [Pm.R, bv.R], [pr[b]], inc=False)
        for h in range(2):
            MM(ps[b][64 * h:64 * h + 64, 128:256], bv.t[:, 1, h, :], Pm.t[:, h, :], [Pm.R, bv.R], [pr[b]], inc=(h == 1))
        yield
        E("act", "copy", [pr[b]], [usb.R], usb.t[:], ps[b][:, 0:128].rearrange("p (h d) -> p h d", h=2))
        E("dve", "tensor_copy", [pr[b]], [wTp.R], rr(wTp.t[0:64, 0, :]), ps[b][0:64, 128:256])
        E("dve", "tensor_copy", [pr[b]], [wTp.R], rr(wTp.t[64:128, 1, :]), ps[b][64:128, 128:256])
        E("pool", "tensor_tensor", [qT_.R, egr.R], [qgTp.R], out=rr(qgTp.t[0:64, 0, :]), in0=qT_.t[0:64, cs], in1=egr.t[0:64, :], op=ALU.mult)
        E("pool", "tensor_tensor", [qT_.R, egr.R], [qgTp.R], out=rr(qgTp.t[64:128, 1, :]), in0=qT_.t[64:128, cs], in1=egr.t[64:128, :], op=ALU.mult)
        yield
        yield "BACK"
        b = BM[2]
        for h in range(2):
            MM(ps[b][:, h * 64:(h + 1) * 64], rr(wTp.t[:, h, :]), rr(Ssb.t[:]), [wTp.R, Ssb.R], [pr[b]], inc=(h == 1))
        E("dve", "tensor_tensor", [usb.R, pr[b]], [vnew.R], rr(vnew.t[:]), usb.t[:], ps[b][:, 0:128].rearrange("p (h d) -> p h d", h=2), ALU.subtract)
        bo = BM[4]
        for h in range(2):
            MM(ps[bo][:, h * 64:(h + 1) * 64], rr(qgTp.t[:, h, :]), rr(Ssb.t[:]), [qgTp.R, Ssb.R], [pr[bo]], start=True, stop=False, inc=False)
            MM(ps[bo][:, h * 64:(h + 1) * 64], rr(qkT.t[:, h, :]), rr(vnew.t[:, h, :]), [qkT.R, vnew.R], [pr[bo]], start=False, stop=True, inc=(h == 1))
        b = BM[5]
        for h in range(2):
            MM(ps[b][64 * h:64 * h + 64, 0:64], bv.t[:, 2, h, :], vnew.t[:, h, :], [bv.R, vnew.R], [pr[b]], inc=(h == 1))
        E("dve", "scalar_tensor_tensor", [Ssb.R, egr.R, pr[b]], [Ssb.R], rr(Ssb.t[:]), Ssb.t[:], egr.t[:, 127:128], ps[b][:, 0:64], ALU.mult, ALU.add)
        yield
        E("act", "copy", [pr[bo]], [osb.R], osb.t[:], ps[bo][:, 0:128].rearrange("p (h d) -> p h d", h=2))
        E("pool", "tensor_tensor", [osb.R], [osq.R], out=osq.t[:], in0=osb.t[:], in1=osb.t[:], op=ALU.mult)
        E("dve", "tensor_reduce", [osq.R], [oss.R], out=oss.t[:], in_=osq.t[:], axis=AX.X, op=ALU.add)
        E("act", "activation", [oss.R], [oss.R], oss.t[:], oss.t[:], AF.Ln, scale=1.0 / 64, bias=EPS)
        E("act", "activation", [oss.R], [oss.R], oss.t[:], oss.t[:], AF.Exp, scale=-0.5)
        E("dve", "tensor_tensor", [osb.R, oss.R], [ot.R], ot.t[:], osb.t[:], oss.t[:].unsqueeze(2).broadcast_to([128, 2, 64]), ALU.mult)
        E("dve", "tensor_tensor", [ot.R, gz.R], [obf.R], obf.t[:], ot.t[:], gz.t[:, 2 * j:2 * j + 2, :], ALU.mult)
        b = nbank(0, 2)
        pb = ps[b][:].bitcast(BF16)
        TR(pb[:, 0:128], obf.t[:].rearrange("p h d -> p (h d)"), C["identb"].t[:], [obf.R, C["identb"].R], [pr[b]])
        E("dve", "tensor_copy", [pr[b]], [mixT.r[u]], mixT.t[:, u, i * 128:(i + 1) * 128], pb[:, 0:128])

    def phase_out(l):
        for hf in range(2):
            for i in range(NT):
                b = nbank(0, 4)
                for c in range(KC):
                    MM(ps[b][:, 0:512], mixT.t[:, c, i * 128:(i + 1) * 128], wbuf[hf].t[:, c, :],
                       [mixT.r[c]] + wbuf[hf].r, [pr[b]], start=(c == 0), stop=(c == KC - 1), inc=(c == KC - 1))
                E("dve", "tensor_tensor", [pr[b], X.r[i]], [X.r[i]], X.t[:, i, hf * 512:(hf + 1) * 512],
                  X.t[:, i, hf * 512:(hf + 1) * 512], ps[b][:, 0:512], ALU.add)

    import contextlib

    class Arena:
        def __init__(self):
            self.stack = contextlib.ExitStack()

        def tl(self, name, shape, dt):
            t = Tl.__new__(Tl)
            t.t = self.stack.enter_context(nc.sbuf_tensor(name, list(shape), dt))
            t.r = [Res(name)]
            t.R = t.r[0]
            return t

        def const(self, k):
            v = cst[k]
            t = self.tl(nm("ac_" + k), v.shape, BF16 if v.dtype == ml_dtypes.bfloat16 else F32)
            S.dma(t.t[:], cd[k], writes=[t.R])
            self.consts = getattr(self, "consts", []) + [t]
            return t

        def close(self):
            for t_ in getattr(self, "consts", []):
                for e_ in ("pe", "act", "dve", "pool"):
                    S._wait(e_, *t_.R.w)
            S.barrier()
            for k_ in ("pe", "act", "dve", "pool"):
                S._wait("sp", k_, S.cnt[k_])
            self.stack.close()

    uid = [0]

    def nm(s):
        uid[0] += 1
        return "%s_%d" % (s, uid[0])

    for s in range(NSEQ):
        for i0 in range(0, NT, 4):
            n = min(4, NT - i0)
            S.dma(X.t[:, i0:i0 + n, :], x_d[s, i0 * 128:(i0 + n) * 128, :].rearrange("(n p) d -> p n d", p=128),
                  writes=[X.r[i] for i in range(i0, i0 + n)])
        for l in range(NL):
            units = [("a", u, [OFF_QA + 128 * u, OFF_KA + 128 * u, OFF_VA + 128 * u, OFF_ZA + 128 * u]) for u in range(3)]
            units += [("b", u, [OFF_QB + 128 * u, OFF_KB + 128 * u, OFF_VB + 128 * u, OFF_ZB + 128 * u]) for u in range(2)]
            units += [("c", u, [OFF_QC + 128 * u, OFF_KC + 128 * u, OFF_VC + 128 * u, OFF_ZC + 128 * u]) for u in range(3)]
            load_cols(0, unit_cols(l, units[0][2]))
            A = Arena()
            ar = {"ss": A.tl(nm("ss"), [128, NT], F32), "junk": A.tl(nm("junk"), [128, D], BF16),
                  "xn": [A.tl(nm("xn"), [128, D], BF16) for _ in range(2)]}
            phase_norm(l, ar)
            A.close()
            A = Arena()
            ar = {"beta": A.tl(nm("beta"), [128, NT, 6], F32), "g": A.tl(nm("g"), [128, NT, 6], F32),
                  "t12": A.tl(nm("t12"), [128, NT, 6], F32),
                  "xb": [A.tl(nm("xb"), [128, QA + 3], F32) for _ in range(3)],
                  "qkvT": [A.tl(nm("qkvT"), [128, QA], F32) for _ in range(3)],
                  "gz": A.tl(nm("gz"), [128, 2 * KA, 64], F32),
                  "S": A.tl(nm("S"), [128, 64], F32)}
            _acc = A.tl(nm("acc"), [128, QA], F32)
            ar["acc"] = [_acc, _acc]
            _et = A.tl(nm("et"), [128, QA], F32)
            ar["et"] = [_et, _et]
            _yq = A.tl(nm("yq"), [128, QA], F32)
            ar["yq"] = [_yq, _yq]

            mix_state = [3, 0]

            def mix_tl(name, shp, dt):
                n = int(np.prod(shp[1:])) * (2 if dt == F32 else 1)
                if mix_state[1] + n > T:
                    mix_state[0] += 1
                    mix_state[1] = 0
                if T < 2048 or mix_state[0] > 7 or n > T:
                    return None
                ap = mixT.t[:, mix_state[0], mix_state[1]:mix_state[1] + n]
                mix_state[1] += n
                if dt == F32:
                    ap = ap.bitcast(F32)
                if len(shp) == 3:
                    ap = ap.rearrange("p (a b) -> p a b", a=shp[1])
                elif len(shp) == 4:
                    ap = ap.rearrange("p (a b c) -> p a b c", a=shp[1], b=shp[2])
                t_ = Tl.__new__(Tl)
                t_.t = ap
                t_.r = [Res(name)]
                t_.R = t_.r[0]
                return t_

            def two(n_, shp, dt=F32):
                t0_ = A.tl(nm(n_), shp, dt)
                ok_alias = n_ in ("kv", "d0", "DTs", "DTi", "egr", "M0p", "usb", "osb", "osq", "ot")
                t1_ = mix_tl(nm(n_ + "m"), shp, dt) if ok_alias else None
                if t1_ is None:
                    t1_ = A.tl(nm(n_), shp, dt)
                return [t0_, t1_]
            ar.update({"kv": two("kv", [128, 2, 2, 64]), "gcs": two("gcs", [128, 2]), "egc": two("egc", [128, 2]),
                       "sbeg": two("sbeg", [128, 2]), "gb": two("gb", [128, 3, 128]), "d0": two("d0", [128, 2, 128]),
                       "DTs": two("DTs", [128, 2, 128]), "DTi": two("DTi", [128, 2, 128]),
                       "egr": two("egr", [128, 128]), "kTp": two("kTpA", [128, 2, 128]), "M0p": two("M0p", [128, 2, 128]),
                       "qkT": two("qkT", [128, 2, 128]),
                       "Pm": two("Pm", [128, 2, 128]), "bv": two("bv", [128, 3, 2, 64]), "usb": two("usb", [128, 2, 64]),
                       "wTp": two("wTp", [128, 2, 128]), "qgTp": two("qgTp", [128, 2, 128]), "vnew": two("vnew", [128, 2, 64]),
                       "osb": two("osb", [128, 2, 64]), "osq": two("osq", [128, 2, 64]), "oss": two("oss", [128, 2]),
                       "ot": two("ot", [128, 2, 64]), "obf": two("obf", [128, 2, 64], BF16)})
            ar["DT"] = ar["d0"]
            for k_ in ("incl2", "nst2", "ident2"):
                t_ = mix_tl(nm("mc_" + k_), [128, 2, 128], F32)
                if t_ is None:
                    ar[k_] = A.const(k_)
                else:
                    S.dma(t_.t[:], cd[k_], writes=[t_.R])
                    A.consts = getattr(A, "consts", []) + [t_]
                    ar[k_] = t_
            ar["MN"] = [[A.tl(nm("MN"), [128, 4, 128], F32) for _ in range(2)],
                        [A.tl(nm("MN"), [128, 4, 128], F32) for _ in range(2)]]
            for n_ in ("kTp", "wTp", "qgTp"):
                for k_ in range(2):
                    E("pool", "tensor_copy", [zerof.R], [ar[n_][k_].R], rr(ar[n_][k_].t[:]), zerof.t[:].rearrange("p (h n) -> p h n", h=2))
            ar["sqr"] = A.tl(nm("sqr"), [128, QA], F32)
            if "a" in phases:
                a_prep(l, ar)
            ui = 0
            for (kind, u, offs) in units[0:3]:
                load_cols((ui + 1) % 2, unit_cols(l, units[ui + 1][2]))
                if "a" in phases:
                    a_unit(l, u, ui % 2, ar)
                ui += 1
            A.close()
            for kind_, ulist in (("b", units[3:5]), ("c", units[5:8])):
                A = Arena()
                ar = {"qT": A.tl(nm("qT"), [128, T], BF16), "kTp": A.tl(nm("kTp"), [128, 2, T], BF16),
                      "vt": A.tl(nm("vt"), [128, NT, 128], BF16), "zs": A.tl(nm("zs"), [128, T], BF16)}
                _et = A.tl(nm("etb"), [128, QB], F32)
                ar["et"] = [_et, _et]
                if kind_ == "b":
                    ar["sp"] = [[A.tl(nm("sp"), [128, QB], F32) for _ in range(2)] for _ in range(2)]
                    ar["P"] = [[A.tl(nm("P"), [128, QB], F32) for _ in range(2)] for _ in range(2)]
                    ar["wts"] = [[A.tl(nm("wts"), [128, QB], BF16) for _ in range(2)] for _ in range(2)]
                    ar["Rb"] = [A.tl(nm("Rb"), [128, QB], F32) for _ in range(2)]
                else:
                    ar.update({"E": [A.tl(nm("E"), [128, QB], BF16) for _ in range(3)],
                               "EM": [A.tl(nm("EM"), [128, QB], BF16) for _ in range(3)],
                               "rec": A.tl(nm("rec"), [128, QB], F32),
                               "qk": [A.tl(nm("qk"), [128, 4, 64], F32) for _ in range(2)],
                               "qkb": [A.tl(nm("qkb"), [128, 4, 64], BF16) for _ in range(2)],
                               "sq": A.tl(nm("sq"), [128, 4, 64], F32), "s4": A.tl(nm("s4"), [128, 4], F32),
                               "rt": A.tl(nm("rt"), [128, 4, 4, 8], F32),
                               "tq": [A.tl(nm("tq"), [128, 256], BF16) for _ in range(2)]})
                    ar["mtab"] = A.const("mtab")
                ar["kTp"].r = [Res("kTp0"), Res("kTp1")]
                E("pool", "memset", [], ar["kTp"].r, ar["kTp"].t[:], 0.0)
                for (kind, u, offs) in ulist:
                    if ui + 1 < len(units):
                        load_cols((ui + 1) % 2, unit_cols(l, units[ui + 1][2]))
                    else:
                        load_cols((ui + 1) % 2, [(g, w_out_d[l][:, g * 128:(g + 1) * 128]) for g in range(4)])
                    if kind == "b" and "b" in phases:
                        b_unit(l, u, ui % 2, ar)
                    if kind == "c" and "c" in phases:
                        c_unit(l, u, ui % 2, ar)
                    ui += 1
                A.close()
            load_cols(1, [(g, w_out_d[l][:, 512 + g * 128:512 + (g + 1) * 128]) for g in range(4)])
            if dbg and l == NL - 1 and s == NSEQ - 1:
                A = Arena()
                dm = A.tl(nm("dm"), [128, T], F32)
                for c in range(KC):
                    E("dve", "tensor_copy", [mixT.r[c]], [dm.R], dm.t[:], mixT.t[:, c, :])
                    S.dma(dbg_d[:, c, :], dm.t[:], reads=[dm.R], writes=[Res("dbgout")])
                E("pool", "memset", [], [dm.R], dm.t[:, 0:1], 0.0)
                A.close()
            phase_out(l)
            S.barrier()
        outres = Res("out")
        for i0 in range(0, NT, 4):
            n = min(4, NT - i0)
            S.dma(out_d[s, i0 * 128:(i0 + n) * 128, :].rearrange("(n p) d -> p n d", p=128), X.t[:, i0:i0 + n, :],
                  reads=[X.r[i] for i in range(i0, i0 + n)], writes=[outres])
        S.finish([outres])
    return nc, cst, S


_CACHE = {}


def kernel(x, norm_w, w_in, conv_w, a_log, dt_bias, gdn_norm_w, q_norm_w, k_norm_w, w_out):
    n_cores = 8
    B, T, _ = x.shape
    NSEQ = B // n_cores
    NL = w_in.shape[0]
    key = (T, NSEQ, NL)
    if key not in _CACHE:
        _CACHE[key] = build_program(T, NSEQ, NL)
    nc, cst, _ = _CACHE[key]
    f = lambda a: np.ascontiguousarray(np.asarray(a, dtype=np.float32))
    shared = {"w_in": f(w_in), "w_out": f(w_out), "norm_w": f(norm_w), "conv_w": f(conv_w), "a_log": f(a_log),
              "dt_bias": f(dt_bias), "gdn_norm_w": f(gdn_norm_w), "q_norm_w": f(q_norm_w), "k_norm_w": f(k_norm_w)}
    for k, v in cst.items():
        shared["c_" + k] = v
    xs = f(x)
    in_maps = []
    for c in range(n_cores):
        m = dict(shared)
        m["x"] = xs[c * NSEQ:(c + 1) * NSEQ]
        in_maps.append(m)
    res = run_bass_kernel_spmd(nc, in_maps, core_ids=list(range(n_cores)))
    return np.concatenate([np.asarray(r["out"], dtype=np.float32) for r in res.results], axis=0)
```

```python
import numpy as np
import ml_dtypes
import concourse.bass as bass
import concourse.mybir as mybir
from concourse.bass_utils import run_bass_kernel_spmd

F32 = mybir.dt.float32
BF16 = mybir.dt.bfloat16
F32R = mybir.dt.float32r


def rr(ap):
    return ap.bitcast(F32R)
AF = mybir.ActivationFunctionType
ALU = mybir.AluOpType
AX = mybir.AxisListType

D = 1024
KC = 8
IN_COLS = 4108
OFF_QA, OFF_KA, OFF_VA, OFF_ZA, OFF_BETA, OFF_ALPHA = 0, 384, 768, 1152, 1536, 1542
OFF_QB, OFF_KB, OFF_VB, OFF_ZB = 1548, 1804, 2060, 2316
OFF_QC, OFF_KC, OFF_VC, OFF_ZC = 2572, 2956, 3340, 3724
EPS = 1e-6
ROPE_THETA = 500000.0


class Res:
    __slots__ = ("name", "w", "r", "excl")

    def __init__(self, name, excl=False):
        self.name = name
        self.w = None
        self.r = []
        self.excl = excl


class Sched:
    def __init__(self, nc, n_dma_sems=24):
        self.nc = nc
        self.eng = {"pe": nc.tensor, "act": nc.scalar, "dve": nc.vector, "pool": nc.gpsimd, "sp": nc.sync}
        self.sems = {}
        self.cnt = {}
        for e in ("pe", "act", "dve", "pool"):
            self.sems[e] = nc.alloc_semaphore("s_" + e)
            self.cnt[e] = 0
        self.dma_sems = []
        for i in range(n_dma_sems):
            k = "d%d" % i
            self.sems[k] = nc.alloc_semaphore("s_" + k)
            self.cnt[k] = 0
            self.dma_sems.append(k)
        self.dma_rr = 0
        self.waited = {}
        self.pe_pending = False
        self.nwaits = 0
        self.nins = 0

    def _wait(self, e, key, val):
        if val <= 0:
            return
        if e == "pe" and key == "pe":
            return
        if key == "pe" and val > self.cnt["pe"]:
            raise RuntimeError("wait on a PE group that has not been closed with inc=True")
        if self.waited.get((e, key), 0) >= val:
            return
        self.waited[(e, key)] = val
        self.eng[e].wait_ge(self.sems[key], val)
        self.nwaits += 1

    def _deps(self, e, reads, writes):
        for t in reads:
            if t.w is not None:
                self._wait(e, *t.w)
        for t in writes:
            if t.w is not None:
                self._wait(e, *t.w)
            for rr in t.r:
                self._wait(e, *rr)

    @staticmethod
    def _compact(lst):
        best = {}
        for k, v in lst:
            if best.get(k, 0) < v:
                best[k] = v
        return list(best.items())

    def _record(self, key, val, reads, writes):
        for t in reads:
            t.r.append((key, val))
            if len(t.r) > 16:
                t.r = self._compact(t.r)
        for t in writes:
            t.w = (key, val)
            t.r = []

    def op(self, e, ins_fn, reads=(), writes=(), inc=True):
        if any(t.excl for t in reads):
            writes = list(writes) + [t for t in reads if t.excl]
            reads = [t for t in reads if not t.excl]
        self._deps(e, reads, writes)
        ins = ins_fn()
        self.nins += 1
        if e == "pe" and not inc:
            self.pe_pending = True
            self._record("pe", self.cnt["pe"] + 1, reads, writes)
            return ins
        if e == "pe":
            self.pe_pending = False
        self.cnt[e] += 1
        ins.then_inc(self.sems[e], 1)
        self._record(e, self.cnt[e], reads, writes)
        return ins

    def dma(self, out, in_, reads=(), writes=(), q="sp", **kw):
        key = self.dma_sems[self.dma_rr % len(self.dma_sems)]
        self.dma_rr += 1
        self._wait(q, key, self.cnt[key])
        self._deps(q, reads, writes)
        ins = self.eng[q].dma_start(out=out, in_=in_, **kw)
        self.nins += 1
        self.cnt[key] += 16
        ins.then_inc(self.sems[key], 16)
        self._record(key, self.cnt[key], reads, writes)
        return ins

    def barrier(self):
        assert not self.pe_pending
        for e in ("pe", "act", "dve", "pool"):
            for k in ("pe", "act", "dve", "pool"):
                self._wait(e, k, self.cnt[k])

    def finish(self, out_res):
        for t in out_res:
            if t.w is not None:
                self._wait("sp", *t.w)


def _consts(T):
    NT = T // 128
    QB = min(512, T)
    KPB = QB // 128
    c = {}
    idx = np.arange(128)
    c["identf"] = np.eye(128, dtype=np.float32)
    c["identb"] = np.eye(128, dtype=np.float32).astype(ml_dtypes.bfloat16)
    c["tri_gt"] = (idx[:, None] > idx[None, :]).astype(np.float32)
    c["tri_le"] = (idx[:, None] <= idx[None, :]).astype(np.float32)
    c["onesf"] = np.ones((128, 128), np.float32)
    c["onesb"] = np.ones((128, 64), np.float32).astype(ml_dtypes.bfloat16)
    bo = np.zeros((128, 128), np.float32)
    bo[:64, :64] = 1.0
    bo[64:, 64:] = 1.0
    c["blockones"] = bo
    incl = (idx[:, None] <= idx[None, :]).astype(np.float32)
    nst = -(idx[:, None] < idx[None, :]).astype(np.float32)
    c["incl2"] = np.stack([incl, incl], 1)
    c["nst2"] = np.stack([nst, nst], 1)
    c["ident2"] = np.stack([np.eye(128, dtype=np.float32)] * 2, 1)
    OFF0 = (KPB - 1) * 128
    W = OFF0 + (T - QB) + QB
    dl = np.arange(W)[None, :] - idx[:, None] - OFF0
    m = ((dl >= 0) & (dl <= 128)).astype(np.float32)
    m += ((dl >= 0) & (dl <= 512) & (dl % 4 == 0)).astype(np.float32)
    m += ((dl >= 0) & (dl <= 2048) & (dl % 16 == 0)).astype(np.float32)
    c["mtab"] = m.astype(ml_dtypes.bfloat16)
    half = 8
    inv_freq = (ROPE_THETA ** (-np.arange(half, dtype=np.float32) / half)).astype(np.float32)
    pos = np.arange(T, dtype=np.float32)
    ang = (pos[:, None] * inv_freq[None, :]).astype(np.float32)
    c["cost"] = np.ascontiguousarray(np.cos(ang).astype(np.float32).reshape(NT, 128, half).transpose(1, 0, 2))
    c["sint"] = np.ascontiguousarray(np.sin(ang).astype(np.float32).reshape(NT, 128, half).transpose(1, 0, 2))
    return c


def build_program(T=2048, NSEQ=2, NL=2, dbg=False, phases="nabco"):
    NT = T // 128
    QB = min(512, T)
    NQB = T // QB
    KPB = QB // 128
    OFF0 = (KPB - 1) * 128
    QA = min(256, T)
    KA = QA // 128
    nc = bass.Bass("TRN2", target_bir_lowering=False)
    cst = _consts(T)

    def din(name, shape, dt=F32):
        return nc.dram_tensor(name, list(shape), dt, kind="ExternalInput").ap()

    x_d = din("x", [NSEQ, T, D])
    w_in_d = din("w_in", [NL, D, IN_COLS])
    w_out_d = din("w_out", [NL, D, D])
    norm_w_d = din("norm_w", [NL, D])
    conv_w_d = din("conv_w", [NL, 4, 1152])
    a_log_d = din("a_log", [NL, 6])
    dt_bias_d = din("dt_bias", [NL, 6])
    gdn_w_d = din("gdn_norm_w", [NL, 64])
    qn_w_d = din("q_norm_w", [NL, 64])
    kn_w_d = din("k_norm_w", [NL, 64])
    cd = {}
    for k, v in cst.items():
        cd[k] = din("c_" + k, v.shape, BF16 if v.dtype == ml_dtypes.bfloat16 else F32)
    out_d = nc.dram_tensor("out", [NSEQ, T, D], F32, kind="ExternalOutput").ap()
    if dbg:
        dbg_d = nc.dram_tensor("dbg_mix", [128, 8, T], F32, kind="ExternalOutput").ap()

    S = Sched(nc)
    eng = S.eng

    def E(e, name, R, W, *a, **k):
        return S.op(e, lambda: getattr(eng[e], name)(*a, **k), R, W)

    def MM(out, lhsT, rhs, R, W, start=True, stop=True, inc=True):
        return S.op("pe", lambda: nc.tensor.matmul(out, lhsT=lhsT, rhs=rhs, start=start, stop=stop), R, W, inc=inc)

    def TR(out, in_, ident, R, W, inc=True):
        return S.op("pe", lambda: nc.tensor.transpose(out, in_, ident), R, W, inc=inc)

    class Tl:
        def __init__(self, name, shape, dt, nres=1):
            self.t = nc.alloc_sbuf_tensor(name, list(shape), dt)
            self.r = [Res("%s_%d" % (name, i)) for i in range(nres)]
            self.R = self.r[0]

    X = Tl("X", [128, NT, D], F32, NT)
    hT = Tl("hT", [128, KC, T], BF16, NQB)
    mixT = Tl("mixT", [128, KC, T], BF16, KC)
    wbuf = [Tl("wbuf%d" % i, [128, KC, 512], BF16, 4) for i in range(2)]
    stage = [Tl("stage%d" % i, [128, KC, 128], F32) for i in range(2)]
    C = {}
    ARENA_CONSTS = ("mtab", "incl2", "nst2", "ident2", "tri_le", "tri_gt", "onesf", "blockones")
    for k, v in cst.items():
        if k in ARENA_CONSTS:
            continue
        C[k] = Tl("k_" + k, v.shape, BF16 if v.dtype == ml_dtypes.bfloat16 else F32)
    normw = Tl("normw", [128, NL, KC], F32)
    convw = Tl("convw", [128, NL, 9, 4], F32)
    nea = Tl("nea", [128, NL, 6], F32)
    dtb = Tl("dtb", [128, NL, 6], F32)
    gw = Tl("gw", [128, NL, 64], F32)
    wqk = Tl("wqk", [128, NL, 4, 64], F32)
    wba = Tl("wba", [128, KC, 12], BF16)
    wbas = Tl("wbas", [128, KC, 12], F32)
    ps = [nc.alloc_psum_tensor("ps%d" % i, [128, 512], F32) for i in range(8)]
    pr = [Res("ps%d" % i, excl=True) for i in range(8)]

    for k in C:
        S.dma(C[k].t[:], cd[k], writes=[C[k].R])
    for l in range(NL):
        S.dma(normw.t[:, l, :], norm_w_d[l].rearrange("(c p) -> p c", p=128), writes=[normw.R],
              allow_slow_non_contiguous=True)
        for g9 in range(9):
            S.dma(convw.t[:, l, g9, :], conv_w_d[l][:, g9 * 128:(g9 + 1) * 128].rearrange("k p -> p k"), writes=[convw.R],
                  allow_slow_non_contiguous=True)
        S.dma(nea.t[:, l, :], a_log_d[l].partition_broadcast(128), writes=[nea.R])
        S.dma(dtb.t[:, l, :], dt_bias_d[l].partition_broadcast(128), writes=[dtb.R])
        S.dma(gw.t[:, l, :], gdn_w_d[l].partition_broadcast(128), writes=[gw.R])
        for j in range(2):
            S.dma(wqk.t[:, l, j, :], qn_w_d[l].partition_broadcast(128), writes=[wqk.R])
            S.dma(wqk.t[:, l, 2 + j, :], kn_w_d[l].partition_broadcast(128), writes=[wqk.R])
    E("act", "activation", [nea.R], [nea.R], nea.t[:], nea.t[:], AF.Exp)
    E("dve", "tensor_scalar", [nea.R], [nea.R], nea.t[:], nea.t[:], -1.0, None, ALU.mult)
    E("dve", "tensor_scalar", [wqk.R], [wqk.R], wqk.t[:, :, 0:2, :], wqk.t[:, :, 0:2, :], 0.125, None, ALU.mult)

    zerof = Tl("zerof", [128, 256], F32)
    E("pool", "memset", [], [zerof.R], zerof.t[:], 0.0)
    for i_, k_ in enumerate(("tri_le", "tri_gt", "onesf", "blockones")):
        C[k_ + "_r"] = Tl("k_" + k_ + "_r", [128, 128], F32)
        S.dma(stage[i_ % 2].t[:, 0, :], cd[k_], writes=[stage[i_ % 2].R])
        E("dve", "tensor_copy", [stage[i_ % 2].R], [C[k_ + "_r"].R], rr(C[k_ + "_r"].t[:]), stage[i_ % 2].t[:, 0, :])
    C["onesf"] = C["onesf_r"]
    C["tri_le"] = C["tri_le_r"]
    bank_rr = [0]
    big_reg = nc.gpsimd.to_reg(1e30)

    def nbank(lo=0, hi=8):
        b = lo + bank_rr[0] % (hi - lo)
        bank_rr[0] += 1
        return b

    stg_rr = [0]

    def load_cols(slot, src2d_list):
        for g, src in src2d_list:
            st = stage[stg_rr[0] % 2]
            stg_rr[0] += 1
            S.dma(st.t[:], src.rearrange("(c p) n -> p c n", p=128), writes=[st.R])
            E("act", "copy", [st.R], [wbuf[slot].r[g]], wbuf[slot].t[:, :, g * 128:(g + 1) * 128], st.t[:])

    def unit_cols(l, offs):
        return [(g, w_in_d[l][:, o:o + 128]) for g, o in enumerate(offs)]

    def phase_norm(l, ar):
        ss, junk, xn = ar["ss"], ar["junk"], ar["xn"]
        for i in range(NT):
            E("act", "activation", [X.r[i]], [junk.R, ss.R], junk.t[:], X.t[:, i, :], AF.Square,
              accum_out=ss.t[:, i:i + 1])
        E("act", "activation", [ss.R], [ss.R], ss.t[:], ss.t[:], AF.Ln, scale=1.0 / D, bias=EPS)
        E("act", "activation", [ss.R], [ss.R], ss.t[:], ss.t[:], AF.Exp, scale=-0.5)
        for i in range(NT):
            xt = xn[i % 2]
            E("dve", "tensor_scalar", [X.r[i], ss.R], [xt.R], xt.t[:], X.t[:, i, :], ss.t[:, i:i + 1], None, ALU.mult)
            b = nbank(0, 2)
            pb = ps[b][:].bitcast(BF16)
            for c in range(KC):
                TR(pb[:, c * 128:(c + 1) * 128], xt.t[:, c * 128:(c + 1) * 128], C["identb"].t[:],
                   [xt.R, C["identb"].R], [pr[b]], inc=(c == KC - 1))
            E("dve", "tensor_tensor", [pr[b], normw.R], [hT.r[i // KPB]],
              hT.t[:, :, i * 128:(i + 1) * 128], pb.rearrange("p (c n) -> p c n", c=KC),
              normw.t[:, l, :].unsqueeze(2).broadcast_to([128, KC, 128]), ALU.mult)

    def proj_fm(slot, g, tb, b, w=None):
        w = w or QB
        for c in range(KC):
            MM(ps[b][:, 0:w], wbuf[slot].t[:, c, g * 128:(g + 1) * 128], hT.t[:, c, tb * w:(tb + 1) * w],
               [wbuf[slot].r[g], hT.r[(tb * w) // QB]], [pr[b]], start=(c == 0), stop=(c == KC - 1), inc=(c == KC - 1))

    def proj_tm(slot, c0, ncol, i, out_ap, b):
        gs = sorted(set([c0 // 128, (c0 + ncol - 1) // 128]))
        gs = list(range(gs[0], gs[-1] + 1))
        for c in range(KC):
            MM(out_ap, hT.t[:, c, i * 128:(i + 1) * 128], wbuf[slot].t[:, c, c0:c0 + ncol],
               [wbuf[slot].r[g] for g in gs] + [hT.r[i // KPB]], [pr[b]],
               start=(c == 0), stop=(c == KC - 1), inc=(c == KC - 1))

    def silu_from_psum(b, ncol, et, out_ap, out_res, extra_reads=()):
        E("act", "activation", [pr[b]], [et.R], et.t[:, 0:ncol], ps[b][:, 0:ncol], AF.Exp, scale=-1.0)
        E("act", "activation", [et.R], [et.R], et.t[:, 0:ncol], et.t[:, 0:ncol], AF.Ln, bias=1.0)
        E("act", "activation", [et.R], [et.R], et.t[:, 0:ncol], et.t[:, 0:ncol], AF.Exp, scale=-1.0)
        E("dve", "tensor_tensor", [pr[b], et.R] + list(extra_reads), [out_res], out_ap, ps[b][:, 0:ncol],
          et.t[:, 0:ncol], ALU.mult)

    def bc_inproj_common(slot, ar, kind):
        zs, et = ar["zs"], ar["et"]
        for tb in range(NQB):
            b = nbank(0, 2)
            proj_fm(slot, 3, tb, b)
            silu_from_psum(b, QB, et[tb % 2], zs.t[:, tb * QB:(tb + 1) * QB], zs.R)

    def b_unit(l, u, slot, ar):
        qT, kTp, vt, zs = ar["qT"], ar["kTp"], ar["vt"], ar["zs"]
        for tb in range(NQB):
            b = nbank(0, 2)
            proj_fm(slot, 0, tb, b)
            E("act", "activation", [pr[b]], [qT.R], qT.t[:, tb * QB:(tb + 1) * QB], ps[b][:, 0:QB], AF.Copy, scale=0.125)
            b = nbank(0, 2)
            proj_fm(slot, 1, tb, b)
            E("act", "copy", [pr[b]], [kTp.r[0]], kTp.t[0:64, 0, tb * QB:(tb + 1) * QB], ps[b][0:64, 0:QB])
            E("dve", "tensor_copy", [pr[b]], [kTp.r[1]], kTp.t[64:128, 1, tb * QB:(tb + 1) * QB], ps[b][64:128, 0:QB])
        for i in range(NT):
            b = nbank(0, 2)
            proj_tm(slot, 256, 128, i, ps[b][:, 0:128], b)
            E("act", "copy", [pr[b]], [vt.R], vt.t[:, i, :], ps[b][:, 0:128])
        bc_inproj_common(slot, ar, "b")
        sp, Pb, wts, Rb = ar["sp"], ar["P"], ar["wts"], ar["Rb"]
        ob = 7
        zb = [2, 3]
        cb = [4, 5]
        for qb in range(NQB):
            qbase = qb * QB
            nkt = (qb + 1) * KPB
            kts = list(range(nkt - 1, -1, -1))

            def stage_a(n, h):
                kt = kts[n]
                kbase = kt * 128
                diag = kbase + 127 >= qbase
                spb, Pt = sp[h][n % 2], Pb[h][n % 2]
                MM(ps[zb[h]][:, 0:QB], kTp.t[:, h, kbase:kbase + 128], qT.t[:, qbase:qbase + QB],
                   [kTp.r[h], qT.R], [pr[zb[h]]])
                E("act", "activation", [pr[zb[h]]], [spb.R], spb.t[:, 0:QB], ps[zb[h]][:, 0:QB], AF.Exp, scale=-1.0)
                E("act", "activation", [spb.R], [spb.R], spb.t[:, 0:QB], spb.t[:, 0:QB], AF.Ln, bias=1.0)
                E("dve", "tensor_tensor", [pr[zb[h]], spb.R], [Pt.R], rr(Pt.t[:, 0:QB]), ps[zb[h]][:, 0:QB],
                  spb.t[:, 0:QB], ALU.add)
                if diag:
                    E("pool", "affine_select", [Pt.R], [Pt.R], out=rr(Pt.t[:, 0:QB]), in_=rr(Pt.t[:, 0:QB]),
                      pattern=[[1, QB]], compare_op=ALU.is_gt, fill=0.0, base=qbase - kbase, channel_multiplier=-1)

            def stage_b1(n, h):
                first = (n == 0)
                Pt = Pb[h][n % 2]
                MM(ps[cb[h]][:, 0:QB], rr(C["tri_gt_r"].t[:]), rr(Pt.t[:, 0:QB]), [C["tri_gt_r"].R, Pt.R], [pr[cb[h]]],
                   start=True, stop=first, inc=first)
                if not first:
                    MM(ps[cb[h]][:, 0:QB], rr(C["onesf_r"].t[:]), rr(Rb[h].t[:, 0:QB]), [C["onesf_r"].R, Rb[h].R], [pr[cb[h]]],
                       start=False, stop=True)

            def stage_b2(n, h):
                kt = kts[n]
                kbase = kt * 128
                diag = kbase + 127 >= qbase
                first = (n == 0)
                last = (n == len(kts) - 1)
                spb, Pt, wt = sp[h][n % 2], Pb[h][n % 2], wts[h][n % 2]
                E("dve", "tensor_tensor", [pr[cb[h]], spb.R], [spb.R], spb.t[:, 0:QB], ps[cb[h]][:, 0:QB],
                  spb.t[:, 0:QB], ALU.add)
                if diag:
                    E("pool", "affine_select", [spb.R], [spb.R], out=spb.t[:, 0:QB], in_=spb.t[:, 0:QB],
                      pattern=[[1, QB]], compare_op=ALU.is_gt, fill=big_reg, base=qbase - kbase, channel_multiplier=-1)
                E("act", "activation", [spb.R], [wt.R], wt.t[:, 0:QB], spb.t[:, 0:QB], AF.Exp, scale=-1.0)
                if not last:
                    if first:
                        E("pool", "tensor_copy", [Pt.R], [Rb[h].R], rr(Rb[h].t[:, 0:QB]), rr(Pt.t[:, 0:QB]))
                    else:
                        E("pool", "tensor_tensor", [Pt.R, Rb[h].R], [Rb[h].R], out=rr(Rb[h].t[:, 0:QB]), in0=Rb[h].t[:, 0:QB],
                          in1=Pt.t[:, 0:QB], op=ALU.add)

            def stage_c(n, h):
                kt = kts[n]
                first = (n == 0)
                last = (n == len(kts) - 1)
                wt = wts[h][n % 2]
                MM(ps[ob][64 * h:64 * h + 64, 0:QB], vt.t[:, kt, 64 * h:64 * h + 64], wt.t[:, 0:QB],
                   [vt.R, wt.R], [pr[ob]], start=first, stop=last)

            for n in range(len(kts) + 2):
                if n < len(kts):
                    for h in range(2):
                        stage_a(n, h)
                if 1 <= n <= len(kts):
                    for h in range(2):
                        stage_b1(n - 1, h)
                    for h in range(2):
                        stage_b2(n - 1, h)
                if n >= 2:
                    for h in range(2):
                        stage_c(n - 2, h)
            E("dve", "tensor_tensor", [pr[ob], zs.R], [mixT.r[3 + u]], mixT.t[:, 3 + u, qbase:qbase + QB],
              ps[ob][:, 0:QB], zs.t[:, qbase:qbase + QB], ALU.mult)

    def c_unit(l, u, slot, ar):
        import os as _os
        CSTOP = int(_os.environ.get("CSTOP", "99"))
        qT, kTp, vt, zs = ar["qT"], ar["kTp"], ar["vt"], ar["zs"]
        qk, sq, s4, qkb = ar["qk"], ar["sq"], ar["s4"], ar["qkb"]
        rt = ar["rt"]
        def c_tile(i):
            p_ = i % 2
            b = p_
            proj_tm(slot, 0, 384, i, ps[b][:, 0:384], b)
            E("act", "copy", [pr[b]], [vt.R], vt.t[:, i, :], ps[b][:, 256:384])
            q4 = qk[i % 2]
            E("act", "copy", [pr[b]], [q4.R], q4.t[:], ps[b][:, 0:256].rearrange("p (a d) -> p a d", a=4))
            yield
            E("pool", "tensor_tensor", [q4.R], [sq[p_].R], out=sq[p_].t[:], in0=q4.t[:], in1=q4.t[:], op=ALU.mult)
            E("dve", "tensor_reduce", [sq[p_].R], [s4[p_].R], out=s4[p_].t[:], in_=sq[p_].t[:], axis=AX.X, op=ALU.add)
            yield
            E("act", "activation", [s4[p_].R], [s4[p_].R], s4[p_].t[:], s4[p_].t[:], AF.Ln, scale=1.0 / 64, bias=EPS)
            E("act", "activation", [s4[p_].R], [s4[p_].R], s4[p_].t[:], s4[p_].t[:], AF.Exp, scale=-0.5)
            E("dve", "tensor_tensor", [q4.R, s4[p_].R], [q4.R], q4.t[:], q4.t[:],
              s4[p_].t[:].unsqueeze(2).broadcast_to([128, 4, 64]), ALU.mult)
            yield
            E("pool", "tensor_tensor", [q4.R, wqk.R], [q4.R], out=q4.t[:], in0=q4.t[:], in1=wqk.t[:, l, :, :], op=ALU.mult)
            yield
            qb_ = qkb[i % 2]
            E("act", "copy", [q4.R], [qb_.R], qb_.t[:], q4.t[:])
            cs = C["cost"].t[:, i, :].unsqueeze(1).broadcast_to([128, 4, 8])
            sn = C["sint"].t[:, i, :].unsqueeze(1).broadcast_to([128, 4, 8])
            x1 = q4.t[:, :, 0:8]
            x2 = q4.t[:, :, 8:16]
            E("dve", "tensor_tensor", [q4.R, C["cost"].R], [rt[p_].R], rt[p_].t[:, 0, :, :], x1, cs, ALU.mult)
            E("dve", "tensor_tensor", [q4.R, C["sint"].R], [rt[p_].R], rt[p_].t[:, 1, :, :], x2, sn, ALU.mult)
            E("dve", "tensor_tensor", [q4.R, C["cost"].R], [rt[p_].R], rt[p_].t[:, 2, :, :], x2, cs, ALU.mult)
            E("dve", "tensor_tensor", [q4.R, C["sint"].R], [rt[p_].R], rt[p_].t[:, 3, :, :], x1, sn, ALU.mult)
            yield
            E("dve", "tensor_tensor", [rt[p_].R], [qb_.R], qb_.t[:, :, 0:8], rt[p_].t[:, 0, :, :], rt[p_].t[:, 1, :, :], ALU.subtract)
            E("dve", "tensor_tensor", [rt[p_].R], [qb_.R], qb_.t[:, :, 8:16], rt[p_].t[:, 2, :, :], rt[p_].t[:, 3, :, :], ALU.add)
            yield
            b2 = 2 + p_
            pb = ps[b2][:].bitcast(BF16)
            qflat = qb_.t[:].rearrange("p a d -> p (a d)")
            TR(pb[:, 0:128], qflat[:, 0:128], C["identb"].t[:], [qb_.R, C["identb"].R], [pr[b2]], inc=False)
            TR(pb[:, 128:256], qflat[:, 128:256], C["identb"].t[:], [qb_.R, C["identb"].R], [pr[b2]])
            yield
            tq = ar["tq"][i % 2]
            E("dve", "tensor_copy", [pr[b2]], [tq.R], tq.t[:], pb[:, 0:256])
            E("act", "copy", [tq.R], [qT.R], qT.t[:, i * 128:(i + 1) * 128], tq.t[:, 0:128])
            E("pool", "tensor_copy", [tq.R], [kTp.r[0]], kTp.t[0:64, 0, i * 128:(i + 1) * 128], tq.t[0:64, 128:256])
            E("pool", "tensor_copy", [tq.R], [kTp.r[1]], kTp.t[64:128, 1, i * 128:(i + 1) * 128], tq.t[64:128, 128:256])
        for i0_ in range(0, NT, 2):
            gens = [c_tile(i_) for i_ in range(i0_, min(NT, i0_ + 2))]
            alive = [True] * len(gens)
            while any(alive):
                for j_, g_ in enumerate(gens):
                    if alive[j_]:
                        try:
                            next(g_)
                        except StopIteration:
                            alive[j_] = False
        if CSTOP <= 4:
            return
        bc_inproj_common(slot, ar, "c")
        if CSTOP <= 5:
            return
        Eb, EMb, rec = ar["E"], ar["EM"], ar["rec"]
        nb_, db_ = 6, 7
        sb = [2, 3, 4, 5]
        NE = len(Eb)
        for qb in range(NQB):
            qbase = qb * QB
            nkt = (qb + 1) * KPB
            pairs = [(kt, h) for kt in range(nkt - 1, -1, -1) for h in range(2)]

            def stage_a(n):
                kt, h = pairs[n]
                kbase = kt * 128
                off = qbase - kbase + OFF0
                b = sb[n % 4]
                Et, EMt = Eb[n % NE], EMb[n % NE]
                MM(ps[b][:, 0:QB], kTp.t[:, h, kbase:kbase + 128], qT.t[:, qbase:qbase + QB],
                   [kTp.r[h], qT.R], [pr[b]])
                E("act", "activation", [pr[b]], [Et.R], Et.t[:, 0:QB], ps[b][:, 0:QB], AF.Exp)
                E("pool" if n % 2 == 0 else "dve", "tensor_tensor", [Et.R, ar["mtab"].R], [EMt.R], out=EMt.t[:, 0:QB],
                  in0=Et.t[:, 0:QB], in1=ar["mtab"].t[:, off:off + QB], op=ALU.mult)

            def stage_b(n):
                kt, h = pairs[n]
                first = (kt == nkt - 1)
                last = (kt == 0)
                EMt = EMb[n % NE]
                MM(ps[nb_][64 * h:64 * h + 64, 0:QB], vt.t[:, kt, 64 * h:64 * h + 64], EMt.t[:, 0:QB],
                   [vt.R, EMt.R], [pr[nb_]], start=first, stop=last, inc=False)
                MM(ps[db_][64 * h:64 * h + 64, 0:QB], C["onesb"].t[:, 0:64], EMt.t[:, 0:QB],
                   [C["onesb"].R, EMt.R], [pr[db_]], start=first, stop=last)

            SK = 2
            for n in range(len(pairs) + SK):
                if n < len(pairs):
                    stage_a(n)
                if n >= SK:
                    stage_b(n - SK)
            E("dve", "reciprocal", [pr[db_]], [rec.R], rec.t[:, 0:QB], ps[db_][:, 0:QB])
            E("dve", "tensor_tensor", [pr[nb_], rec.R], [rec.R], rec.t[:, 0:QB], ps[nb_][:, 0:QB], rec.t[:, 0:QB], ALU.mult)
            E("pool", "tensor_tensor", [rec.R, zs.R], [mixT.r[5 + u]], out=mixT.t[:, 5 + u, qbase:qbase + QB],
              in0=rec.t[:, 0:QB], in1=zs.t[:, qbase:qbase + QB], op=ALU.mult)

    def a_prep(l, ar):
        beta, gg, t12 = ar["beta"], ar["g"], ar["t12"]
        S.dma(wbas.t[:], w_in_d[l][:, OFF_BETA:OFF_BETA + 12].rearrange("(c p) n -> p c n", p=128), writes=[wbas.R])
        E("pool", "tensor_copy", [wbas.R], [wba.R], wba.t[:], wbas.t[:])
        b = nbank(0, 2)
        for i in range(NT):
            for c in range(KC):
                MM(ps[b][:, i * 12:(i + 1) * 12], hT.t[:, c, i * 128:(i + 1) * 128], wba.t[:, c, :],
                   [wba.R, hT.r[i // KPB]], [pr[b]], start=(c == 0), stop=(c == KC - 1),
                   inc=(c == KC - 1 and i == NT - 1))
        pv = ps[b][:, 0:NT * 12].rearrange("p (n k) -> p n k", k=12)
        E("act", "activation", [pr[b]], [beta.R], beta.t[:], pv[:, :, 0:6], AF.Exp, scale=-1.0)
        E("act", "activation", [beta.R], [beta.R], beta.t[:], beta.t[:], AF.Ln, bias=1.0)
        E("act", "activation", [beta.R], [beta.R], beta.t[:], beta.t[:], AF.Exp, scale=-1.0)
        E("dve", "tensor_tensor", [pr[b], dtb.R], [t12.R], t12.t[:], pv[:, :, 6:12],
          dtb.t[:, l, :].unsqueeze(1).broadcast_to([128, NT, 6]), ALU.add)
        E("act", "activation", [t12.R], [t12.R], t12.t[:], t12.t[:], AF.Exp)
        E("act", "activation", [t12.R], [t12.R], t12.t[:], t12.t[:], AF.Ln, bias=1.0)
        E("dve", "tensor_tensor", [t12.R, nea.R], [gg.R], rr(gg.t[:]), t12.t[:],
          nea.t[:, l, :].unsqueeze(1).broadcast_to([128, NT, 6]), ALU.mult)

    def a_unit(l, u, slot, ar):
        beta, gg = ar["beta"], ar["g"]
        xb, acc, et, yq, qkvT = ar["xb"], ar["acc"], ar["et"], ar["yq"], ar["qkvT"]
        sqr, etg = ar["sqr"], ar["etg"]
        gz = ar["gz"]
        Ssb = ar["S"]
        identf, tri_le, onesf = C["identf"], C["tri_le"], C["onesf"]
        E("pool", "tensor_copy", [zerof.R], [Ssb.R], rr(Ssb.t[:]), zerof.t[:, 0:64])
        for g in range(3):
            E("pool", "memset", [], [xb[g].R], xb[g].t[:, 0:3], 0.0)
        for tb in range(T // QA):
            def grp(g):
                b = nbank(0, 2)
                proj_fm(slot, g, tb, b, QA)
                gi = g * 3 + u
                xg = xb[g]
                E("act", "copy", [pr[b]], [xg.R], xg.t[:, 3:3 + QA], ps[b][:, 0:QA])
                yield
                a_ = acc[g]
                E("dve", "tensor_scalar", [xg.R, convw.R], [a_.R], a_.t[:, 0:QA], xg.t[:, 0:QA],
                  convw.t[:, l, gi, 0:1], None, ALU.mult)
                for k in range(1, 4):
                    E("dve", "scalar_tensor_tensor", [xg.R, convw.R, a_.R], [a_.R], a_.t[:, 0:QA], xg.t[:, k:k + QA],
                      convw.t[:, l, gi, k:k + 1], a_.t[:, 0:QA], ALU.mult, ALU.add)
                E("pool", "tensor_copy", [xg.R], [xg.R], xg.t[:, 0:3], xg.t[:, QA:QA + 3])
                yield
                e_ = etg[g]
                E("act", "activation", [a_.R], [e_.R], e_.t[:, 0:QA], a_.t[:, 0:QA], AF.Exp, scale=-1.0)
                E("act", "activation", [e_.R], [e_.R], e_.t[:, 0:QA], e_.t[:, 0:QA], AF.Ln, bias=1.0)
                E("act", "activation", [e_.R], [e_.R], e_.t[:, 0:QA], e_.t[:, 0:QA], AF.Exp, scale=-1.0)
                yield
                if g == 2:
                    E("dve", "tensor_tensor", [a_.R, e_.R], [qkvT[2].R], qkvT[2].t[:, 0:QA], a_.t[:, 0:QA], e_.t[:, 0:QA], ALU.mult)
                else:
                    y = yq[g]
                    E("dve", "tensor_tensor", [a_.R, e_.R], [y.R], y.t[:, 0:QA], a_.t[:, 0:QA], e_.t[:, 0:QA], ALU.mult)
                    yield
                    E("pool", "tensor_tensor", [y.R], [sqr.R], out=rr(sqr.t[:, 0:QA]), in0=y.t[:, 0:QA], in1=y.t[:, 0:QA], op=ALU.mult)
                    b2 = nbank(0, 2)
                    MM(ps[b2][:, 0:QA], rr(C["blockones_r"].t[:]), rr(sqr.t[:, 0:QA]), [C["blockones_r"].R, sqr.R], [pr[b2]])
                    yield
                    E("act", "activation", [pr[b2]], [e_.R], e_.t[:, 0:QA], ps[b2][:, 0:QA], AF.Ln, bias=EPS)
                    E("act", "activation", [e_.R], [e_.R], e_.t[:, 0:QA], e_.t[:, 0:QA], AF.Exp, scale=-0.5)
                    yield
                    E("dve", "scalar_tensor_tensor", [y.R, e_.R], [qkvT[g].R], rr(qkvT[g].t[:, 0:QA]), y.t[:, 0:QA],
                      0.125 if g == 0 else 1.0, e_.t[:, 0:QA], ALU.mult, ALU.mult)

            ggens = [grp(g) for g in range(3)]
            galive = [True] * 3
            while any(galive):
                for j_, g_ in enumerate(ggens):
                    if galive[j_]:
                        try:
                            next(g_)
                        except StopIteration:
                            galive[j_] = False
            b = nbank(0, 2)
            for j in range(KA):
                i = tb * KA + j
                proj_tm(slot, 384, 128, i, ps[b][:, j * 128:(j + 1) * 128], b)
            silu_from_psum(b, QA, et[0], gz.t[:].rearrange("p a d -> p (a d)"), gz.R)
            E("pool", "tensor_tensor", [gz.R, gw.R], [gz.R], out=gz.t[:],
              in0=gz.t[:],
              in1=gw.t[:, l, :].unsqueeze(1).broadcast_to([128, KA * 2, 64]), op=ALU.mult)
            gens = [a_tile(l, u, tb * KA + j, j, ar) for j in range(KA)]
            front = [True] * KA
            while any(front):
                for j, g_ in enumerate(gens):
                    if front[j]:
                        if next(g_) == "BACK":
                            front[j] = False
            for g_ in gens:
                for _ in g_:
                    pass

    def a_tile(l, u, i, j, ar):
        beta, gg = ar["beta"], ar["g"]
        qkvT, gz, Ssb = ar["qkvT"], ar["gz"], ar["S"]
        identf, tri_le, onesf = C["identf"], C["tri_le"], C["onesf"]
        k = j % 2
        BM = [[2, 3, 4, 5, 6, 7], [5, 6, 7, 2, 3, 4]][k]
        kv, gcs, egc, sbeg, gb, d0, DT, DTs, DTi = (ar[n][k] for n in ("kv", "gcs", "egc", "sbeg", "gb", "d0", "DT", "DTs", "DTi"))
        egr, kTp, M0p, qkT, MN, Pm, bv, usb, wTp, qgTp, vnew = (ar[n][k] for n in ("egr", "kTp", "M0p", "qkT", "MN", "Pm", "bv", "usb", "wTp", "qgTp", "vnew"))
        osb, osq, oss, ot, obf = (ar[n][k] for n in ("osb", "osq", "oss", "ot", "obf"))
        cs = slice(j * 128, (j + 1) * 128)
        bi = beta.t[:, i, 2 * u:2 * u + 2]
        gi = gg.t[:, i, 2 * u:2 * u + 2]
        qT_, kT_, vT_ = qkvT[0], qkvT[1], qkvT[2]
        b = BM[0]
        TR(ps[b][:, 0:128], kT_.t[:, cs], identf.t[:], [kT_.R, identf.R], [pr[b]], inc=False)
        TR(ps[b][:, 128:256], vT_.t[:, cs], identf.t[:], [vT_.R, identf.R], [pr[b]], inc=False)
        MM(ps[b][:, 256:258], rr(C["tri_le_r"].t[:]), rr(gi), [C["tri_le_r"].R, gg.R], [pr[b]])
        yield
        E("act", "copy", [pr[b]], [kv.R], kv.t[:], ps[b][:, 0:256].rearrange("p (a h d) -> p a h d", a=2, h=2))
        E("dve", "tensor_copy", [pr[b]], [gcs.R], gcs.t[:], ps[b][:, 256:258])
        E("act", "activation", [gcs.R], [egc.R], egc.t[:], gcs.t[:], AF.Exp)
        E("pool", "tensor_tensor", [egc.R, beta.R], [sbeg.R], out=sbeg.t[:], in0=egc.t[:], in1=bi, op=ALU.mult)
        yield
        for h in range(2):
            E("dve", "tensor_scalar", [onesf.R, gg.R], [gb.R], rr(gb.t[:, h, :]), onesf.t[:], gi[:, h:h + 1], None, ALU.mult)
            E("dve", "tensor_scalar", [onesf.R, gg.R], [gb.R], rr(gb.t[:, 2, 64 * h:64 * h + 64]), onesf.t[:, 0:64], gi[:, h:h + 1], None, ALU.mult)
        b = BM[1]
        for h in range(3):
            MM(ps[b][:, h * 128:(h + 1) * 128], rr(gb.t[:, h, :]), rr(C["tri_le_r"].t[:]), [gb.R, C["tri_le_r"].R], [pr[b]], inc=(h == 2))
        yield
        for h in range(2):
            E("dve", "tensor_scalar", [pr[b], gcs.R], [d0.R], d0.t[:, h, :], ps[b][:, h * 128:(h + 1) * 128],
              gcs.t[:, h:h + 1], 0.0, ALU.subtract, ALU.min)
        E("act", "activation", [d0.R], [DT.R], DT.t[:], d0.t[:], AF.Exp)
        E("act", "activation", [pr[b]], [egr.R], egr.t[:], ps[b][:, 256:384], AF.Exp)
        E("pool", "tensor_tensor", [DT.R, ar["nst2"].R], [DTs.R], out=DTs.t[:], in0=DT.t[:], in1=ar["nst2"].t[:], op=ALU.mult)
        E("pool", "tensor_tensor", [DT.R, ar["incl2"].R], [DTi.R], out=DTi.t[:], in0=DT.t[:], in1=ar["incl2"].t[:], op=ALU.mult)
        yield
        E("pool", "tensor_copy", [kT_.R], [kTp.R], rr(kTp.t[0:64, 0, :]), rr(kT_.t[0:64, cs]))
        E("pool", "tensor_copy", [kT_.R], [kTp.R], rr(kTp.t[64:128, 1, :]), rr(kT_.t[64:128, cs]))
        b = BM[2]
        for h in range(2):
            MM(ps[b][:, h * 128:(h + 1) * 128], rr(kTp.t[:, h, :]), rr(kT_.t[:, cs]), [kTp.R, kT_.R], [pr[b]], inc=False)
        for h in range(2):
            MM(ps[b][:, 256 + h * 128:256 + (h + 1) * 128], rr(kTp.t[:, h, :]), rr(qT_.t[:, cs]), [kTp.R, qT_.R], [pr[b]], inc=(h == 1))
        yield
        E("dve", "tensor_tensor", [pr[b], DTs.R], [M0p.R], M0p.t[:], ps[b][:, 0:256].rearrange("p (h n) -> p h n", h=2), DTs.t[:], ALU.mult)
        E("dve", "tensor_tensor", [pr[b], DTi.R], [qkT.R], rr(qkT.t[:]), ps[b][:, 256:512].rearrange("p (h n) -> p h n", h=2), DTi.t[:], ALU.mult)
        yield
        b = BM[3]
        for h in range(2):
            TR(ps[b][:, h * 128:(h + 1) * 128], M0p.t[:, h, :], identf.t[:], [M0p.R, identf.R], [pr[b]], inc=(h == 1))
        mn = MN[0]
        yield
        for h in range(2):
            E("dve", "tensor_scalar", [pr[b], beta.R], [mn.R], rr(mn.t[:, 2 + h, :]), ps[b][:, h * 128:(h + 1) * 128], bi[:, h:h + 1], None, ALU.mult)
        b = BM[3]
        for h in range(2):
            TR(ps[b][:, h * 128:(h + 1) * 128], mn.t[:, 2 + h, :], identf.t[:], [mn.R, identf.R], [pr[b]], inc=(h == 1))
        yield
        E("act", "copy", [pr[b]], [mn.R], rr(mn.t[:, 0:2, :]), ps[b][:, 0:256].rearrange("p (h n) -> p h n", h=2))
        E("dve", "tensor_tensor", [pr[b], ar["ident2"].R], [Pm.R], rr(Pm.t[:]), ps[b][:, 0:256].rearrange("p (h n) -> p h n", h=2), ar["ident2"].t[:], ALU.add)
        yield
        for lv in range(1, 7):
            mo = MN[(lv - 1) % 2]
            mw = MN[lv % 2]
            b = BM[4]
            if lv < 6:
                for h in range(2):
                    MM(ps[b][:, h * 128:(h + 1) * 128], rr(mo.t[:, 2 + h, :]), rr(mo.t[:, h, :]), [mo.R], [pr[b]], inc=False)
            for h in range(2):
                MM(ps[b][:, 256 + h * 128:256 + (h + 1) * 128], rr(mo.t[:, h, :]), rr(mo.t[:, 2 + h, :]), [mo.R], [pr[b]], inc=(h == 1))
            yield
            if lv < 6:
                E("act", "copy", [pr[b]], [mw.R], rr(mw.t[:, 0:2, :]), ps[b][:, 0:256].rearrange("p (h n) -> p h n", h=2))
            E("dve", "tensor_copy", [pr[b]], [mw.R], rr(mw.t[:, 2:4, :]), ps[b][:, 256:512].rearrange("p (h n) -> p h n", h=2))
            yield
            b = BM[5]
            for h in range(2):
                MM(ps[b][:, h * 128:(h + 1) * 128], rr(mw.t[:, 2 + h, :]), rr(Pm.t[:, h, :]), [mw.R, Pm.R], [pr[b]], inc=(h == 1))
            yield
            E("dve", "tensor_tensor", [pr[b], Pm.R], [Pm.R], rr(Pm.t[:]), Pm.t[:], ps[b][:, 0:256].rearrange("p (h n) -> p h n", h=2), ALU.add)
            yield
        yield
        E("pool", "tensor_tensor", [kv.R, beta.R], [bv.R], out=rr(bv.t[:, 0, :, :]), in0=kv.t[:, 1, :, :],
          in1=bi.unsqueeze(2).broadcast_to([128, 2, 64]), op=ALU.mult)
        E("pool", "tensor_tensor", [kv.R, sbeg.R], [bv.R], out=rr(bv.t[:, 1, :, :]), in0=kv.t[:, 0, :, :],
          in1=sbeg.t[:].unsqueeze(2).broadcast_to([128, 2, 64]), op=ALU.mult)
        E("pool", "tensor_tensor", [kv.R, DT.R], [bv.R], out=rr(bv.t[:, 2, :, :]), in0=kv.t[:, 0, :, :],
          in1=DT.t[:, :, 127:128].broadcast_to([128, 2, 64]), op=ALU.mult)
        b = BM[1]
        for h in range(2):
            MM(ps[b][:, h * 64:(h + 1) * 64], rr(Pm.t[:, h, :]), rr(bv.t[:, 0, h, :]), [Pm.R, bv.R], [pr[b]], inc=False)
        for h in range(2):
            MM(ps[b][64 * h:64 * h + 64, 128:256], bv.t[:, 1, h, :], Pm.t[:, h, :], [Pm.R, bv.R], [pr[b]], inc=(h == 1))
        yield
        E("act", "copy", [pr[b]], [usb.R], usb.t[:], ps[b][:, 0:128].rearrange("p (h d) -> p h d", h=2))
        E("dve", "tensor_copy", [pr[b]], [wTp.R], rr(wTp.t[0:64, 0, :]), ps[b][0:64, 128:256])
        E("dve", "tensor_copy", [pr[b]], [wTp.R], rr(wTp.t[64:128, 1, :]), ps[b][64:128, 128:256])
        E("pool", "tensor_tensor", [qT_.R, egr.R], [qgTp.R], out=rr(qgTp.t[0:64, 0, :]), in0=qT_.t[0:64, cs], in1=egr.t[0:64, :], op=ALU.mult)
        E("pool", "tensor_tensor", [qT_.R, egr.R], [qgTp.R], out=rr(qgTp.t[64:128, 1, :]), in0=qT_.t[64:128, cs], in1=egr.t[64:128, :], op=ALU.mult)
        yield
        yield "BACK"
        b = BM[2]
        for h in range(2):
            MM(ps[b][:, h * 64:(h + 1) * 64], rr(wTp.t[:, h, :]), rr(Ssb.t[:]), [wTp.R, Ssb.R], [pr[b]], inc=(h == 1))
        E("dve", "tensor_tensor", [usb.R, pr[b]], [vnew.R], rr(vnew.t[:]), usb.t[:], ps[b][:, 0:128].rearrange("p (h d) -> p h d", h=2), ALU.subtract)
        bo = BM[4]
        for h in range(2):
            MM(ps[bo][:, h * 64:(h + 1) * 64], rr(qgTp.t[:, h, :]), rr(Ssb.t[:]), [qgTp.R, Ssb.R], [pr[bo]], start=True, stop=False, inc=False)
            MM(ps[bo][:, h * 64:(h + 1) * 64], rr(qkT.t[:, h, :]), rr(vnew.t[:, h, :]), [qkT.R, vnew.R], [pr[bo]], start=False, stop=True, inc=(h == 1))
        b = BM[5]
        for h in range(2):
            MM(ps[b][64 * h:64 * h + 64, 0:64], bv.t[:, 2, h, :], vnew.t[:, h, :], [bv.R, vnew.R], [pr[b]], inc=(h == 1))
        E("dve", "scalar_tensor_tensor", [Ssb.R, egr.R, pr[b]], [Ssb.R], rr(Ssb.t[:]), Ssb.t[:], egr.t[:, 127:128], ps[b][:, 0:64], ALU.mult, ALU.add)
        yield
        E("act", "copy", [pr[bo]], [osb.R], osb.t[:], ps[bo][:, 0:128].rearrange("p (h d) -> p h d", h=2))
        E("pool", "tensor_tensor", [osb.R], [osq.R], out=osq.t[:], in0=osb.t[:], in1=osb.t[:], op=ALU.mult)
        E("dve", "tensor_reduce", [osq.R], [oss.R], out=oss.t[:], in_=osq.t[:], axis=AX.X, op=ALU.add)
        E("act", "activation", [oss.R], [oss.R], oss.t[:], oss.t[:], AF.Ln, scale=1.0 / 64, bias=EPS)
        E("act", "activation", [oss.R], [oss.R], oss.t[:], oss.t[:], AF.Exp, scale=-0.5)
        E("dve", "tensor_tensor", [osb.R, oss.R], [ot.R], ot.t[:], osb.t[:], oss.t[:].unsqueeze(2).broadcast_to([128, 2, 64]), ALU.mult)
        E("dve", "tensor_tensor", [ot.R, gz.R], [obf.R], obf.t[:], ot.t[:], gz.t[:, 2 * j:2 * j + 2, :], ALU.mult)
        b = nbank(0, 2)
        pb = ps[b][:].bitcast(BF16)
        TR(pb[:, 0:128], obf.t[:].rearrange("p h d -> p (h d)"), C["identb"].t[:], [obf.R, C["identb"].R], [pr[b]])
        E("dve", "tensor_copy", [pr[b]], [mixT.r[u]], mixT.t[:, u, i * 128:(i + 1) * 128], pb[:, 0:128])

    def phase_out(l):
        for hf in range(2):
            for i in range(NT):
                b = nbank(0, 4)
                for c in range(KC):
                    MM(ps[b][:, 0:512], mixT.t[:, c, i * 128:(i + 1) * 128], wbuf[hf].t[:, c, :],
                       [mixT.r[c]] + wbuf[hf].r, [pr[b]], start=(c == 0), stop=(c == KC - 1), inc=(c == KC - 1))
                E("dve", "tensor_tensor", [pr[b], X.r[i]], [X.r[i]], X.t[:, i, hf * 512:(hf + 1) * 512],
                  X.t[:, i, hf * 512:(hf + 1) * 512], ps[b][:, 0:512], ALU.add)

    import contextlib

    class Arena:
        def __init__(self):
            self.stack = contextlib.ExitStack()

        def tl(self, name, shape, dt):
            t = Tl.__new__(Tl)
            t.t = self.stack.enter_context(nc.sbuf_tensor(name, list(shape), dt))
            t.r = [Res(name)]
            t.R = t.r[0]
            return t

        def const(self, k):
            v = cst[k]
            t = self.tl(nm("ac_" + k), v.shape, BF16 if v.dtype == ml_dtypes.bfloat16 else F32)
            S.dma(t.t[:], cd[k], writes=[t.R])
            self.consts = getattr(self, "consts", []) + [t]
            return t

        def close(self):
            for t_ in getattr(self, "consts", []):
                for e_ in ("pe", "act", "dve", "pool"):
                    S._wait(e_, *t_.R.w)
            S.barrier()
            for k_ in ("pe", "act", "dve", "pool"):
                S._wait("sp", k_, S.cnt[k_])
            self.stack.close()

    uid = [0]

    def nm(s):
        uid[0] += 1
        return "%s_%d" % (s, uid[0])

    for s in range(NSEQ):
        for i0 in range(0, NT, 4):
            n = min(4, NT - i0)
            S.dma(X.t[:, i0:i0 + n, :], x_d[s, i0 * 128:(i0 + n) * 128, :].rearrange("(n p) d -> p n d", p=128),
                  writes=[X.r[i] for i in range(i0, i0 + n)])
        for l in range(NL):
            units = [("a", u, [OFF_QA + 128 * u, OFF_KA + 128 * u, OFF_VA + 128 * u, OFF_ZA + 128 * u]) for u in range(3)]
            units += [("b", u, [OFF_QB + 128 * u, OFF_KB + 128 * u, OFF_VB + 128 * u, OFF_ZB + 128 * u]) for u in range(2)]
            units += [("c", u, [OFF_QC + 128 * u, OFF_KC + 128 * u, OFF_VC + 128 * u, OFF_ZC + 128 * u]) for u in range(3)]
            load_cols(0, unit_cols(l, units[0][2]))
            A = Arena()
            ar = {"ss": A.tl(nm("ss"), [128, NT], F32), "junk": A.tl(nm("junk"), [128, D], BF16),
                  "xn": [A.tl(nm("xn"), [128, D], BF16) for _ in range(2)]}
            phase_norm(l, ar)
            A.close()
            A = Arena()
            ar = {"beta": A.tl(nm("beta"), [128, NT, 6], F32), "g": A.tl(nm("g"), [128, NT, 6], F32),
                  "t12": A.tl(nm("t12"), [128, NT, 6], F32),
                  "xb": [A.tl(nm("xb"), [128, QA + 3], F32) for _ in range(3)],
                  "qkvT": [A.tl(nm("qkvT"), [128, QA], F32) for _ in range(3)],
                  "gz": A.tl(nm("gz"), [128, 2 * KA, 64], F32),
                  "S": A.tl(nm("S"), [128, 64], F32)}
            _acc = A.tl(nm("acc"), [128, QA], F32)
            _et = A.tl(nm("et"), [128, QA], F32)
            ar["et"] = [_et, _et]
            _yq = A.tl(nm("yq"), [128, QA], F32)

            mix_state = [3, 0]

            def mix_tl(name, shp, dt):
                n = int(np.prod(shp[1:])) * (2 if dt == F32 else 1)
                if mix_state[1] + n > T:
                    mix_state[0] += 1
                    mix_state[1] = 0
                if T < 2048 or mix_state[0] > 7 or n > T:
                    return None
                ap = mixT.t[:, mix_state[0], mix_state[1]:mix_state[1] + n]
                mix_state[1] += n
                if dt == F32:
                    ap = ap.bitcast(F32)
                if len(shp) == 3:
                    ap = ap.rearrange("p (a b) -> p a b", a=shp[1])
                elif len(shp) == 4:
                    ap = ap.rearrange("p (a b c) -> p a b c", a=shp[1], b=shp[2])
                t_ = Tl.__new__(Tl)
                t_.t = ap
                t_.r = [Res(name)]
                t_.R = t_.r[0]
                return t_

            def two(n_, shp, dt=F32):
                t0_ = A.tl(nm(n_), shp, dt)
                ok_alias = n_ in ("kv", "d0", "DTs", "DTi", "egr", "M0p", "usb", "osb", "osq", "ot")
                t1_ = mix_tl(nm(n_ + "m"), shp, dt) if ok_alias else None
                if t1_ is None:
                    t1_ = A.tl(nm(n_), shp, dt)
                return [t0_, t1_]
            ar.update({"kv": two("kv", [128, 2, 2, 64]), "gcs": two("gcs", [128, 2]), "egc": two("egc", [128, 2]),
                       "sbeg": two("sbeg", [128, 2]), "gb": two("gb", [128, 3, 128]), "d0": two("d0", [128, 2, 128]),
                       "DTs": two("DTs", [128, 2, 128]), "DTi": two("DTi", [128, 2, 128]),
                       "egr": two("egr", [128, 128]), "kTp": two("kTpA", [128, 2, 128]), "M0p": two("M0p", [128, 2, 128]),
                       "qkT": two("qkT", [128, 2, 128]),
                       "Pm": two("Pm", [128, 2, 128]), "bv": two("bv", [128, 3, 2, 64]), "usb": two("usb", [128, 2, 64]),
                       "wTp": two("wTp", [128, 2, 128]), "qgTp": two("qgTp", [128, 2, 128]), "vnew": two("vnew", [128, 2, 64]),
                       "osb": two("osb", [128, 2, 64]), "osq": two("osq", [128, 2, 64]), "oss": two("oss", [128, 2]),
                       "ot": two("ot", [128, 2, 64]), "obf": two("obf", [128, 2, 64], BF16)})
            ar["DT"] = ar["d0"]

            def extra(n_, base):
                t_ = mix_tl(nm(n_), [128, QA], F32)
                return t_ if t_ is not None else (A.tl(nm(n_), [128, QA], F32) if T < 2048 else base)
            ar["acc"] = [_acc, extra("acc1", _acc), extra("acc2", _acc)]
            ar["etg"] = [_et, extra("et1", _et), extra("et2", _et)]
            ar["yq"] = [_yq, extra("yq1", _yq)]
            for k_ in ("incl2", "nst2", "ident2"):
                t_ = mix_tl(nm("mc_" + k_), [128, 2, 128], F32)
                if t_ is None:
                    ar[k_] = A.const(k_)
                else:
                    S.dma(t_.t[:], cd[k_], writes=[t_.R])
                    A.consts = getattr(A, "consts", []) + [t_]
                    ar[k_] = t_
            ar["MN"] = [[A.tl(nm("MN"), [128, 4, 128], F32) for _ in range(2)],
                        [A.tl(nm("MN"), [128, 4, 128], F32) for _ in range(2)]]
            for n_ in ("kTp", "wTp", "qgTp"):
                for k_ in range(2):
                    E("pool", "tensor_copy", [zerof.R], [ar[n_][k_].R], rr(ar[n_][k_].t[:]), zerof.t[:].rearrange("p (h n) -> p h n", h=2))
            ar["sqr"] = A.tl(nm("sqr"), [128, QA], F32)
            if "a" in phases:
                a_prep(l, ar)
            ui = 0
            for (kind, u, offs) in units[0:3]:
                load_cols((ui + 1) % 2, unit_cols(l, units[ui + 1][2]))
                if "a" in phases:
                    a_unit(l, u, ui % 2, ar)
                ui += 1
            A.close()
            for kind_, ulist in (("b", units[3:5]), ("c", units[5:8])):
                A = Arena()
                ar = {"qT": A.tl(nm("qT"), [128, T], BF16), "kTp": A.tl(nm("kTp"), [128, 2, T], BF16),
                      "vt": A.tl(nm("vt"), [128, NT, 128], BF16), "zs": A.tl(nm("zs"), [128, T], BF16)}
                _et = A.tl(nm("etb"), [128, QB], F32)
                ar["et"] = [_et, _et]
                if kind_ == "b":
                    ar["sp"] = [[A.tl(nm("sp"), [128, QB], F32) for _ in range(2)] for _ in range(2)]
                    ar["P"] = [[A.tl(nm("P"), [128, QB], F32) for _ in range(2)] for _ in range(2)]
                    ar["wts"] = [[A.tl(nm("wts"), [128, QB], BF16) for _ in range(2)] for _ in range(2)]
                    ar["Rb"] = [A.tl(nm("Rb"), [128, QB], F32) for _ in range(2)]
                else:
                    ar.update({"E": [A.tl(nm("E"), [128, QB], BF16) for _ in range(3)],
                               "EM": [A.tl(nm("EM"), [128, QB], BF16) for _ in range(3)],
                               "rec": A.tl(nm("rec"), [128, QB], F32),
                               "qk": [A.tl(nm("qk"), [128, 4, 64], F32) for _ in range(2)],
                               "qkb": [A.tl(nm("qkb"), [128, 4, 64], BF16) for _ in range(2)],
                               "sq": [A.tl(nm("sq"), [128, 4, 64], F32) for _ in range(2)],
                               "s4": [A.tl(nm("s4"), [128, 4], F32) for _ in range(2)],
                               "rt": [A.tl(nm("rt"), [128, 4, 4, 8], F32) for _ in range(2)],
                               "tq": [A.tl(nm("tq"), [128, 256], BF16) for _ in range(2)]})
                    ar["mtab"] = A.const("mtab")
                ar["kTp"].r = [Res("kTp0"), Res("kTp1")]
                E("pool", "memset", [], ar["kTp"].r, ar["kTp"].t[:], 0.0)
                for (kind, u, offs) in ulist:
                    if ui + 1 < len(units):
                        load_cols((ui + 1) % 2, unit_cols(l, units[ui + 1][2]))
                    else:
                        load_cols((ui + 1) % 2, [(g, w_out_d[l][:, g * 128:(g + 1) * 128]) for g in range(4)])
                    if kind == "b" and "b" in phases:
                        b_unit(l, u, ui % 2, ar)
                    if kind == "c" and "c" in phases:
                        c_unit(l, u, ui % 2, ar)
                    ui += 1
                A.close()
            load_cols(1, [(g, w_out_d[l][:, 512 + g * 128:512 + (g + 1) * 128]) for g in range(4)])
            if dbg and l == NL - 1 and s == NSEQ - 1:
                A = Arena()
                dm = A.tl(nm("dm"), [128, T], F32)
                for c in range(KC):
                    E("dve", "tensor_copy", [mixT.r[c]], [dm.R], dm.t[:], mixT.t[:, c, :])
                    S.dma(dbg_d[:, c, :], dm.t[:], reads=[dm.R], writes=[Res("dbgout")])
                E("pool", "memset", [], [dm.R], dm.t[:, 0:1], 0.0)
                A.close()
            phase_out(l)
            S.barrier()
        outres = Res("out")
        for i0 in range(0, NT, 4):
            n = min(4, NT - i0)
            S.dma(out_d[s, i0 * 128:(i0 + n) * 128, :].rearrange("(n p) d -> p n d", p=128), X.t[:, i0:i0 + n, :],
                  reads=[X.r[i] for i in range(i0, i0 + n)], writes=[outres])
        S.finish([outres])
    return nc, cst, S


_CACHE = {}


def kernel(x, norm_w, w_in, conv_w, a_log, dt_bias, gdn_norm_w, q_norm_w, k_norm_w, w_out):
    n_cores = 8
    B, T, _ = x.shape
    NSEQ = B // n_cores
    NL = w_in.shape[0]
    key = (T, NSEQ, NL)
    if key not in _CACHE:
        _CACHE[key] = build_program(T, NSEQ, NL)
    nc, cst, _ = _CACHE[key]
    f = lambda a: np.ascontiguousarray(np.asarray(a, dtype=np.float32))
    shared = {"w_in": f(w_in), "w_out": f(w_out), "norm_w": f(norm_w), "conv_w": f(conv_w), "a_log": f(a_log),
              "dt_bias": f(dt_bias), "gdn_norm_w": f(gdn_norm_w), "q_norm_w": f(q_norm_w), "k_norm_w": f(k_norm_w)}
    for k, v in cst.items():
        shared["c_" + k] = v
    xs = f(x)
    in_maps = []
    for c in range(n_cores):
        m = dict(shared)
        m["x"] = xs[c * NSEQ:(c + 1) * NSEQ]
        in_maps.append(m)
    res = run_bass_kernel_spmd(nc, in_maps, core_ids=list(range(n_cores)))
    return np.concatenate([np.asarray(r["out"], dtype=np.float32) for r in res.results], axis=0)
```

```python
import numpy as np
import ml_dtypes
import concourse.bass as bass
import concourse.mybir as mybir
from concourse.bass_utils import run_bass_kernel_spmd

F32 = mybir.dt.float32
BF16 = mybir.dt.bfloat16
F32R = mybir.dt.float32r


def rr(ap):
    return ap.bitcast(F32R)
AF = mybir.ActivationFunctionType
ALU = mybir.AluOpType
AX = mybir.AxisListType

D = 1024
KC = 8
IN_COLS = 4108
OFF_QA, OFF_KA, OFF_VA, OFF_ZA, OFF_BETA, OFF_ALPHA = 0, 384, 768, 1152, 1536, 1542
OFF_QB, OFF_KB, OFF_VB, OFF_ZB = 1548, 1804, 2060, 2316
OFF_QC, OFF_KC, OFF_VC, OFF_ZC = 2572, 2956, 3340, 3724
EPS = 1e-6
ROPE_THETA = 500000.0


class Res:
    __slots__ = ("name", "w", "r", "excl")

    def __init__(self, name, excl=False):
        self.name = name
        self.w = None
        self.r = []
        self.excl = excl


class Sched:
    def __init__(self, nc, n_dma_sems=24):
        self.nc = nc
        self.eng = {"pe": nc.tensor, "act": nc.scalar, "dve": nc.vector, "pool": nc.gpsimd, "sp": nc.sync}
        self.sems = {}
        self.cnt = {}
        for e in ("pe", "act", "dve", "pool"):
            self.sems[e] = nc.alloc_semaphore("s_" + e)
            self.cnt[e] = 0
        self.dma_sems = []
        for i in range(n_dma_sems):
            k = "d%d" % i
            self.sems[k] = nc.alloc_semaphore("s_" + k)
            self.cnt[k] = 0
            self.dma_sems.append(k)
        self.dma_rr = 0
        self.waited = {}
        self.pe_pending = False
        self.nwaits = 0
        self.nins = 0

    def _wait(self, e, key, val):
        if val <= 0:
            return
        if e == "pe" and key == "pe":
            return
        if key == "pe" and val > self.cnt["pe"]:
            raise RuntimeError("wait on a PE group that has not been closed with inc=True")
        if self.waited.get((e, key), 0) >= val:
            return
        self.waited[(e, key)] = val
        self.eng[e].wait_ge(self.sems[key], val)
        self.nwaits += 1

    def _deps(self, e, reads, writes):
        for t in reads:
            if t.w is not None:
                self._wait(e, *t.w)
        for t in writes:
            if t.w is not None:
                self._wait(e, *t.w)
            for rr in t.r:
                self._wait(e, *rr)

    @staticmethod
    def _compact(lst):
        best = {}
        for k, v in lst:
            if best.get(k, 0) < v:
                best[k] = v
        return list(best.items())

    def _record(self, key, val, reads, writes):
        for t in reads:
            t.r.append((key, val))
            if len(t.r) > 16:
                t.r = self._compact(t.r)
        for t in writes:
            t.w = (key, val)
            t.r = []

    def op(self, e, ins_fn, reads=(), writes=(), inc=True):
        if any(t.excl for t in reads):
            writes = list(writes) + [t for t in reads if t.excl]
            reads = [t for t in reads if not t.excl]
        self._deps(e, reads, writes)
        ins = ins_fn()
        self.nins += 1
        if e == "pe" and not inc:
            self.pe_pending = True
            self._record("pe", self.cnt["pe"] + 1, reads, writes)
            return ins
        if e == "pe":
            self.pe_pending = False
        self.cnt[e] += 1
        ins.then_inc(self.sems[e], 1)
        self._record(e, self.cnt[e], reads, writes)
        return ins

    def dma(self, out, in_, reads=(), writes=(), q="sp", **kw):
        key = self.dma_sems[self.dma_rr % len(self.dma_sems)]
        self.dma_rr += 1
        self._wait(q, key, self.cnt[key])
        self._deps(q, reads, writes)
        ins = self.eng[q].dma_start(out=out, in_=in_, **kw)
        self.nins += 1
        self.cnt[key] += 16
        ins.then_inc(self.sems[key], 16)
        self._record(key, self.cnt[key], reads, writes)
        return ins

    def barrier(self):
        assert not self.pe_pending
        for e in ("pe", "act", "dve", "pool"):
            for k in ("pe", "act", "dve", "pool"):
                self._wait(e, k, self.cnt[k])

    def finish(self, out_res):
        for t in out_res:
            if t.w is not None:
                self._wait("sp", *t.w)


def _consts(T):
    NT = T // 128
    QB = min(512, T)
    KPB = QB // 128
    c = {}
    idx = np.arange(128)
    c["identf"] = np.eye(128, dtype=np.float32)
    c["identb"] = np.eye(128, dtype=np.float32).astype(ml_dtypes.bfloat16)
    c["tri_gt"] = (idx[:, None] > idx[None, :]).astype(np.float32)
    c["tri_le"] = (idx[:, None] <= idx[None, :]).astype(np.float32)
    c["onesf"] = np.ones((128, 128), np.float32)
    c["onesb"] = np.ones((128, 64), np.float32).astype(ml_dtypes.bfloat16)
    bo = np.zeros((128, 128), np.float32)
    bo[:64, :64] = 1.0
    bo[64:, 64:] = 1.0
    c["blockones"] = bo
    incl = (idx[:, None] <= idx[None, :]).astype(np.float32)
    nst = -(idx[:, None] < idx[None, :]).astype(np.float32)
    c["incl2"] = np.stack([incl, incl], 1)
    c["nst2"] = np.stack([nst, nst], 1)
    c["ident2"] = np.stack([np.eye(128, dtype=np.float32)] * 2, 1)
    OFF0 = (KPB - 1) * 128
    W = OFF0 + (T - QB) + QB
    dl = np.arange(W)[None, :] - idx[:, None] - OFF0
    m = ((dl >= 0) & (dl <= 128)).astype(np.float32)
    m += ((dl >= 0) & (dl <= 512) & (dl % 4 == 0)).astype(np.float32)
    m += ((dl >= 0) & (dl <= 2048) & (dl % 16 == 0)).astype(np.float32)
    c["mtab"] = m.astype(ml_dtypes.bfloat16)
    half = 8
    inv_freq = (ROPE_THETA ** (-np.arange(half, dtype=np.float32) / half)).astype(np.float32)
    pos = np.arange(T, dtype=np.float32)
    ang = (pos[:, None] * inv_freq[None, :]).astype(np.float32)
    c["cost"] = np.ascontiguousarray(np.cos(ang).astype(np.float32).reshape(NT, 128, half).transpose(1, 0, 2))
    c["sint"] = np.ascontiguousarray(np.sin(ang).astype(np.float32).reshape(NT, 128, half).transpose(1, 0, 2))
    return c


def build_program(T=2048, NSEQ=2, NL=2, dbg=False, phases="nabco"):
    NT = T // 128
    QB = min(512, T)
    NQB = T // QB
    KPB = QB // 128
    OFF0 = (KPB - 1) * 128
    QA = min(256, T)
    KA = QA // 128
    nc = bass.Bass("TRN2", target_bir_lowering=False)
    cst = _consts(T)

    def din(name, shape, dt=F32):
        return nc.dram_tensor(name, list(shape), dt, kind="ExternalInput").ap()

    x_d = din("x", [NSEQ, T, D])
    w_in_d = din("w_in", [NL, D, IN_COLS])
    w_out_d = din("w_out", [NL, D, D])
    norm_w_d = din("norm_w", [NL, D])
    conv_w_d = din("conv_w", [NL, 4, 1152])
    a_log_d = din("a_log", [NL, 6])
    dt_bias_d = din("dt_bias", [NL, 6])
    gdn_w_d = din("gdn_norm_w", [NL, 64])
    qn_w_d = din("q_norm_w", [NL, 64])
    kn_w_d = din("k_norm_w", [NL, 64])
    cd = {}
    for k, v in cst.items():
        cd[k] = din("c_" + k, v.shape, BF16 if v.dtype == ml_dtypes.bfloat16 else F32)
    out_d = nc.dram_tensor("out", [NSEQ, T, D], F32, kind="ExternalOutput").ap()
    if dbg:
        dbg_d = nc.dram_tensor("dbg_mix", [128, 8, T], F32, kind="ExternalOutput").ap()

    S = Sched(nc)
    eng = S.eng

    def E(e, name, R, W, *a, **k):
        return S.op(e, lambda: getattr(eng[e], name)(*a, **k), R, W)

    def MM(out, lhsT, rhs, R, W, start=True, stop=True, inc=True):
        return S.op("pe", lambda: nc.tensor.matmul(out, lhsT=lhsT, rhs=rhs, start=start, stop=stop), R, W, inc=inc)

    def TR(out, in_, ident, R, W, inc=True):
        return S.op("pe", lambda: nc.tensor.transpose(out, in_, ident), R, W, inc=inc)

    class Tl:
        def __init__(self, name, shape, dt, nres=1):
            self.t = nc.alloc_sbuf_tensor(name, list(shape), dt)
            self.r = [Res("%s_%d" % (name, i)) for i in range(nres)]
            self.R = self.r[0]

    X = Tl("X", [128, NT, D], F32, NT)
    hT = Tl("hT", [128, KC, T], BF16, NQB)
    mixT = Tl("mixT", [128, KC, T], BF16, KC)
    wbuf = [Tl("wbuf%d" % i, [128, KC, 512], BF16, 4) for i in range(2)]
    stage = [Tl("stage%d" % i, [128, KC, 128], F32) for i in range(2)]
    C = {}
    ARENA_CONSTS = ("mtab", "incl2", "nst2", "ident2", "tri_le", "tri_gt", "onesf", "blockones")
    for k, v in cst.items():
        if k in ARENA_CONSTS:
            continue
        C[k] = Tl("k_" + k, v.shape, BF16 if v.dtype == ml_dtypes.bfloat16 else F32)
    normw = Tl("normw", [128, NL, KC], F32)
    convw = Tl("convw", [128, NL, 9, 4], F32)
    nea = Tl("nea", [128, NL, 6], F32)
    dtb = Tl("dtb", [128, NL, 6], F32)
    gw = Tl("gw", [128, NL, 64], F32)
    wqk = Tl("wqk", [128, NL, 4, 64], F32)
    wba = Tl("wba", [128, KC, 12], BF16)
    wbas = Tl("wbas", [128, KC, 12], F32)
    ps = [nc.alloc_psum_tensor("ps%d" % i, [128, 512], F32) for i in range(8)]
    pr = [Res("ps%d" % i, excl=True) for i in range(8)]

    for k in C:
        S.dma(C[k].t[:], cd[k], writes=[C[k].R])
    for l in range(NL):
        S.dma(normw.t[:, l, :], norm_w_d[l].rearrange("(c p) -> p c", p=128), writes=[normw.R],
              allow_slow_non_contiguous=True)
        for g9 in range(9):
            S.dma(convw.t[:, l, g9, :], conv_w_d[l][:, g9 * 128:(g9 + 1) * 128].rearrange("k p -> p k"), writes=[convw.R],
                  allow_slow_non_contiguous=True)
        S.dma(nea.t[:, l, :], a_log_d[l].partition_broadcast(128), writes=[nea.R])
        S.dma(dtb.t[:, l, :], dt_bias_d[l].partition_broadcast(128), writes=[dtb.R])
        S.dma(gw.t[:, l, :], gdn_w_d[l].partition_broadcast(128), writes=[gw.R])
        for j in range(2):
            S.dma(wqk.t[:, l, j, :], qn_w_d[l].partition_broadcast(128), writes=[wqk.R])
            S.dma(wqk.t[:, l, 2 + j, :], kn_w_d[l].partition_broadcast(128), writes=[wqk.R])
    E("act", "activation", [nea.R], [nea.R], nea.t[:], nea.t[:], AF.Exp)
    E("dve", "tensor_scalar", [nea.R], [nea.R], nea.t[:], nea.t[:], -1.0, None, ALU.mult)
    E("dve", "tensor_scalar", [wqk.R], [wqk.R], wqk.t[:, :, 0:2, :], wqk.t[:, :, 0:2, :], 0.125, None, ALU.mult)

    zerof = Tl("zerof", [128, 256], F32)
    E("pool", "memset", [], [zerof.R], zerof.t[:], 0.0)
    for i_, k_ in enumerate(("tri_le", "tri_gt", "onesf", "blockones")):
        C[k_ + "_r"] = Tl("k_" + k_ + "_r", [128, 128], F32)
        S.dma(stage[i_ % 2].t[:, 0, :], cd[k_], writes=[stage[i_ % 2].R])
        E("dve", "tensor_copy", [stage[i_ % 2].R], [C[k_ + "_r"].R], rr(C[k_ + "_r"].t[:]), stage[i_ % 2].t[:, 0, :])
    C["onesf"] = C["onesf_r"]
    C["tri_le"] = C["tri_le_r"]
    bank_rr = [0]
    big_reg = nc.gpsimd.to_reg(1e30)

    def nbank(lo=0, hi=8):
        b = lo + bank_rr[0] % (hi - lo)
        bank_rr[0] += 1
        return b

    stg_rr = [0]

    def load_cols(slot, src2d_list):
        for g, src in src2d_list:
            st = stage[stg_rr[0] % 2]
            stg_rr[0] += 1
            S.dma(st.t[:], src.rearrange("(c p) n -> p c n", p=128), writes=[st.R])
            E("act", "copy", [st.R], [wbuf[slot].r[g]], wbuf[slot].t[:, :, g * 128:(g + 1) * 128], st.t[:])

    def unit_cols(l, offs):
        return [(g, w_in_d[l][:, o:o + 128]) for g, o in enumerate(offs)]

    def phase_norm(l, ar):
        ss, junk, xn = ar["ss"], ar["junk"], ar["xn"]
        for i in range(NT):
            E("act", "activation", [X.r[i]], [junk.R, ss.R], junk.t[:], X.t[:, i, :], AF.Square,
              accum_out=ss.t[:, i:i + 1])
        E("act", "activation", [ss.R], [ss.R], ss.t[:], ss.t[:], AF.Ln, scale=1.0 / D, bias=EPS)
        E("act", "activation", [ss.R], [ss.R], ss.t[:], ss.t[:], AF.Exp, scale=-0.5)
        for i in range(NT):
            xt = xn[i % 2]
            E("dve", "tensor_scalar", [X.r[i], ss.R], [xt.R], xt.t[:], X.t[:, i, :], ss.t[:, i:i + 1], None, ALU.mult)
            b = nbank(0, 2)
            pb = ps[b][:].bitcast(BF16)
            for c in range(KC):
                TR(pb[:, c * 128:(c + 1) * 128], xt.t[:, c * 128:(c + 1) * 128], C["identb"].t[:],
                   [xt.R, C["identb"].R], [pr[b]], inc=(c == KC - 1))
            E("dve", "tensor_tensor", [pr[b], normw.R], [hT.r[i // KPB]],
              hT.t[:, :, i * 128:(i + 1) * 128], pb.rearrange("p (c n) -> p c n", c=KC),
              normw.t[:, l, :].unsqueeze(2).broadcast_to([128, KC, 128]), ALU.mult)

    def proj_fm(slot, g, tb, b, w=None):
        w = w or QB
        for c in range(KC):
            MM(ps[b][:, 0:w], wbuf[slot].t[:, c, g * 128:(g + 1) * 128], hT.t[:, c, tb * w:(tb + 1) * w],
               [wbuf[slot].r[g], hT.r[(tb * w) // QB]], [pr[b]], start=(c == 0), stop=(c == KC - 1), inc=(c == KC - 1))

    def proj_tm(slot, c0, ncol, i, out_ap, b):
        gs = sorted(set([c0 // 128, (c0 + ncol - 1) // 128]))
        gs = list(range(gs[0], gs[-1] + 1))
        for c in range(KC):
            MM(out_ap, hT.t[:, c, i * 128:(i + 1) * 128], wbuf[slot].t[:, c, c0:c0 + ncol],
               [wbuf[slot].r[g] for g in gs] + [hT.r[i // KPB]], [pr[b]],
               start=(c == 0), stop=(c == KC - 1), inc=(c == KC - 1))

    def silu_from_psum(b, ncol, et, out_ap, out_res, extra_reads=()):
        E("act", "activation", [pr[b]], [et.R], et.t[:, 0:ncol], ps[b][:, 0:ncol], AF.Exp, scale=-1.0)
        E("act", "activation", [et.R], [et.R], et.t[:, 0:ncol], et.t[:, 0:ncol], AF.Ln, bias=1.0)
        E("act", "activation", [et.R], [et.R], et.t[:, 0:ncol], et.t[:, 0:ncol], AF.Exp, scale=-1.0)
        E("dve", "tensor_tensor", [pr[b], et.R] + list(extra_reads), [out_res], out_ap, ps[b][:, 0:ncol],
          et.t[:, 0:ncol], ALU.mult)

    def bc_inproj_common(slot, ar, kind):
        zs, et = ar["zs"], ar["et"]
        for tb in range(NQB):
            b = nbank(0, 2)
            proj_fm(slot, 3, tb, b)
            silu_from_psum(b, QB, et[tb % 2], zs.t[:, tb * QB:(tb + 1) * QB], zs.R)

    def b_unit(l, u, slot, ar):
        qT, kTp, vt, zs = ar["qT"], ar["kTp"], ar["vt"], ar["zs"]
        for tb in range(NQB):
            b = nbank(0, 2)
            proj_fm(slot, 0, tb, b)
            E("act", "activation", [pr[b]], [qT.R], qT.t[:, tb * QB:(tb + 1) * QB], ps[b][:, 0:QB], AF.Copy, scale=0.125)
            b = nbank(0, 2)
            proj_fm(slot, 1, tb, b)
            E("act", "copy", [pr[b]], [kTp.r[0]], kTp.t[0:64, 0, tb * QB:(tb + 1) * QB], ps[b][0:64, 0:QB])
            E("dve", "tensor_copy", [pr[b]], [kTp.r[1]], kTp.t[64:128, 1, tb * QB:(tb + 1) * QB], ps[b][64:128, 0:QB])
        for i in range(NT):
            b = nbank(0, 2)
            proj_tm(slot, 256, 128, i, ps[b][:, 0:128], b)
            E("act", "copy", [pr[b]], [vt.R], vt.t[:, i, :], ps[b][:, 0:128])
        bc_inproj_common(slot, ar, "b")
        sp, Pb, wts, Rb = ar["sp"], ar["P"], ar["wts"], ar["Rb"]
        ob = 7
        zb = [2, 3]
        cb = [4, 5]
        for qb in range(NQB):
            qbase = qb * QB
            nkt = (qb + 1) * KPB
            kts = list(range(nkt - 1, -1, -1))

            def stage_a(n, h):
                kt = kts[n]
                kbase = kt * 128
                diag = kbase + 127 >= qbase
                spb, Pt = sp[h][n % 2], Pb[h][n % 2]
                MM(ps[zb[h]][:, 0:QB], kTp.t[:, h, kbase:kbase + 128], qT.t[:, qbase:qbase + QB],
                   [kTp.r[h], qT.R], [pr[zb[h]]])
                E("act", "activation", [pr[zb[h]]], [spb.R], spb.t[:, 0:QB], ps[zb[h]][:, 0:QB], AF.Exp, scale=-1.0)
                E("act", "activation", [spb.R], [spb.R], spb.t[:, 0:QB], spb.t[:, 0:QB], AF.Ln, bias=1.0)
                E("dve", "tensor_tensor", [pr[zb[h]], spb.R], [Pt.R], rr(Pt.t[:, 0:QB]), ps[zb[h]][:, 0:QB],
                  spb.t[:, 0:QB], ALU.add)
                if diag:
                    E("pool", "affine_select", [Pt.R], [Pt.R], out=rr(Pt.t[:, 0:QB]), in_=rr(Pt.t[:, 0:QB]),
                      pattern=[[1, QB]], compare_op=ALU.is_gt, fill=0.0, base=qbase - kbase, channel_multiplier=-1)

            def stage_b1(n, h):
                first = (n == 0)
                Pt = Pb[h][n % 2]
                MM(ps[cb[h]][:, 0:QB], rr(C["tri_gt_r"].t[:]), rr(Pt.t[:, 0:QB]), [C["tri_gt_r"].R, Pt.R], [pr[cb[h]]],
                   start=True, stop=first, inc=first)
                if not first:
                    MM(ps[cb[h]][:, 0:QB], rr(C["onesf_r"].t[:]), rr(Rb[h].t[:, 0:QB]), [C["onesf_r"].R, Rb[h].R], [pr[cb[h]]],
                       start=False, stop=True)

            def stage_b2(n, h):
                kt = kts[n]
                kbase = kt * 128
                diag = kbase + 127 >= qbase
                first = (n == 0)
                last = (n == len(kts) - 1)
                spb, Pt, wt = sp[h][n % 2], Pb[h][n % 2], wts[h][n % 2]
                E("dve", "tensor_tensor", [pr[cb[h]], spb.R], [spb.R], spb.t[:, 0:QB], ps[cb[h]][:, 0:QB],
                  spb.t[:, 0:QB], ALU.add)
                if diag:
                    E("pool", "affine_select", [spb.R], [spb.R], out=spb.t[:, 0:QB], in_=spb.t[:, 0:QB],
                      pattern=[[1, QB]], compare_op=ALU.is_gt, fill=big_reg, base=qbase - kbase, channel_multiplier=-1)
                E("act", "activation", [spb.R], [wt.R], wt.t[:, 0:QB], spb.t[:, 0:QB], AF.Exp, scale=-1.0)
                if not last:
                    if first:
                        E("pool", "tensor_copy", [Pt.R], [Rb[h].R], rr(Rb[h].t[:, 0:QB]), rr(Pt.t[:, 0:QB]))
                    else:
                        E("pool", "tensor_tensor", [Pt.R, Rb[h].R], [Rb[h].R], out=rr(Rb[h].t[:, 0:QB]), in0=Rb[h].t[:, 0:QB],
                          in1=Pt.t[:, 0:QB], op=ALU.add)

            def stage_c(n, h):
                kt = kts[n]
                first = (n == 0)
                last = (n == len(kts) - 1)
                wt = wts[h][n % 2]
                MM(ps[ob][64 * h:64 * h + 64, 0:QB], vt.t[:, kt, 64 * h:64 * h + 64], wt.t[:, 0:QB],
                   [vt.R, wt.R], [pr[ob]], start=first, stop=last)

            for n in range(len(kts) + 2):
                if n < len(kts):
                    for h in range(2):
                        stage_a(n, h)
                if 1 <= n <= len(kts):
                    for h in range(2):
                        stage_b1(n - 1, h)
                    for h in range(2):
                        stage_b2(n - 1, h)
                if n >= 2:
                    for h in range(2):
                        stage_c(n - 2, h)
            E("dve", "tensor_tensor", [pr[ob], zs.R], [mixT.r[3 + u]], mixT.t[:, 3 + u, qbase:qbase + QB],
              ps[ob][:, 0:QB], zs.t[:, qbase:qbase + QB], ALU.mult)

    def c_unit(l, u, slot, ar):
        import os as _os
        CSTOP = int(_os.environ.get("CSTOP", "99"))
        qT, kTp, vt, zs = ar["qT"], ar["kTp"], ar["vt"], ar["zs"]
        qk, sq, s4, qkb = ar["qk"], ar["sq"], ar["s4"], ar["qkb"]
        rt = ar["rt"]
        def c_tile(i):
            p_ = i % 2
            b = p_
            proj_tm(slot, 0, 384, i, ps[b][:, 0:384], b)
            E("act", "copy", [pr[b]], [vt.R], vt.t[:, i, :], ps[b][:, 256:384])
            q4 = qk[i % 2]
            E("act", "copy", [pr[b]], [q4.R], q4.t[:], ps[b][:, 0:256].rearrange("p (a d) -> p a d", a=4))
            yield
            E("pool", "tensor_tensor", [q4.R], [sq[p_].R], out=sq[p_].t[:], in0=q4.t[:], in1=q4.t[:], op=ALU.mult)
            E("dve", "tensor_reduce", [sq[p_].R], [s4[p_].R], out=s4[p_].t[:], in_=sq[p_].t[:], axis=AX.X, op=ALU.add)
            yield
            E("act", "activation", [s4[p_].R], [s4[p_].R], s4[p_].t[:], s4[p_].t[:], AF.Ln, scale=1.0 / 64, bias=EPS)
            E("act", "activation", [s4[p_].R], [s4[p_].R], s4[p_].t[:], s4[p_].t[:], AF.Exp, scale=-0.5)
            E("dve", "tensor_tensor", [q4.R, s4[p_].R], [q4.R], q4.t[:], q4.t[:],
              s4[p_].t[:].unsqueeze(2).broadcast_to([128, 4, 64]), ALU.mult)
            yield
            E("pool", "tensor_tensor", [q4.R, wqk.R], [q4.R], out=q4.t[:], in0=q4.t[:], in1=wqk.t[:, l, :, :], op=ALU.mult)
            yield
            qb_ = qkb[i % 2]
            E("act", "copy", [q4.R], [qb_.R], qb_.t[:], q4.t[:])
            cs = C["cost"].t[:, i, :].unsqueeze(1).broadcast_to([128, 4, 8])
            sn = C["sint"].t[:, i, :].unsqueeze(1).broadcast_to([128, 4, 8])
            x1 = q4.t[:, :, 0:8]
            x2 = q4.t[:, :, 8:16]
            E("dve", "tensor_tensor", [q4.R, C["cost"].R], [rt[p_].R], rt[p_].t[:, 0, :, :], x1, cs, ALU.mult)
            E("dve", "tensor_tensor", [q4.R, C["sint"].R], [rt[p_].R], rt[p_].t[:, 1, :, :], x2, sn, ALU.mult)
            E("dve", "tensor_tensor", [q4.R, C["cost"].R], [rt[p_].R], rt[p_].t[:, 2, :, :], x2, cs, ALU.mult)
            E("dve", "tensor_tensor", [q4.R, C["sint"].R], [rt[p_].R], rt[p_].t[:, 3, :, :], x1, sn, ALU.mult)
            yield
            E("dve", "tensor_tensor", [rt[p_].R], [qb_.R], qb_.t[:, :, 0:8], rt[p_].t[:, 0, :, :], rt[p_].t[:, 1, :, :], ALU.subtract)
            E("dve", "tensor_tensor", [rt[p_].R], [qb_.R], qb_.t[:, :, 8:16], rt[p_].t[:, 2, :, :], rt[p_].t[:, 3, :, :], ALU.add)
            yield
            b2 = 2 + p_
            pb = ps[b2][:].bitcast(BF16)
            qflat = qb_.t[:].rearrange("p a d -> p (a d)")
            TR(pb[:, 0:128], qflat[:, 0:128], C["identb"].t[:], [qb_.R, C["identb"].R], [pr[b2]], inc=False)
            TR(pb[:, 128:256], qflat[:, 128:256], C["identb"].t[:], [qb_.R, C["identb"].R], [pr[b2]])
            yield
            tq = ar["tq"][i % 2]
            E("dve", "tensor_copy", [pr[b2]], [tq.R], tq.t[:], pb[:, 0:256])
            E("act", "copy", [tq.R], [qT.R], qT.t[:, i * 128:(i + 1) * 128], tq.t[:, 0:128])
            E("pool", "tensor_copy", [tq.R], [kTp.r[0]], kTp.t[0:64, 0, i * 128:(i + 1) * 128], tq.t[0:64, 128:256])
            E("pool", "tensor_copy", [tq.R], [kTp.r[1]], kTp.t[64:128, 1, i * 128:(i + 1) * 128], tq.t[64:128, 128:256])
        for i0_ in range(0, NT, 2):
            gens = [c_tile(i_) for i_ in range(i0_, min(NT, i0_ + 2))]
            alive = [True] * len(gens)
            while any(alive):
                for j_, g_ in enumerate(gens):
                    if alive[j_]:
                        try:
                            next(g_)
                        except StopIteration:
                            alive[j_] = False
        if CSTOP <= 4:
            return
        bc_inproj_common(slot, ar, "c")
        if CSTOP <= 5:
            return
        Eb, EMb, rec = ar["E"], ar["EM"], ar["rec"]
        nb_, db_ = 6, 7
        sb = [2, 3, 4, 5]
        NE = len(Eb)
        for qb in range(NQB):
            qbase = qb * QB
            nkt = (qb + 1) * KPB
            pairs = [(kt, h) for kt in range(nkt - 1, -1, -1) for h in range(2)]

            def stage_a(n):
                kt, h = pairs[n]
                kbase = kt * 128
                off = qbase - kbase + OFF0
                b = sb[n % 4]
                Et, EMt = Eb[n % NE], EMb[n % NE]
                MM(ps[b][:, 0:QB], kTp.t[:, h, kbase:kbase + 128], qT.t[:, qbase:qbase + QB],
                   [kTp.r[h], qT.R], [pr[b]])
                E("act", "activation", [pr[b]], [Et.R], Et.t[:, 0:QB], ps[b][:, 0:QB], AF.Exp)
                E("pool" if n % 2 == 0 else "dve", "tensor_tensor", [Et.R, ar["mtab"].R], [EMt.R], out=EMt.t[:, 0:QB],
                  in0=Et.t[:, 0:QB], in1=ar["mtab"].t[:, off:off + QB], op=ALU.mult)

            def stage_b(n):
                kt, h = pairs[n]
                first = (kt == nkt - 1)
                last = (kt == 0)
                EMt = EMb[n % NE]
                MM(ps[nb_][64 * h:64 * h + 64, 0:QB], vt.t[:, kt, 64 * h:64 * h + 64], EMt.t[:, 0:QB],
                   [vt.R, EMt.R], [pr[nb_]], start=first, stop=last, inc=False)
                MM(ps[db_][64 * h:64 * h + 64, 0:QB], C["onesb"].t[:, 0:64], EMt.t[:, 0:QB],
                   [C["onesb"].R, EMt.R], [pr[db_]], start=first, stop=last)

            SK = 2
            for n in range(len(pairs) + SK):
                if n < len(pairs):
                    stage_a(n)
                if n >= SK:
                    stage_b(n - SK)
            E("dve", "reciprocal", [pr[db_]], [rec.R], rec.t[:, 0:QB], ps[db_][:, 0:QB])
            E("dve", "tensor_tensor", [pr[nb_], rec.R], [rec.R], rec.t[:, 0:QB], ps[nb_][:, 0:QB], rec.t[:, 0:QB], ALU.mult)
            E("pool", "tensor_tensor", [rec.R, zs.R], [mixT.r[5 + u]], out=mixT.t[:, 5 + u, qbase:qbase + QB],
              in0=rec.t[:, 0:QB], in1=zs.t[:, qbase:qbase + QB], op=ALU.mult)

    def a_prep(l, ar):
        beta, gg, t12 = ar["beta"], ar["g"], ar["t12"]
        S.dma(wbas.t[:], w_in_d[l][:, OFF_BETA:OFF_BETA + 12].rearrange("(c p) n -> p c n", p=128), writes=[wbas.R])
        E("pool", "tensor_copy", [wbas.R], [wba.R], wba.t[:], wbas.t[:])
        b = nbank(0, 2)
        for i in range(NT):
            for c in range(KC):
                MM(ps[b][:, i * 12:(i + 1) * 12], hT.t[:, c, i * 128:(i + 1) * 128], wba.t[:, c, :],
                   [wba.R, hT.r[i // KPB]], [pr[b]], start=(c == 0), stop=(c == KC - 1),
                   inc=(c == KC - 1 and i == NT - 1))
        pv = ps[b][:, 0:NT * 12].rearrange("p (n k) -> p n k", k=12)
        E("act", "activation", [pr[b]], [beta.R], beta.t[:], pv[:, :, 0:6], AF.Exp, scale=-1.0)
        E("act", "activation", [beta.R], [beta.R], beta.t[:], beta.t[:], AF.Ln, bias=1.0)
        E("act", "activation", [beta.R], [beta.R], beta.t[:], beta.t[:], AF.Exp, scale=-1.0)
        E("dve", "tensor_tensor", [pr[b], dtb.R], [t12.R], t12.t[:], pv[:, :, 6:12],
          dtb.t[:, l, :].unsqueeze(1).broadcast_to([128, NT, 6]), ALU.add)
        E("act", "activation", [t12.R], [t12.R], t12.t[:], t12.t[:], AF.Exp)
        E("act", "activation", [t12.R], [t12.R], t12.t[:], t12.t[:], AF.Ln, bias=1.0)
        E("dve", "tensor_tensor", [t12.R, nea.R], [gg.R], rr(gg.t[:]), t12.t[:],
          nea.t[:, l, :].unsqueeze(1).broadcast_to([128, NT, 6]), ALU.mult)

    def a_unit(l, u, slot, ar):
        beta, gg = ar["beta"], ar["g"]
        xb, acc, et, yq, qkvT = ar["xb"], ar["acc"], ar["et"], ar["yq"], ar["qkvT"]
        sqr, etg = ar["sqr"], ar["etg"]
        gz = ar["gz"]
        Ssb = ar["S"]
        identf, tri_le, onesf = C["identf"], C["tri_le"], C["onesf"]
        E("pool", "tensor_copy", [zerof.R], [Ssb.R], rr(Ssb.t[:]), zerof.t[:, 0:64])
        for g in range(3):
            E("pool", "memset", [], [xb[g].R], xb[g].t[:, 0:3], 0.0)
        for tb in range(T // QA):
            def grp(g):
                b = nbank(0, 2)
                proj_fm(slot, g, tb, b, QA)
                gi = g * 3 + u
                xg = xb[g]
                E("act", "copy", [pr[b]], [xg.R], xg.t[:, 3:3 + QA], ps[b][:, 0:QA])
                yield
                a_ = acc[g]
                E("dve", "tensor_scalar", [xg.R, convw.R], [a_.R], a_.t[:, 0:QA], xg.t[:, 0:QA],
                  convw.t[:, l, gi, 0:1], None, ALU.mult)
                for k in range(1, 4):
                    E("dve", "scalar_tensor_tensor", [xg.R, convw.R, a_.R], [a_.R], a_.t[:, 0:QA], xg.t[:, k:k + QA],
                      convw.t[:, l, gi, k:k + 1], a_.t[:, 0:QA], ALU.mult, ALU.add)
                E("pool", "tensor_copy", [xg.R], [xg.R], xg.t[:, 0:3], xg.t[:, QA:QA + 3])
                yield
                e_ = etg[g]
                E("act", "activation", [a_.R], [e_.R], e_.t[:, 0:QA], a_.t[:, 0:QA], AF.Exp, scale=-1.0)
                E("act", "activation", [e_.R], [e_.R], e_.t[:, 0:QA], e_.t[:, 0:QA], AF.Ln, bias=1.0)
                E("act", "activation", [e_.R], [e_.R], e_.t[:, 0:QA], e_.t[:, 0:QA], AF.Exp, scale=-1.0)
                yield
                if g == 2:
                    E("dve", "tensor_tensor", [a_.R, e_.R], [qkvT[2].R], qkvT[2].t[:, 0:QA], a_.t[:, 0:QA], e_.t[:, 0:QA], ALU.mult)
                else:
                    y = yq[g]
                    E("dve", "tensor_tensor", [a_.R, e_.R], [y.R], y.t[:, 0:QA], a_.t[:, 0:QA], e_.t[:, 0:QA], ALU.mult)
                    yield
                    E("pool", "tensor_tensor", [y.R], [sqr.R], out=rr(sqr.t[:, 0:QA]), in0=y.t[:, 0:QA], in1=y.t[:, 0:QA], op=ALU.mult)
                    b2 = nbank(0, 2)
                    MM(ps[b2][:, 0:QA], rr(C["blockones_r"].t[:]), rr(sqr.t[:, 0:QA]), [C["blockones_r"].R, sqr.R], [pr[b2]])
                    yield
                    E("act", "activation", [pr[b2]], [e_.R], e_.t[:, 0:QA], ps[b2][:, 0:QA], AF.Ln, bias=EPS)
                    E("act", "activation", [e_.R], [e_.R], e_.t[:, 0:QA], e_.t[:, 0:QA], AF.Exp, scale=-0.5)
                    yield
                    E("dve", "scalar_tensor_tensor", [y.R, e_.R], [qkvT[g].R], rr(qkvT[g].t[:, 0:QA]), y.t[:, 0:QA],
                      0.125 if g == 0 else 1.0, e_.t[:, 0:QA], ALU.mult, ALU.mult)

            ggens = [grp(g) for g in range(3)]
            galive = [True] * 3
            while any(galive):
                for j_, g_ in enumerate(ggens):
                    if galive[j_]:
                        try:
                            next(g_)
                        except StopIteration:
                            galive[j_] = False
            b = nbank(0, 2)
            for j in range(KA):
                i = tb * KA + j
                proj_tm(slot, 384, 128, i, ps[b][:, j * 128:(j + 1) * 128], b)
            silu_from_psum(b, QA, et[0], gz.t[:].rearrange("p a d -> p (a d)"), gz.R)
            E("pool", "tensor_tensor", [gz.R, gw.R], [gz.R], out=gz.t[:],
              in0=gz.t[:],
              in1=gw.t[:, l, :].unsqueeze(1).broadcast_to([128, KA * 2, 64]), op=ALU.mult)
            gens = [a_tile(l, u, tb * KA + j, j, ar) for j in range(KA)]
            front = [True] * KA
            while any(front):
                for j, g_ in enumerate(gens):
                    if front[j]:
                        if next(g_) == "BACK":
                            front[j] = False
            for g_ in gens:
                while next(g_) != "OUT":
                    pass
            oalive = [True] * len(gens)
            while any(oalive):
                for j, g_ in enumerate(gens):
                    if oalive[j]:
                        try:
                            next(g_)
                        except StopIteration:
                            oalive[j] = False

    def a_tile(l, u, i, j, ar):
        beta, gg = ar["beta"], ar["g"]
        qkvT, gz, Ssb = ar["qkvT"], ar["gz"], ar["S"]
        identf, tri_le, onesf = C["identf"], C["tri_le"], C["onesf"]
        k = j % 2
        BM = [[2, 3, 4, 5, 6, 7], [5, 6, 7, 2, 3, 4]][k]
        kv, gcs, egc, sbeg, gb, d0, DT, DTs, DTi = (ar[n][k] for n in ("kv", "gcs", "egc", "sbeg", "gb", "d0", "DT", "DTs", "DTi"))
        egr, kTp, M0p, qkT, MN, Pm, bv, usb, wTp, qgTp, vnew = (ar[n][k] for n in ("egr", "kTp", "M0p", "qkT", "MN", "Pm", "bv", "usb", "wTp", "qgTp", "vnew"))
        osb, osq, oss, ot, obf = (ar[n][k] for n in ("osb", "osq", "oss", "ot", "obf"))
        cs = slice(j * 128, (j + 1) * 128)
        bi = beta.t[:, i, 2 * u:2 * u + 2]
        gi = gg.t[:, i, 2 * u:2 * u + 2]
        qT_, kT_, vT_ = qkvT[0], qkvT[1], qkvT[2]
        b = BM[0]
        TR(ps[b][:, 0:128], kT_.t[:, cs], identf.t[:], [kT_.R, identf.R], [pr[b]], inc=False)
        TR(ps[b][:, 128:256], vT_.t[:, cs], identf.t[:], [vT_.R, identf.R], [pr[b]], inc=False)
        MM(ps[b][:, 256:258], rr(C["tri_le_r"].t[:]), rr(gi), [C["tri_le_r"].R, gg.R], [pr[b]])
        yield
        E("act", "copy", [pr[b]], [kv.R], kv.t[:], ps[b][:, 0:256].rearrange("p (a h d) -> p a h d", a=2, h=2))
        E("dve", "tensor_copy", [pr[b]], [gcs.R], gcs.t[:], ps[b][:, 256:258])
        E("act", "activation", [gcs.R], [egc.R], egc.t[:], gcs.t[:], AF.Exp)
        E("pool", "tensor_tensor", [egc.R, beta.R], [sbeg.R], out=sbeg.t[:], in0=egc.t[:], in1=bi, op=ALU.mult)
        yield
        for h in range(2):
            E("dve", "tensor_scalar", [onesf.R, gg.R], [gb.R], rr(gb.t[:, h, :]), onesf.t[:], gi[:, h:h + 1], None, ALU.mult)
            E("dve", "tensor_scalar", [onesf.R, gg.R], [gb.R], rr(gb.t[:, 2, 64 * h:64 * h + 64]), onesf.t[:, 0:64], gi[:, h:h + 1], None, ALU.mult)
        b = BM[1]
        for h in range(3):
            MM(ps[b][:, h * 128:(h + 1) * 128], rr(gb.t[:, h, :]), rr(C["tri_le_r"].t[:]), [gb.R, C["tri_le_r"].R], [pr[b]], inc=(h == 2))
        yield
        for h in range(2):
            E("dve", "tensor_scalar", [pr[b], gcs.R], [d0.R], d0.t[:, h, :], ps[b][:, h * 128:(h + 1) * 128],
              gcs.t[:, h:h + 1], 0.0, ALU.subtract, ALU.min)
        E("act", "activation", [d0.R], [DT.R], DT.t[:], d0.t[:], AF.Exp)
        E("act", "activation", [pr[b]], [egr.R], egr.t[:], ps[b][:, 256:384], AF.Exp)
        E("pool", "tensor_tensor", [DT.R, ar["nst2"].R], [DTs.R], out=DTs.t[:], in0=DT.t[:], in1=ar["nst2"].t[:], op=ALU.mult)
        E("pool", "tensor_tensor", [DT.R, ar["incl2"].R], [DTi.R], out=DTi.t[:], in0=DT.t[:], in1=ar["incl2"].t[:], op=ALU.mult)
        yield
        E("pool", "tensor_copy", [kT_.R], [kTp.R], rr(kTp.t[0:64, 0, :]), rr(kT_.t[0:64, cs]))
        E("pool", "tensor_copy", [kT_.R], [kTp.R], rr(kTp.t[64:128, 1, :]), rr(kT_.t[64:128, cs]))
        b = BM[2]
        for h in range(2):
            MM(ps[b][:, h * 128:(h + 1) * 128], rr(kTp.t[:, h, :]), rr(kT_.t[:, cs]), [kTp.R, kT_.R], [pr[b]], inc=False)
        for h in range(2):
            MM(ps[b][:, 256 + h * 128:256 + (h + 1) * 128], rr(kTp.t[:, h, :]), rr(qT_.t[:, cs]), [kTp.R, qT_.R], [pr[b]], inc=(h == 1))
        yield
        E("dve", "tensor_tensor", [pr[b], DTs.R], [M0p.R], M0p.t[:], ps[b][:, 0:256].rearrange("p (h n) -> p h n", h=2), DTs.t[:], ALU.mult)
        E("dve", "tensor_tensor", [pr[b], DTi.R], [qkT.R], rr(qkT.t[:]), ps[b][:, 256:512].rearrange("p (h n) -> p h n", h=2), DTi.t[:], ALU.mult)
        yield
        b = BM[3]
        for h in range(2):
            TR(ps[b][:, h * 128:(h + 1) * 128], M0p.t[:, h, :], identf.t[:], [M0p.R, identf.R], [pr[b]], inc=(h == 1))
        mn = MN[0]
        yield
        for h in range(2):
            E("dve", "tensor_scalar", [pr[b], beta.R], [mn.R], rr(mn.t[:, 2 + h, :]), ps[b][:, h * 128:(h + 1) * 128], bi[:, h:h + 1], None, ALU.mult)
        b = BM[3]
        for h in range(2):
            TR(ps[b][:, h * 128:(h + 1) * 128], mn.t[:, 2 + h, :], identf.t[:], [mn.R, identf.R], [pr[b]], inc=(h == 1))
        yield
        E("act", "copy", [pr[b]], [mn.R], rr(mn.t[:, 0:2, :]), ps[b][:, 0:256].rearrange("p (h n) -> p h n", h=2))
        E("dve", "tensor_tensor", [pr[b], ar["ident2"].R], [Pm.R], rr(Pm.t[:]), ps[b][:, 0:256].rearrange("p (h n) -> p h n", h=2), ar["ident2"].t[:], ALU.add)
        yield
        for lv in range(1, 7):
            mo = MN[(lv - 1) % 2]
            mw = MN[lv % 2]
            b = BM[4]
            if lv < 6:
                for h in range(2):
                    MM(ps[b][:, h * 128:(h + 1) * 128], rr(mo.t[:, 2 + h, :]), rr(mo.t[:, h, :]), [mo.R], [pr[b]], inc=False)
            for h in range(2):
                MM(ps[b][:, 256 + h * 128:256 + (h + 1) * 128], rr(mo.t[:, h, :]), rr(mo.t[:, 2 + h, :]), [mo.R], [pr[b]], inc=(h == 1))
            yield
            if lv < 6:
                E("act", "copy", [pr[b]], [mw.R], rr(mw.t[:, 0:2, :]), ps[b][:, 0:256].rearrange("p (h n) -> p h n", h=2))
            E("dve", "tensor_copy", [pr[b]], [mw.R], rr(mw.t[:, 2:4, :]), ps[b][:, 256:512].rearrange("p (h n) -> p h n", h=2))
            yield
            b = BM[5]
            for h in range(2):
                MM(ps[b][:, h * 128:(h + 1) * 128], rr(mw.t[:, 2 + h, :]), rr(Pm.t[:, h, :]), [mw.R, Pm.R], [pr[b]], inc=(h == 1))
            yield
            E("dve", "tensor_tensor", [pr[b], Pm.R], [Pm.R], rr(Pm.t[:]), Pm.t[:], ps[b][:, 0:256].rearrange("p (h n) -> p h n", h=2), ALU.add)
            yield
        yield
        E("pool", "tensor_tensor", [kv.R, beta.R], [bv.R], out=rr(bv.t[:, 0, :, :]), in0=kv.t[:, 1, :, :],
          in1=bi.unsqueeze(2).broadcast_to([128, 2, 64]), op=ALU.mult)
        E("pool", "tensor_tensor", [kv.R, sbeg.R], [bv.R], out=rr(bv.t[:, 1, :, :]), in0=kv.t[:, 0, :, :],
          in1=sbeg.t[:].unsqueeze(2).broadcast_to([128, 2, 64]), op=ALU.mult)
        E("pool", "tensor_tensor", [kv.R, DT.R], [bv.R], out=rr(bv.t[:, 2, :, :]), in0=kv.t[:, 0, :, :],
          in1=DT.t[:, :, 127:128].broadcast_to([128, 2, 64]), op=ALU.mult)
        b = BM[1]
        for h in range(2):
            MM(ps[b][:, h * 64:(h + 1) * 64], rr(Pm.t[:, h, :]), rr(bv.t[:, 0, h, :]), [Pm.R, bv.R], [pr[b]], inc=False)
        for h in range(2):
            MM(ps[b][64 * h:64 * h + 64, 128:256], bv.t[:, 1, h, :], Pm.t[:, h, :], [Pm.R, bv.R], [pr[b]], inc=(h == 1))
        yield
        E("act", "copy", [pr[b]], [usb.R], usb.t[:], ps[b][:, 0:128].rearrange("p (h d) -> p h d", h=2))
        E("dve", "tensor_copy", [pr[b]], [wTp.R], rr(wTp.t[0:64, 0, :]), ps[b][0:64, 128:256])
        E("dve", "tensor_copy", [pr[b]], [wTp.R], rr(wTp.t[64:128, 1, :]), ps[b][64:128, 128:256])
        E("pool", "tensor_tensor", [qT_.R, egr.R], [qgTp.R], out=rr(qgTp.t[0:64, 0, :]), in0=qT_.t[0:64, cs], in1=egr.t[0:64, :], op=ALU.mult)
        E("pool", "tensor_tensor", [qT_.R, egr.R], [qgTp.R], out=rr(qgTp.t[64:128, 1, :]), in0=qT_.t[64:128, cs], in1=egr.t[64:128, :], op=ALU.mult)
        yield
        yield "BACK"
        b = BM[2]
        for h in range(2):
            MM(ps[b][:, h * 64:(h + 1) * 64], rr(wTp.t[:, h, :]), rr(Ssb.t[:]), [wTp.R, Ssb.R], [pr[b]], inc=(h == 1))
        E("dve", "tensor_tensor", [usb.R, pr[b]], [vnew.R], rr(vnew.t[:]), usb.t[:], ps[b][:, 0:128].rearrange("p (h d) -> p h d", h=2), ALU.subtract)
        bo = BM[4]
        for h in range(2):
            MM(ps[bo][:, h * 64:(h + 1) * 64], rr(qgTp.t[:, h, :]), rr(Ssb.t[:]), [qgTp.R, Ssb.R], [pr[bo]], start=True, stop=False, inc=False)
            MM(ps[bo][:, h * 64:(h + 1) * 64], rr(qkT.t[:, h, :]), rr(vnew.t[:, h, :]), [qkT.R, vnew.R], [pr[bo]], start=False, stop=True, inc=(h == 1))
        b = BM[5]
        for h in range(2):
            MM(ps[b][64 * h:64 * h + 64, 0:64], bv.t[:, 2, h, :], vnew.t[:, h, :], [bv.R, vnew.R], [pr[b]], inc=(h == 1))
        E("dve", "scalar_tensor_tensor", [Ssb.R, egr.R, pr[b]], [Ssb.R], rr(Ssb.t[:]), Ssb.t[:], egr.t[:, 127:128], ps[b][:, 0:64], ALU.mult, ALU.add)
        yield
        yield "OUT"
        E("act", "copy", [pr[bo]], [osb.R], osb.t[:], ps[bo][:, 0:128].rearrange("p (h d) -> p h d", h=2))
        yield
        E("pool", "tensor_tensor", [osb.R], [osq.R], out=osq.t[:], in0=osb.t[:], in1=osb.t[:], op=ALU.mult)
        E("dve", "tensor_reduce", [osq.R], [oss.R], out=oss.t[:], in_=osq.t[:], axis=AX.X, op=ALU.add)
        yield
        E("act", "activation", [oss.R], [oss.R], oss.t[:], oss.t[:], AF.Ln, scale=1.0 / 64, bias=EPS)
        E("act", "activation", [oss.R], [oss.R], oss.t[:], oss.t[:], AF.Exp, scale=-0.5)
        yield
        E("dve", "tensor_tensor", [osb.R, oss.R], [ot.R], ot.t[:], osb.t[:], oss.t[:].unsqueeze(2).broadcast_to([128, 2, 64]), ALU.mult)
        E("dve", "tensor_tensor", [ot.R, gz.R], [obf.R], obf.t[:], ot.t[:], gz.t[:, 2 * j:2 * j + 2, :], ALU.mult)
        b = nbank(0, 2)
        pb = ps[b][:].bitcast(BF16)
        TR(pb[:, 0:128], obf.t[:].rearrange("p h d -> p (h d)"), C["identb"].t[:], [obf.R, C["identb"].R], [pr[b]])
        E("dve", "tensor_copy", [pr[b]], [mixT.r[u]], mixT.t[:, u, i * 128:(i + 1) * 128], pb[:, 0:128])

    def phase_out(l):
        for hf in range(2):
            for i in range(NT):
                b = nbank(0, 4)
                for c in range(KC):
                    MM(ps[b][:, 0:512], mixT.t[:, c, i * 128:(i + 1) * 128], wbuf[hf].t[:, c, :],
                       [mixT.r[c]] + wbuf[hf].r, [pr[b]], start=(c == 0), stop=(c == KC - 1), inc=(c == KC - 1))
                E("dve", "tensor_tensor", [pr[b], X.r[i]], [X.r[i]], X.t[:, i, hf * 512:(hf + 1) * 512],
                  X.t[:, i, hf * 512:(hf + 1) * 512], ps[b][:, 0:512], ALU.add)

    import contextlib

    class Arena:
        def __init__(self):
            self.stack = contextlib.ExitStack()

        def tl(self, name, shape, dt):
            t = Tl.__new__(Tl)
            t.t = self.stack.enter_context(nc.sbuf_tensor(name, list(shape), dt))
            t.r = [Res(name)]
            t.R = t.r[0]
            return t

        def const(self, k):
            v = cst[k]
            t = self.tl(nm("ac_" + k), v.shape, BF16 if v.dtype == ml_dtypes.bfloat16 else F32)
            S.dma(t.t[:], cd[k], writes=[t.R])
            self.consts = getattr(self, "consts", []) + [t]
            return t

        def close(self):
            for t_ in getattr(self, "consts", []):
                for e_ in ("pe", "act", "dve", "pool"):
                    S._wait(e_, *t_.R.w)
            S.barrier()
            for k_ in ("pe", "act", "dve", "pool"):
                S._wait("sp", k_, S.cnt[k_])
            self.stack.close()

    uid = [0]

    def nm(s):
        uid[0] += 1
        return "%s_%d" % (s, uid[0])

    for s in range(NSEQ):
        for i0 in range(0, NT, 4):
            n = min(4, NT - i0)
            S.dma(X.t[:, i0:i0 + n, :], x_d[s, i0 * 128:(i0 + n) * 128, :].rearrange("(n p) d -> p n d", p=128),
                  writes=[X.r[i] for i in range(i0, i0 + n)])
        for l in range(NL):
            units = [("a", u, [OFF_QA + 128 * u, OFF_KA + 128 * u, OFF_VA + 128 * u, OFF_ZA + 128 * u]) for u in range(3)]
            units += [("b", u, [OFF_QB + 128 * u, OFF_KB + 128 * u, OFF_VB + 128 * u, OFF_ZB + 128 * u]) for u in range(2)]
            units += [("c", u, [OFF_QC + 128 * u, OFF_KC + 128 * u, OFF_VC + 128 * u, OFF_ZC + 128 * u]) for u in range(3)]
            load_cols(0, unit_cols(l, units[0][2]))
            A = Arena()
            ar = {"ss": A.tl(nm("ss"), [128, NT], F32), "junk": A.tl(nm("junk"), [128, D], BF16),
                  "xn": [A.tl(nm("xn"), [128, D], BF16) for _ in range(2)]}
            phase_norm(l, ar)
            A.close()
            A = Arena()
            ar = {"beta": A.tl(nm("beta"), [128, NT, 6], F32), "g": A.tl(nm("g"), [128, NT, 6], F32),
                  "t12": A.tl(nm("t12"), [128, NT, 6], F32),
                  "xb": [A.tl(nm("xb"), [128, QA + 3], F32) for _ in range(3)],
                  "qkvT": [A.tl(nm("qkvT"), [128, QA], F32) for _ in range(3)],
                  "gz": A.tl(nm("gz"), [128, 2 * KA, 64], F32),
                  "S": A.tl(nm("S"), [128, 64], F32)}
            _acc = A.tl(nm("acc"), [128, QA], F32)
            _et = A.tl(nm("et"), [128, QA], F32)
            ar["et"] = [_et, _et]
            _yq = A.tl(nm("yq"), [128, QA], F32)

            mix_state = [3, 0]

            def mix_tl(name, shp, dt):
                n = int(np.prod(shp[1:])) * (2 if dt == F32 else 1)
                if mix_state[1] + n > T:
                    mix_state[0] += 1
                    mix_state[1] = 0
                if T < 2048 or mix_state[0] > 7 or n > T:
                    return None
                ap = mixT.t[:, mix_state[0], mix_state[1]:mix_state[1] + n]
                mix_state[1] += n
                if dt == F32:
                    ap = ap.bitcast(F32)
                if len(shp) == 3:
                    ap = ap.rearrange("p (a b) -> p a b", a=shp[1])
                elif len(shp) == 4:
                    ap = ap.rearrange("p (a b c) -> p a b c", a=shp[1], b=shp[2])
                t_ = Tl.__new__(Tl)
                t_.t = ap
                t_.r = [Res(name)]
                t_.R = t_.r[0]
                return t_

            def two(n_, shp, dt=F32):
                t0_ = A.tl(nm(n_), shp, dt)
                ok_alias = n_ in ("kv", "d0", "DTs", "DTi", "egr", "M0p", "usb", "osb", "osq", "ot")
                t1_ = mix_tl(nm(n_ + "m"), shp, dt) if ok_alias else None
                if t1_ is None:
                    t1_ = A.tl(nm(n_), shp, dt)
                return [t0_, t1_]
            ar.update({"kv": two("kv", [128, 2, 2, 64]), "gcs": two("gcs", [128, 2]), "egc": two("egc", [128, 2]),
                       "sbeg": two("sbeg", [128, 2]), "gb": two("gb", [128, 3, 128]), "d0": two("d0", [128, 2, 128]),
                       "DTs": two("DTs", [128, 2, 128]), "DTi": two("DTi", [128, 2, 128]),
                       "egr": two("egr", [128, 128]), "kTp": two("kTpA", [128, 2, 128]), "M0p": two("M0p", [128, 2, 128]),
                       "qkT": two("qkT", [128, 2, 128]),
                       "Pm": two("Pm", [128, 2, 128]), "bv": two("bv", [128, 3, 2, 64]), "usb": two("usb", [128, 2, 64]),
                       "wTp": two("wTp", [128, 2, 128]), "qgTp": two("qgTp", [128, 2, 128]), "vnew": two("vnew", [128, 2, 64]),
                       "osb": two("osb", [128, 2, 64]), "osq": two("osq", [128, 2, 64]), "oss": two("oss", [128, 2]),
                       "ot": two("ot", [128, 2, 64]), "obf": two("obf", [128, 2, 64], BF16)})
            ar["DT"] = ar["d0"]

            def extra(n_, base):
                t_ = mix_tl(nm(n_), [128, QA], F32)
                return t_ if t_ is not None else (A.tl(nm(n_), [128, QA], F32) if T < 2048 else base)
            ar["acc"] = [_acc, extra("acc1", _acc), extra("acc2", _acc)]
            ar["etg"] = [_et, extra("et1", _et), extra("et2", _et)]
            ar["yq"] = [_yq, extra("yq1", _yq)]
            for k_ in ("incl2", "nst2", "ident2"):
                t_ = mix_tl(nm("mc_" + k_), [128, 2, 128], F32)
                if t_ is None:
                    ar[k_] = A.const(k_)
                else:
                    S.dma(t_.t[:], cd[k_], writes=[t_.R])
                    A.consts = getattr(A, "consts", []) + [t_]
                    ar[k_] = t_
            ar["MN"] = [[A.tl(nm("MN"), [128, 4, 128], F32) for _ in range(2)],
                        [A.tl(nm("MN"), [128, 4, 128], F32) for _ in range(2)]]
            for n_ in ("kTp", "wTp", "qgTp"):
                for k_ in range(2):
                    E("pool", "tensor_copy", [zerof.R], [ar[n_][k_].R], rr(ar[n_][k_].t[:]), zerof.t[:].rearrange("p (h n) -> p h n", h=2))
            ar["sqr"] = A.tl(nm("sqr"), [128, QA], F32)
            if "a" in phases:
                a_prep(l, ar)
            ui = 0
            for (kind, u, offs) in units[0:3]:
                load_cols((ui + 1) % 2, unit_cols(l, units[ui + 1][2]))
                if "a" in phases:
                    a_unit(l, u, ui % 2, ar)
                ui += 1
            A.close()
            for kind_, ulist in (("b", units[3:5]), ("c", units[5:8])):
                A = Arena()
                ar = {"qT": A.tl(nm("qT"), [128, T], BF16), "kTp": A.tl(nm("kTp"), [128, 2, T], BF16),
                      "vt": A.tl(nm("vt"), [128, NT, 128], BF16), "zs": A.tl(nm("zs"), [128, T], BF16)}
                _et = A.tl(nm("etb"), [128, QB], F32)
                ar["et"] = [_et, _et]
                if kind_ == "b":
                    ar["sp"] = [[A.tl(nm("sp"), [128, QB], F32) for _ in range(2)] for _ in range(2)]
                    ar["P"] = [[A.tl(nm("P"), [128, QB], F32) for _ in range(2)] for _ in range(2)]
                    ar["wts"] = [[A.tl(nm("wts"), [128, QB], BF16) for _ in range(2)] for _ in range(2)]
                    ar["Rb"] = [A.tl(nm("Rb"), [128, QB], F32) for _ in range(2)]
                else:
                    ar.update({"E": [A.tl(nm("E"), [128, QB], BF16) for _ in range(3)],
                               "EM": [A.tl(nm("EM"), [128, QB], BF16) for _ in range(3)],
                               "rec": A.tl(nm("rec"), [128, QB], F32),
                               "qk": [A.tl(nm("qk"), [128, 4, 64], F32) for _ in range(2)],
                               "qkb": [A.tl(nm("qkb"), [128, 4, 64], BF16) for _ in range(2)],
                               "sq": [A.tl(nm("sq"), [128, 4, 64], F32) for _ in range(2)],
                               "s4": [A.tl(nm("s4"), [128, 4], F32) for _ in range(2)],
                               "rt": [A.tl(nm("rt"), [128, 4, 4, 8], F32) for _ in range(2)],
                               "tq": [A.tl(nm("tq"), [128, 256], BF16) for _ in range(2)]})
                    ar["mtab"] = A.const("mtab")
                ar["kTp"].r = [Res("kTp0"), Res("kTp1")]
                E("pool", "memset", [], ar["kTp"].r, ar["kTp"].t[:], 0.0)
                for (kind, u, offs) in ulist:
                    if ui + 1 < len(units):
                        load_cols((ui + 1) % 2, unit_cols(l, units[ui + 1][2]))
                    else:
                        load_cols((ui + 1) % 2, [(g, w_out_d[l][:, g * 128:(g + 1) * 128]) for g in range(4)])
                    if kind == "b" and "b" in phases:
                        b_unit(l, u, ui % 2, ar)
                    if kind == "c" and "c" in phases:
                        c_unit(l, u, ui % 2, ar)
                    ui += 1
                A.close()
            load_cols(1, [(g, w_out_d[l][:, 512 + g * 128:512 + (g + 1) * 128]) for g in range(4)])
            if dbg and l == NL - 1 and s == NSEQ - 1:
                A = Arena()
                dm = A.tl(nm("dm"), [128, T], F32)
                for c in range(KC):
                    E("dve", "tensor_copy", [mixT.r[c]], [dm.R], dm.t[:], mixT.t[:, c, :])
                    S.dma(dbg_d[:, c, :], dm.t[:], reads=[dm.R], writes=[Res("dbgout")])
                E("pool", "memset", [], [dm.R], dm.t[:, 0:1], 0.0)
                A.close()
            phase_out(l)
            S.barrier()
        outres = Res("out")
        for i0 in range(0, NT, 4):
            n = min(4, NT - i0)
            S.dma(out_d[s, i0 * 128:(i0 + n) * 128, :].rearrange("(n p) d -> p n d", p=128), X.t[:, i0:i0 + n, :],
                  reads=[X.r[i] for i in range(i0, i0 + n)], writes=[outres])
        S.finish([outres])
    return nc, cst, S


_CACHE = {}


def kernel(x, norm_w, w_in, conv_w, a_log, dt_bias, gdn_norm_w, q_norm_w, k_norm_w, w_out):
    n_cores = 8
    B, T, _ = x.shape
    NSEQ = B // n_cores
    NL = w_in.shape[0]
    key = (T, NSEQ, NL)
    if key not in _CACHE:
        _CACHE[key] = build_program(T, NSEQ, NL)
    nc, cst, _ = _CACHE[key]
    f = lambda a: np.ascontiguousarray(np.asarray(a, dtype=np.float32))
    shared = {"w_in": f(w_in), "w_out": f(w_out), "norm_w": f(norm_w), "conv_w": f(conv_w), "a_log": f(a_log),
              "dt_bias": f(dt_bias), "gdn_norm_w": f(gdn_norm_w), "q_norm_w": f(q_norm_w), "k_norm_w": f(k_norm_w)}
    for k, v in cst.items():
        shared["c_" + k] = v
    xs = f(x)
    in_maps = []
    for c in range(n_cores):
        m = dict(shared)
        m["x"] = xs[c * NSEQ:(c + 1) * NSEQ]
        in_maps.append(m)
    res = run_bass_kernel_spmd(nc, in_maps, core_ids=list(range(n_cores)))
    return np.concatenate([np.asarray(r["out"], dtype=np.float32) for r in res.results], axis=0)
```
